# Optimizing a Trainium2 kernel written in Bass

```python
import math
import jax, jax.numpy as jnp
from jax import lax
import numpy as np

D_MODEL = 1024
BATCH = 32
SEQ = 2048
DEPTH = 1

MEM_LEN = 256
MEM_HEADS = 4
MEM_HEAD_DIM = 128
GM_WIDTH = D_MODEL // 2
GM_CHUNK = 128
GM_GROUPS = 4
GM_GROUP_W = GM_WIDTH // GM_GROUPS
MLA_HEADS = 8
MLA_NOPE = 128
MLA_ROPE = 64
MLA_V = 128
Q_LORA = 384
KV_LORA = 256
ROPE_BASE = 10000.0
Q_BLOCK = 128
D_FF = 4 * D_MODEL
N_BRANCH = 3
EPS = 1e-6
W_GM = 2 * GM_WIDTH
W_MLA = Q_LORA + KV_LORA + MLA_ROPE
W_MEMQ = MEM_HEADS * MEM_HEAD_DIM
W_GATE = N_BRANCH * D_MODEL
W_IN_COLS = W_GM + W_MLA + W_MEMQ + W_GATE

kernel_name = "hybrid_gmlp_mla_memory_gated_block"


def rmsnorm(x, g):
    xf = x.astype(jnp.float32)
    y = xf * lax.rsqrt(jnp.mean(xf * xf, axis=-1, keepdims=True) + EPS)
    return (y * g.astype(jnp.float32)).astype(x.dtype)


def layernorm(x, g, b):
    xf = x.astype(jnp.float32)
    mu = jnp.mean(xf, axis=-1, keepdims=True)
    xc = xf - mu
    y = xc * lax.rsqrt(jnp.mean(xc * xc, axis=-1, keepdims=True) + EPS)
    return (y * g.astype(jnp.float32) + b.astype(jnp.float32)).astype(x.dtype)


def rope_tables(positions):
    inv_freq = ROPE_BASE ** (-jnp.arange(0, MLA_ROPE, 2, dtype=jnp.float32) / MLA_ROPE)
    ang = positions.astype(jnp.float32)[..., None] * inv_freq
    return jnp.cos(ang), jnp.sin(ang)


def apply_rope(x, cos, sin):
    x1, x2 = jnp.split(x.astype(jnp.float32), 2, axis=-1)
    return jnp.concatenate([x1 * cos - x2 * sin, x2 * cos + x1 * sin], axis=-1).astype(x.dtype)


def gmlp_branch(z_u, z_v, g_ln, b_ln, w_s, b_s):
    B, S, _ = z_u.shape
    u = jax.nn.gelu(z_u)
    v = layernorm(jax.nn.gelu(z_v), g_ln, b_ln)
    v5 = v.reshape(B, S // GM_CHUNK, GM_CHUNK, GM_GROUPS, GM_GROUP_W)
    w_causal = jnp.tril(w_s).astype(v.dtype)
    mixed = jnp.einsum('gts,bnsgw->bntgw', w_causal, v5) + b_s.T[:, :, None].astype(v.dtype)
    return u * mixed.reshape(B, S, GM_WIDTH)


def mla_branch(c_q, c_kv, k_pe, cos, sin, g_cq, w_uq, g_ckv, w_ukv,
               g_q_nope, g_q_pe, g_k_nope, g_k_pe):
    B, S, _ = c_q.shape
    q = (rmsnorm(c_q, g_cq) @ w_uq).reshape(B, S, MLA_HEADS, MLA_NOPE + MLA_ROPE)
    q_nope, q_pe = q[..., :MLA_NOPE], q[..., MLA_NOPE:]
    kv = (rmsnorm(c_kv, g_ckv) @ w_ukv).reshape(B, S, MLA_HEADS, MLA_NOPE + MLA_V)
    k_nope, v = kv[..., :MLA_NOPE], kv[..., MLA_NOPE:]
    q_nope = rmsnorm(q_nope, g_q_nope)
    k_nope = rmsnorm(k_nope, g_k_nope)
    q_pe = apply_rope(rmsnorm(q_pe, g_q_pe), cos[:, :, None, :], sin[:, :, None, :])
    k_pe = apply_rope(rmsnorm(k_pe, g_k_pe), cos, sin)
    scale = 1.0 / math.sqrt(MLA_NOPE + MLA_ROPE)
    nb = S // Q_BLOCK
    qn_b = q_nope.reshape(B, nb, Q_BLOCK, MLA_HEADS, MLA_NOPE).transpose(1, 0, 2, 3, 4)
    qp_b = q_pe.reshape(B, nb, Q_BLOCK, MLA_HEADS, MLA_ROPE).transpose(1, 0, 2, 3, 4)
    key_pos = jnp.arange(S)

    def block(args):
        qn, qp, i = args
        s = (jnp.einsum('bqhd,bkhd->bhqk', qn, k_nope)
             + jnp.einsum('bqhd,bkd->bhqk', qp, k_pe)).astype(jnp.float32) * scale
        q_pos = i * Q_BLOCK + jnp.arange(Q_BLOCK)
        s = jnp.where((q_pos[:, None] >= key_pos[None, :])[None, None], s, -jnp.inf)
        p = jax.nn.softmax(s, axis=-1).astype(v.dtype)
        return jnp.einsum('bhqk,bkhd->bqhd', p, v)

    out = lax.map(block, (qn_b, qp_b, jnp.arange(nb)))
    return out.transpose(1, 0, 2, 3, 4).reshape(B, S, MLA_HEADS * MLA_V)


def memory_branch(q_m, mem, g_mem, w_mem_kv, g_mq, g_mk):
    B, S, _ = q_m.shape
    M = mem.shape[1]
    q = rmsnorm(q_m.reshape(B, S, MEM_HEADS, MEM_HEAD_DIM), g_mq)
    kv = (rmsnorm(mem, g_mem) @ w_mem_kv).reshape(B, M, 2, MEM_HEADS, MEM_HEAD_DIM)
    k = rmsnorm(kv[:, :, 0], g_mk)
    v = kv[:, :, 1]
    s = jnp.einsum('bshd,bmhd->bhsm', q, k).astype(jnp.float32) / math.sqrt(MEM_HEAD_DIM)
    p = jax.nn.softmax(s, axis=-1).astype(v.dtype)
    return jnp.einsum('bhsm,bmhd->bshd', p, v).reshape(B, S, MEM_HEADS * MEM_HEAD_DIM)


def setup_inputs(seed: int = 0) -> dict:
    key = jax.random.key(seed)
    ks = iter(jax.random.split(key, 40))

    def w(shape, fan_in):
        return jax.random.normal(next(ks), (DEPTH,) + shape, jnp.float32) * (fan_in ** -0.5)

    def gain(n):
        return 1.0 + 0.02 * jax.random.normal(next(ks), (DEPTH, n), jnp.float32)

    x = jax.random.normal(next(ks), (BATCH, SEQ, D_MODEL), jnp.float32)
    mem = jax.random.normal(next(ks), (BATCH, MEM_LEN, D_MODEL), jnp.float32)
    offset = jax.random.randint(next(ks), (BATCH, 1), 0, 4096, dtype=jnp.int32)
    positions = (offset + jnp.arange(SEQ, dtype=jnp.int32)[None, :]).astype(jnp.int32)
    return {
        "x": x,
        "mem": mem,
        "positions": positions,
        "g_mix": gain(D_MODEL),
        "w_in": w((D_MODEL, W_IN_COLS), D_MODEL),
        "g_cq": gain(Q_LORA),
        "w_uq": w((Q_LORA, MLA_HEADS * (MLA_NOPE + MLA_ROPE)), Q_LORA),
        "g_ckv": gain(KV_LORA),
        "w_ukv": w((KV_LORA, MLA_HEADS * (MLA_NOPE + MLA_V)), KV_LORA),
        "g_q_nope": gain(MLA_NOPE),
        "g_q_pe": gain(MLA_ROPE),
        "g_k_nope": gain(MLA_NOPE),
        "g_k_pe": gain(MLA_ROPE),
        "g_gm_ln": gain(GM_WIDTH),
        "b_gm_ln": 0.02 * jax.random.normal(next(ks), (DEPTH, GM_WIDTH), jnp.float32),
        "w_spatial": w((GM_GROUPS, GM_CHUNK, GM_CHUNK), GM_CHUNK),
        "b_spatial": 1.0 + 0.02 * jax.random.normal(next(ks), (DEPTH, GM_GROUPS, GM_CHUNK), jnp.float32),
        "g_mem": gain(D_MODEL),
        "w_mem_kv": w((D_MODEL, 2 * MEM_HEADS * MEM_HEAD_DIM), D_MODEL),
        "g_mq": gain(MEM_HEAD_DIM),
        "g_mk": gain(MEM_HEAD_DIM),
        "w_o_gm": w((GM_WIDTH, D_MODEL), GM_WIDTH),
        "w_o_mla": w((MLA_HEADS * MLA_V, D_MODEL), MLA_HEADS * MLA_V),
        "w_o_mem": w((MEM_HEADS * MEM_HEAD_DIM, D_MODEL), MEM_HEADS * MEM_HEAD_DIM),
        "w_out": w((D_MODEL, D_MODEL), D_MODEL),
        "g_ffn": gain(D_MODEL),
        "w_ff1": w((D_MODEL, D_FF), D_MODEL),
        "w_ff2": w((D_FF, D_MODEL), D_FF),
    }


def reference(x, mem, positions, g_mix, w_in, g_cq, w_uq, g_ckv, w_ukv,
              g_q_nope, g_q_pe, g_k_nope, g_k_pe, g_gm_ln, b_gm_ln, w_spatial, b_spatial,
              g_mem, w_mem_kv, g_mq, g_mk, w_o_gm, w_o_mla, w_o_mem, w_out,
              g_ffn, w_ff1, w_ff2):
    cos, sin = rope_tables(positions)
    split_at = [GM_WIDTH, W_GM, W_GM + Q_LORA, W_GM + Q_LORA + KV_LORA,
                W_GM + W_MLA, W_GM + W_MLA + W_MEMQ]
    for l in range(DEPTH):
        h = rmsnorm(x, g_mix[l])
        z = h @ w_in[l]
        z_u, z_v, c_q, c_kv, k_pe, q_m, z_g = jnp.split(z, split_at, axis=-1)
        y_gm = gmlp_branch(z_u, z_v, g_gm_ln[l], b_gm_ln[l], w_spatial[l], b_spatial[l]) @ w_o_gm[l]
        y_mla = mla_branch(c_q, c_kv, k_pe, cos, sin, g_cq[l], w_uq[l], g_ckv[l], w_ukv[l],
                           g_q_nope[l], g_q_pe[l], g_k_nope[l], g_k_pe[l]) @ w_o_mla[l]
        y_mem = memory_branch(q_m, mem, g_mem[l], w_mem_kv[l], g_mq[l], g_mk[l]) @ w_o_mem[l]
        gates = jax.nn.sigmoid(z_g).reshape(z_g.shape[:-1] + (N_BRANCH, D_MODEL))
        merged = gates[..., 0, :] * y_gm + gates[..., 1, :] * y_mla + gates[..., 2, :] * y_mem
        x = x + merged @ w_out[l]
        h2 = rmsnorm(x, g_ffn[l])
        x = x + jnp.square(jax.nn.relu(h2 @ w_ff1[l])) @ w_ff2[l]
    return x
```

```python
import math
import numpy as np
from contextlib import ExitStack
import concourse.bass as bass
import concourse.mybir as mybir
from concourse.bass_utils import run_bass_kernel_spmd

F32 = mybir.dt.float32
BF16 = mybir.dt.bfloat16
I32 = mybir.dt.int32
AF = mybir.ActivationFunctionType
ALU = mybir.AluOpType
AX = mybir.AxisListType

ENGS = ("pe", "act", "dve", "pool", "sp")
GEN = 12000
EPS = 1e-6
NCORES = 8
DBG_STOP = None
DBG_KV = 9
SEQ_PER_CORE = 4
SEQ = 2048
D = 1024
T = 512
W_IN_COLS = 5312


class _Rec:
    def __init__(self):
        self.call = None

    def __getattr__(self, name):
        def f(*a, **k):
            self.call = (name, a, k)
            return self
        return f


def _freeze(fn):
    r = _Rec()
    fn(r)
    name, a, k = r.call
    return lambda h: getattr(h, name)(*a, **k)


class Sched:
    def __init__(self, nc, stack, n_gen=10):
        self.nc = nc
        self.ops = {e: [] for e in ENGS}
        self.cnt = {e: 0 for e in ENGS}
        self.sems = {e: [stack.enter_context(nc.semaphore(f"s_{e}_{g}")) for g in range(n_gen)]
                     for e in ("pe", "act", "dve", "pool")}
        self.n_gen = n_gen
        self.last_w = {}
        self.readers = {}
        self.seen = {e: {} for e in ENGS}
        self.dma_sems = {}
        self.dma_cnt = {}
        self.stack = stack
        self.null = False
        self.npe = 0
        self.marks = []

    def mark(self, name):
        if not self.null:
            self.marks.append((name, self.npe, self.cnt["act"], self.cnt["dve"]))

    def _wait(self, eng, tok):
        if tok is None:
            return
        if tok[0] == "eng":
            _, pe, seq = tok
            key = ("eng", pe)
            if self.seen[eng].get(key, 0) >= seq:
                return
            self.seen[eng][key] = seq
            g = (seq - 1) // GEN
            v = seq - g * GEN
            sem = self.sems[pe][g]
            self.ops[eng].append(lambda h, sem=sem, v=v: h.wait_ge(sem, v))
        else:
            _, name, val = tok
            key = ("dma", name)
            if self.seen[eng].get(key, 0) >= val:
                return
            self.seen[eng][key] = val
            sem = self.dma_sems[name]
            self.ops[eng].append(lambda h, sem=sem, v=val: h.wait_ge(sem, v))

    def _deps(self, eng, reads, writes):
        toks = []
        for r in reads:
            if isinstance(r, tuple) and r[0] == "ps":
                for t in self.readers.get(r, ()):
                    if not (t[0] == "eng" and t[1] == eng):
                        toks.append(t)
        for r in reads:
            t = self.last_w.get(r)
            if t is not None:
                toks.append(t)
        for w in writes:
            t = self.last_w.get(w)
            if t is not None:
                toks.append(t)
            toks.extend(self.readers.get(w, ()))
        for t in toks:
            if t[0] == "eng" and t[1] == eng and eng == "pe":
                continue
            self._wait(eng, t)

    def _commit(self, tok, reads, writes):
        for r in reads:
            self.readers.setdefault(r, []).append(tok)
        for w in writes:
            self.last_w[w] = tok
            self.readers[w] = []

    def op(self, eng, fn, reads=(), writes=()):
        if self.null:
            return None
        fn = _freeze(fn)
        if eng == "pe":
            self.npe += 1
        self._deps(eng, reads, writes)
        self.cnt[eng] += 1
        seq = self.cnt[eng]
        g = (seq - 1) // GEN
        assert g < self.n_gen, "semaphore generations exhausted"
        sem = self.sems[eng][g]
        self.ops[eng].append(lambda h, fn=fn, sem=sem: fn(h).then_inc(sem, 1))
        tok = ("eng", eng, seq)
        self._commit(tok, reads, writes)
        return tok

    def op_noinc(self, eng, fn, reads=(), wdeps=()):
        if self.null:
            return
        fn = _freeze(fn)
        if eng == "pe":
            self.npe += 1
        self._deps(eng, reads, wdeps)
        self.ops[eng].append(lambda h, fn=fn: fn(h))

    def dma(self, queue, name, fn, reads=(), writes=()):
        if self.null:
            return None
        if name not in self.dma_sems:
            self.dma_sems[name] = self.stack.enter_context(self.nc.semaphore(f"d_{name}"))
            self.dma_cnt[name] = 0
        fn = _freeze(fn)
        self._deps(queue, reads, writes)
        sem = self.dma_sems[name]
        self.dma_cnt[name] += 16
        self.ops[queue].append(lambda h, fn=fn, sem=sem: fn(h).then_inc(sem, 16))
        tok = ("dma", name, self.dma_cnt[name])
        self._commit(tok, reads, writes)
        return tok

    def emit(self):
        nc = self.nc
        with nc.Block() as block:
            @block.tensor
            def _(h):
                for f in self.ops["pe"]:
                    f(h)

            @block.scalar
            def _(h):
                for f in self.ops["act"]:
                    f(h)

            @block.vector
            def _(h):
                for f in self.ops["dve"]:
                    f(h)

            @block.gpsimd
            def _(h):
                for f in self.ops["pool"]:
                    f(h)

            @block.sync
            def _(h):
                for f in self.ops["sp"]:
                    f(h)


class Region:
    def __init__(self, nc, stack, name, nbytes):
        self.name = name
        self.t = stack.enter_context(nc.sbuf_tensor(name, [128, nbytes // 4], F32))
        self.tb = self.t.bitcast(BF16)
        self.nbytes = nbytes

    def f32(self, off, n):
        assert off % 4 == 0 and off + 4 * n <= self.nbytes
        return self.t[:, off // 4: off // 4 + n]

    def bf(self, off, n):
        assert off % 2 == 0 and off + 2 * n <= self.nbytes
        return self.tb[:, off // 2: off // 2 + n]

    def keys(self, off, nbytes):
        return [(self.name, p) for p in range(off // 1024, (off + nbytes - 1) // 1024 + 1)]


WEIGHTS = [("w_in", 1024, 5312), ("w_uq", 384, 1536), ("w_ukv", 256, 2048), ("w_mem_kv", 1024, 1024),
           ("w_o_gm", 512, 1024), ("w_o_mla", 1024, 1024), ("w_o_mem", 512, 1024), ("w_out", 1024, 1024),
           ("w_ff1", 1024, 4096), ("w_ff2", 4096, 1024)]
GAINS = [("g_mix", 1024), ("g_cq", 384), ("g_ckv", 256), ("g_q_nope", 128), ("g_q_pe", 64), ("g_k_nope", 128),
         ("g_k_pe", 64), ("g_gm_ln", 512), ("b_gm_ln", 512), ("b_spatial", 512), ("g_mem", 1024), ("g_mq", 128),
         ("g_mk", 128), ("g_ffn", 1024)]


def build(nseq=SEQ_PER_CORE, ntile=SEQ // T):
    nc = bass.Bass("TRN2", target_bir_lowering=False)
    dram = {}
    x_d = nc.dram_tensor("x", [SEQ_PER_CORE, SEQ, D], F32, kind="ExternalInput").ap()
    mem_d = nc.dram_tensor("mem", [SEQ_PER_CORE, 256, D], F32, kind="ExternalInput").ap()
    pos_d = nc.dram_tensor("positions", [SEQ_PER_CORE, SEQ], I32, kind="ExternalInput").ap()
    invf_d = nc.dram_tensor("invf", [1, 32], F32, kind="ExternalInput").ap()
    wsp_d = nc.dram_tensor("w_spatial", [512, 128], F32, kind="ExternalInput").ap()
    for n, k, m in WEIGHTS:
        dram[n] = nc.dram_tensor(n, [k, m], F32, kind="ExternalInput").ap()
        dram[n + "_b"] = nc.dram_tensor(n + "_b", [k, m], BF16, kind="Internal").ap()
    for n, m in GAINS:
        dram[n] = nc.dram_tensor(n, [1, m], F32, kind="ExternalInput").ap()
    out_d = nc.dram_tensor("out", [SEQ_PER_CORE, SEQ, D], F32, kind="ExternalOutput").ap()

    with ExitStack() as st:
        S = Sched(nc, st)
        sbt = lambda name, shape, dt: st.enter_context(nc.sbuf_tensor(name, shape, dt))
        pb = [st.enter_context(nc.psum_tensor(f"pb{i}", [128, 512], F32)) for i in range(8)]
        pbb = [p.bitcast(BF16) for p in pb]
        pk = [("ps", i) for i in range(8)]
        gp_state = [0]

        gp_n = [8]

        def gp():
            i = gp_state[0] % gp_n[0]
            gp_state[0] = (i + 1) % gp_n[0]
            return i

        ident = sbt("ident", [128, 128], BF16)
        identf = sbt("identf", [128, 128], F32)
        ones_b = sbt("ones_b", [128, 128], BF16)
        negtri = sbt("negtri", [128, 128], BF16)
        wcT = sbt("wcT", [128, 4, 128], BF16)
        gcol = sbt("gcol", [128, 3, 8], F32)
        cst = sbt("cst", [128, 4], F32)
        gcol2 = sbt("gcol2", [128, 2], F32)
        gb = {}
        for n, m in GAINS:
            if n in ("g_mix", "g_ffn", "g_mem"):
                continue
            gb[n] = sbt("gb_" + n, [128, m], F32)
        invf = sbt("invf_sb", [128, 32], F32)
        pos_i = sbt("pos_i", [128, 16], I32)
        pos_f = sbt("pos_f", [128, 16], F32)
        sincos = sbt("sincos", [128, 2, 2, 4, 32], F32)
        knT = sbt("knT", [128, 8, SEQ], BF16)
        kpT = sbt("kpT", [64, SEQ], BF16)
        vtok = sbt("vtok", [128, 16, 1024], BF16)
        mkT = sbt("mkT", [128, 4, 256], BF16)
        mv = sbt("mv", [128, 2, 512], BF16)
        RA = Region(nc, st, "RA", 33 * 1024)
        qmT = sbt("qmT", [128, 4, T], BF16)
        OmT = sbt("OmT", [128, 4, T], BF16)
        hT = sbt("hT", [128, 8, T], BF16)
        xmid = sbt("xmid", [128, 4, D], F32)
        xs = [sbt(f"xs{i}", [128, D], F32) for i in range(2)]
        pT_all = sbt("pT_all", [128, 4, T], BF16)
        pT = [pT_all[:, i, :] for i in range(4)]
        NSLOT = 6
        wring = sbt("wring", [128, NSLOT, 2304], BF16)
        AR = Region(nc, st, "AR", 14 * 1024)
        stat = sbt("stat", [128, 96], F32)

        A_uT, A_vtok, A_gmT, A_cqT, A_ckvT, A_qnT, A_qpT = 0, 4096, 8192, 12288, 15360, 17408, 25600
        uT = RA.bf(A_uT, 2048).rearrange("p (c n) -> p c n", n=T)
        v_tok = RA.bf(A_vtok, 2048).rearrange("p (c n) -> p c n", n=512)
        gmT = RA.bf(A_gmT, 2048).rearrange("p (c n) -> p c n", n=T)
        cqT = RA.bf(A_cqT, 1536).rearrange("p (c n) -> p c n", n=T)
        ckvT = RA.bf(A_ckvT, 1024).rearrange("p (c n) -> p c n", n=T)
        qnT = RA.bf(A_qnT, 4096).rearrange("p (c n) -> p c n", n=T)
        qpT = RA.bf(A_qpT, 4096).rearrange("p (c n) -> p c n", n=T)
        OT = RA.bf(0, 4096).rearrange("p (c n) -> p c n", n=T)
        mergedT = qnT
        actT = RA.bf(0, 16384).rearrange("p (c n) -> p c n", n=T)

        def kA(off, nbytes):
            return RA.keys(off, nbytes)

        class WS:
            plan = []
            idx = 0
            issued = 0
            slot_of = {}
            free = []
            groups = {}

        SLOT_ELEMS = 2304

        def wslot_view(slot, kc, n):
            return wring[:, slot, 0:kc * n].rearrange("p (c n) -> p c n", n=n)

        def w_issue(j, slot):
            name, k0, kc, n0, n = WS.plan[j]
            src = dram[name + "_b"][k0 * 128:(k0 + kc) * 128, n0:n0 + n].rearrange("(c p) n -> p c n", p=128)
            dst = wslot_view(slot, kc, n)
            S.dma("sp", f"w{slot}", lambda h, dst=dst, src=src: h.dma_start(out=dst, in_=src),
                  reads=[("wb", name)], writes=[("w", slot)])

        class WView:
            def __init__(self, parts, kper, nper):
                self.parts, self.kper, self.nper = parts, kper, nper

            def __getitem__(self, idx):
                _, kc, cs = idx
                ki, kr = divmod(kc, self.kper)
                if cs.start is None:
                    assert len(self.parts[ki]) == 1
                    return self.parts[ki][0][:, kr, :]
                ni = cs.start // self.nper
                assert (cs.stop - 1) // self.nper == ni
                return self.parts[ki][ni][:, kr, cs.start - ni * self.nper:cs.stop - ni * self.nper]

        def w_try_issue():
            while WS.issued < len(WS.plan) and WS.free:
                slot = WS.free.pop(0)
                WS.slot_of[WS.issued] = slot
                w_issue(WS.issued, slot)
                WS.issued += 1

        def wrelease(group):
            if S.null:
                return
            WS.free.extend(WS.groups.pop(group, []))
            w_try_issue()

        def wget(name, k0, kc, n0, n, hold=None, nper=None, keep=False, gate=False, group=None):
            nper = nper or n
            kper = kc if kc * nper <= SLOT_ELEMS else min(kc, 4)
            assert kper * nper <= SLOT_ELEMS and kc % kper == 0 and n % nper == 0
            phys = [(name, k0 + ki * kper, kper, n0 + ni * nper, nper) for ki in range(kc // kper) for ni in range(n // nper)]
            parts = [[None] * (n // nper) for _ in range(kc // kper)]
            keys = []
            if S.null:
                for q_, ph in enumerate(phys):
                    WS.plan.append(ph)
                    parts[q_ // (n // nper)][q_ % (n // nper)] = wslot_view(0, kper, nper)
                    keys.append(("w", 0))
                return WView(parts, kper, nper), keys
            if gate:
                group = "gate"
            if group is None:
                group = "default"
            if (group in ("default", "gate")) and not keep:
                WS.free.extend(WS.groups.pop(group, []))
            for q_, ph in enumerate(phys):
                i = WS.idx
                assert WS.plan[i] == ph
                WS.idx += 1
                if WS.issued <= i:
                    assert WS.free, "weight ring too small for the live set"
                    w_try_issue()
                slot = WS.slot_of.pop(i)
                WS.groups.setdefault(group, []).append(slot)
                parts[q_ // (n // nper)][q_ % (n // nper)] = wslot_view(slot, kper, nper)
                keys.append(("w", slot))
            w_try_issue()
            return WView(parts, kper, nper), keys

        def mmg(out_ap, pairs, wkey, reads, start=True, stop=True):
            n = len(pairs)
            reads = list(reads)
            for i, (l, r) in enumerate(pairs):
                s_, e_ = (start and i == 0), (stop and i == n - 1)
                fn = lambda h, l=l, r=r, s_=s_, e_=e_: h.matmul(out_ap, lhsT=l, rhs=r, start=s_, stop=e_)
                if i < n - 1:
                    S.op_noinc("pe", fn, reads=reads, wdeps=[wkey] if i == 0 else ())
                else:
                    S.op("pe", fn, reads=reads, writes=[wkey])

        def transposes(bank, items, reads):
            n = len(items)
            for i, (src, m, c0) in enumerate(items):
                fn = lambda h, src=src, m=m, c0=c0: h.transpose(pbb[bank][0:m, c0:c0 + 128], src, ident[:, :])
                if i < n - 1:
                    S.op_noinc("pe", fn, reads=list(reads) + ["ident"], wdeps=[pk[bank]] if i == 0 else ())
                else:
                    S.op("pe", fn, reads=list(reads) + ["ident"], writes=[pk[bank]])

        def interleave(*gens):
            live = list(gens)
            while live:
                for g_ in list(live):
                    try:
                        next(g_)
                    except StopIteration:
                        live.remove(g_)

        def rstd_from_ssq(ssq_ap, out_ap, n, inv_n, keys_in, keys_out, tcol=56):
            tmpk = ("stat", tcol // 8)
            tmp = stat[:, tcol:tcol + n]
            S.op("act", lambda h: h.activation(out=tmp, in_=ssq_ap, func=AF.Sqrt, scale=inv_n, bias=cst[:, 0:1]),
                 reads=list(keys_in) + ["cst"], writes=[tmpk])
            S.op("dve", lambda h: h.reciprocal(out=out_ap, in_=tmp), reads=[tmpk], writes=list(keys_out))

        S.op("pool", lambda h: h.memset(identf[:, :], 1.0), writes=["identf"])
        S.op("pool", lambda h: h.affine_select(out=identf[:, :], in_=identf[:, :], pattern=[[-1, 128]],
                                                compare_op=ALU.is_equal, fill=0.0, base=0, channel_multiplier=1),
             reads=["identf"], writes=["identf"])
        S.op("dve", lambda h: h.tensor_copy(out=ident[:, :], in_=identf[:, :]), reads=["identf"], writes=["ident"])
        S.op("dve", lambda h: h.memset(ones_b[:, :], 1.0), writes=["ones_b"])
        S.op("dve", lambda h: h.memset(cst[:, 0:1], EPS), writes=["cst"])
        S.op("dve", lambda h: h.memset(cst[:, 1:2], -math.pi), reads=["cst"], writes=["cst"])
        trif = AR.f32(0, 128)
        S.op("pool", lambda h: h.memset(trif, 0.0), writes=AR.keys(0, 512))
        S.op("pool", lambda h: h.affine_select(out=trif, in_=trif, pattern=[[1, 128]], compare_op=ALU.is_ge,
                                                fill=-30000.0, base=0, channel_multiplier=-1),
             reads=AR.keys(0, 512), writes=AR.keys(0, 512))
        S.op("dve", lambda h: h.tensor_copy(out=negtri[:, :], in_=trif), reads=AR.keys(0, 512), writes=["negtri"])
        for n, m in GAINS:
            if n in gb:
                S.dma("sp", "gl", lambda h, n=n: h.dma_start(out=gb[n][:, :], in_=dram[n][0, :].partition_broadcast(128)),
                      writes=[("gb", n)])
        S.dma("sp", "gl", lambda h: h.dma_start(out=invf[:, :], in_=invf_d[0, :].partition_broadcast(128)), writes=["invf"])
        for i, n in enumerate(("g_mix", "g_ffn", "g_mem")):
            S.dma("sp", "gl", lambda h, i=i, n=n: h.dma_start(out=gcol[:, i, :], in_=dram[n][0, :].rearrange("(c p) -> p c", p=128),
                                                               allow_slow_non_contiguous=True), writes=[("gcol", i)])
        for i, n in enumerate(("g_q_nope", "g_k_nope")):
            S.dma("sp", "gl", lambda h, i=i, n=n: h.dma_start(out=gcol2[:, i:i + 1], in_=dram[n][0, :].rearrange("(c p) -> p c", p=128),
                                                               allow_slow_non_contiguous=True), writes=[("gcol2", i)])
        gl_tok = ("dma", "gl", S.dma_cnt["gl"])
        for key in list(S.last_w.keys()):
            if S.last_w[key][0] == "dma" and S.last_w[key][1] == "gl":
                S.last_w[key] = gl_tok
        for g in range(4):
            wtmp = AR.f32(1024, 128)
            wtmp2 = AR.f32(2048, 128)
            S.dma("sp", "wsp", lambda h, g=g: h.dma_start(out=wtmp, in_=wsp_d[g * 128:(g + 1) * 128, :]),
                  writes=AR.keys(1024, 512))
            S.op("pe", lambda h: h.transpose(pb[0][:, 0:128], wtmp, identf[:, :]),
                 reads=AR.keys(1024, 512) + ["identf"], writes=[pk[0]])
            S.op("dve", lambda h: h.tensor_copy(out=wtmp2, in_=pb[0][:, 0:128]), reads=[pk[0]], writes=AR.keys(2048, 512))
            S.op("pool", lambda h, g=g: h.affine_select(out=wcT[:, g, :], in_=wtmp2, pattern=[[1, 128]], compare_op=ALU.is_ge,
                                                         fill=0.0, base=0, channel_multiplier=-1),
                 reads=AR.keys(2048, 512), writes=["wcT"])

        wdims = {n: (k, m) for n, k, m in WEIGHTS}
        for n in ("w_mem_kv", "w_in", "w_uq", "w_ukv", "w_o_gm", "w_o_mla", "w_o_mem", "w_out", "w_ff1", "w_ff2"):
            k, m = wdims[n]
            for r0 in range(0, k, 128):
                S.dma("pool", "cast_" + n, lambda h, n=n, r0=r0: h.dma_start(out=dram[n + "_b"][r0:r0 + 128, :],
                                                                             in_=dram[n][r0:r0 + 128, :]))
            S.last_w[("wb", n)] = ("dma", "cast_" + n, S.dma_cnt["cast_" + n])
            S.readers[("wb", n)] = []

        GM, GF, GME = 0, 1, 2

        def norm_to_T(src_ap, src_keys, gidx, dstT, dst_keys_fn, col0, nsub_cols=128):
            junk = AR.bf(0, 1024)
            jk = AR.keys(0, 2048)
            xn = AR.bf(2048, 1024)
            xk = AR.keys(2048, 2048)
            S.op("act", lambda h: h.activation(out=junk, in_=src_ap, func=AF.Square, accum_out=stat[:, 0:1]),
                 reads=src_keys, writes=jk + [("stat", 0)])
            rstd_from_ssq(stat[:, 0:1], stat[:, 1:2], 1, 1.0 / D, [("stat", 0)], [("stat", 0)])
            S.op("dve", lambda h: h.tensor_scalar(out=xn, in0=src_ap, scalar1=stat[:, 1:2], scalar2=None, op0=ALU.mult),
                 reads=list(src_keys) + [("stat", 0)], writes=xk)
            b = gp()
            transposes(b, [(xn[:, kc * 128:(kc + 1) * 128], 128, kc * 128) for kc in range(8)], xk)
            gbc = gcol[:, gidx, :].unsqueeze(2).to_broadcast([128, 8, 128])
            S.op("dve", lambda h: h.tensor_tensor(out=dstT[:, :, col0:col0 + 128],
                                                  in0=pbb[b][:, 0:1024].rearrange("p (c n) -> p c n", n=128),
                                                  in1=gbc, op=ALU.mult),
                 reads=[pk[b], ("gcol", gidx)], writes=dst_keys_fn())

        hT_keys = [("hT", 0)]

        def seq_prologue(b):
            S.dma("sp", "pos", lambda h: h.dma_start(out=pos_i[:, :], in_=pos_d[b, :].rearrange("(j p) -> p j", p=128),
                                                       allow_slow_non_contiguous=True), writes=["pos_i"])
            S.op("dve", lambda h: h.tensor_copy(out=pos_f[:, :], in_=pos_i[:, :]), reads=["pos_i"], writes=["pos_f"])
            for s in range(2):
                xsl = xs[s]
                S.dma("sp", f"xs{s}", lambda h, s=s, xsl=xsl: h.dma_start(out=xsl[:, :], in_=mem_d[b, s * 128:(s + 1) * 128, :]),
                      writes=[("xs", s)])
                norm_to_T(xsl[:, :], [("xs", s)], GME, hT, lambda: hT_keys, s * 128)
            wk, wkk = wget("w_mem_kv", 0, 8, 0, 512)
            for s in range(2):
                bk = gp()
                mmg(pb[bk][:, :], [(hT[:, kc, s * 128:(s + 1) * 128], wk[:, kc, :]) for kc in range(8)], pk[bk], hT_keys + list(wkk))
                junk = AR.bf(0, 512)
                for hh in range(4):
                    S.op("act", lambda h, hh=hh: h.activation(out=junk[:, 0:128], in_=pb[bk][:, hh * 128:(hh + 1) * 128],
                                                              func=AF.Square, accum_out=stat[:, 8 + hh:9 + hh]),
                         reads=[pk[bk]], writes=AR.keys(0, 1024) + [("stat", 1)])
                rstd_from_ssq(stat[:, 8:12], stat[:, 12:16], 4, 1.0 / 128, [("stat", 1)], [("stat", 1)])
                kn = AR.bf(4096, 512)
                for hh in range(4):
                    S.op("dve", lambda h, hh=hh: h.scalar_tensor_tensor(out=kn[:, hh * 128:(hh + 1) * 128],
                                                                        in0=pb[bk][:, hh * 128:(hh + 1) * 128],
                                                                        scalar=stat[:, 12 + hh:13 + hh], in1=gb["g_mk"][:, :],
                                                                        op0=ALU.mult, op1=ALU.mult),
                         reads=[pk[bk], ("stat", 1), ("gb", "g_mk")], writes=AR.keys(4096, 1024))
                bt = gp()
                transposes(bt, [(kn[:, hh * 128:(hh + 1) * 128], 128, hh * 128) for hh in range(4)], AR.keys(4096, 1024))
                S.op("act", lambda h, s=s, bt=bt: h.activation(out=mkT[:, :, s * 128:(s + 1) * 128],
                                                              in_=pbb[bt][:, 0:512].rearrange("p (c n) -> p c n", n=128),
                                                              func=AF.Copy),
                     reads=[pk[bt]], writes=["mkT"])
            wv, wvk = wget("w_mem_kv", 0, 8, 512, 512)
            for s in range(2):
                bk = gp()
                mmg(pb[bk][:, :], [(hT[:, kc, s * 128:(s + 1) * 128], wv[:, kc, :]) for kc in range(8)], pk[bk], hT_keys + list(wvk))
                S.op("act", lambda h, s=s, bk=bk: h.activation(out=mv[:, s, :], in_=pb[bk][:, :], func=AF.Copy),
                     reads=[pk[bk]], writes=["mv"])

        def tile_body(b, t):
            t0 = t * T
            gp_n[0] = 8
            S.mark(("tile", b, t))
            if t == 0:
                rope_tables(t)
            S.mark(("n1", b, t))
            for s in range(4):
                xsl = xs[s % 2]
                S.dma("sp", f"xs{s % 2}", lambda h, s=s, xsl=xsl: h.dma_start(out=xsl[:, :], in_=x_d[b, t0 + s * 128:t0 + (s + 1) * 128, :]),
                      writes=[("xs", s % 2)])
                norm_to_T(xsl[:, :], [("xs", s % 2)], GM, hT, lambda: hT_keys, s * 128)

            S.mark(("zu", b, t))
            xmid_b = xmid.bitcast(BF16)
            xs_b = [x_.bitcast(BF16) for x_ in xs]

            def gate_view(g):
                if g < 16:
                    return xmid_b[:, g // 4, (g % 4) * 512:(g % 4 + 1) * 512], ("xmid", g // 4)
                g2 = g - 16
                return xs_b[g2 // 4][:, (g2 % 4) * 512:(g2 % 4 + 1) * 512], ("xs", g2 // 4)

            def gate_gen():
                for mb in range(2):
                    for br in range(3):
                        wg, wgk = wget("w_in", 0, 8, 2240 + br * 1024 + mb * 512, 512, gate=True)
                        for m in range(4):
                            g = (mb * 3 + br) * 4 + m
                            bg = gp()
                            mmg(pb[bg][:, :], [(wg[:, kc, m * 128:(m + 1) * 128], hT[:, kc, :]) for kc in range(8)], pk[bg], hT_keys + list(wgk))
                            gv, gk = gate_view(g)
                            S.op("act", lambda h, bg=bg, gv=gv: h.activation(out=gv, in_=pb[bg][:, :], func=AF.Identity), reads=[pk[bg]], writes=[gk])
                            yield g

            gg = gate_gen()

            def pump(n):
                for _ in range(n):
                    next(gg, None)

            w, wkey = wget("w_in", 0, 8, 0, 512)
            for m in range(4):
                bk = gp()
                mmg(pb[bk][:, :], [(w[:, kc, m * 128:(m + 1) * 128], hT[:, kc, :]) for kc in range(8)], pk[bk], hT_keys + list(wkey))
                S.op("act", lambda h, m=m, bk=bk: h.activation(out=uT[:, m, :], in_=pb[bk][:, :], func=AF.Gelu_apprx_tanh),
                     reads=[pk[bk]], writes=kA(A_uT + m * 1024, 1024))
            S.mark(("zv", b, t))
            w, wkey = wget("w_in", 0, 8, 512, 512)
            for s in range(4):
                bk = gp()
                mmg(pb[bk][:, :], [(hT[:, kc, s * 128:(s + 1) * 128], w[:, kc, :]) for kc in range(8)], pk[bk], hT_keys + list(wkey))
                vg = AR.f32(s * 2048, 512)
                vgk = AR.keys(s * 2048, 2048)
                S.op("act", lambda h, bk=bk, vg=vg: h.activation(out=vg, in_=pb[bk][:, :], func=AF.Gelu_apprx_tanh), reads=[pk[bk]], writes=vgk)
                S.op("dve", lambda h, s=s, vg=vg: h.bn_stats(out=stat[:, 64 + 6 * s:70 + 6 * s], in_=vg), reads=vgk, writes=[("stat", 8), ("stat", 9), ("stat", 10)])
            for s in range(4):
                S.op("dve", lambda h, s=s: h.bn_aggr(out=stat[:, 16 + 2 * s:18 + 2 * s], in_=stat[:, 64 + 6 * s:70 + 6 * s]),
                     reads=[("stat", 8), ("stat", 9), ("stat", 10)], writes=[("stat", 2)])
            var4 = stat[:, 16:24].rearrange("p (s a) -> p s a", a=2)[:, :, 1:2]
            S.op("dve", lambda h: h.tensor_copy(out=stat[:, 28:32].unsqueeze(2), in_=var4), reads=[("stat", 2)], writes=[("stat", 3)])
            rstd_from_ssq(stat[:, 28:32], stat[:, 24:28], 4, 1.0, [("stat", 3)], [("stat", 3)])
            for s in range(4):
                vg = AR.f32(s * 2048, 512)
                vgk = AR.keys(s * 2048, 2048)
                S.op("dve", lambda h, s=s, vg=vg: h.tensor_scalar(out=vg, in0=vg, scalar1=stat[:, 16 + 2 * s:17 + 2 * s], scalar2=stat[:, 24 + s:25 + s],
                                                              op0=ALU.subtract, op1=ALU.mult),
                     reads=vgk + [("stat", 2), ("stat", 3)], writes=vgk)
                S.op("dve", lambda h, vg=vg: h.tensor_tensor(out=vg, in0=vg, in1=gb["g_gm_ln"][:, :], op=ALU.mult),
                     reads=vgk + [("gb", "g_gm_ln")], writes=vgk)
                S.op("dve", lambda h, s=s, vg=vg: h.tensor_tensor(out=v_tok[:, s, :], in0=vg, in1=gb["b_gm_ln"][:, :], op=ALU.add),
                     reads=vgk + [("gb", "b_gm_ln")], writes=kA(A_vtok + s * 1024, 1024))
            wrelease("default")
            S.mark(("c", b, t))
            w1, wkey1 = wget("w_in", 0, 8, 1024, 512, group="c")
            w2, wkey2 = wget("w_in", 0, 8, 1536, 192, group="c")
            wqm, wqmk = wget("w_in", 0, 8, 1728, 512, group="qm")

            def c_sub(s):
                b1 = gp()
                mmg(pb[b1][:, :], [(hT[:, kc, s * 128:(s + 1) * 128], w1[:, kc, :]) for kc in range(8)], pk[b1], hT_keys + list(wkey1))
                b2 = gp()
                mmg(pb[b2][:, 0:192], [(hT[:, kc, s * 128:(s + 1) * 128], w2[:, kc, :]) for kc in range(8)], pk[b2], hT_keys + list(wkey2))
                yield
                csb = AR.f32(0, 704)
                ck = AR.keys(0, 2816)
                S.op("act", lambda h, b1=b1: h.activation(out=csb[:, 0:512], in_=pb[b1][:, :], func=AF.Copy), reads=[pk[b1]], writes=ck)
                S.op("act", lambda h, b2=b2: h.activation(out=csb[:, 512:704], in_=pb[b2][:, 0:192], func=AF.Copy), reads=[pk[b2]] + ck, writes=ck)
                yield
                junk = AR.bf(4096, 512)
                jk = AR.keys(4096, 1024)
                for i, (c0, c1) in enumerate(((0, 384), (384, 640), (640, 704))):
                    S.op("act", lambda h, i=i, c0=c0, c1=c1: h.activation(out=junk[:, 0:c1 - c0], in_=csb[:, c0:c1], func=AF.Square,
                                                                        accum_out=stat[:, 26 + i:27 + i]),
                         reads=ck, writes=jk + [("stat", 3)])
                yield
                S.op("dve", lambda h: h.tensor_scalar(out=stat[:, 26:27], in0=stat[:, 26:27], scalar1=1.0 / 384, scalar2=None, op0=ALU.mult),
                     reads=[("stat", 3)], writes=[("stat", 3)])
                S.op("dve", lambda h: h.tensor_scalar(out=stat[:, 27:28], in0=stat[:, 27:28], scalar1=1.0 / 256, scalar2=None, op0=ALU.mult),
                     reads=[("stat", 3)], writes=[("stat", 3)])
                S.op("dve", lambda h: h.tensor_scalar(out=stat[:, 28:29], in0=stat[:, 28:29], scalar1=1.0 / 64, scalar2=None, op0=ALU.mult),
                     reads=[("stat", 3)], writes=[("stat", 3)])
                yield
                rstd_from_ssq(stat[:, 26:29], stat[:, 29:32], 3, 1.0, [("stat", 3)], [("stat", 3)], tcol=56)
                yield
                cn = AR.bf(5120, 640)
                cnk = AR.keys(5120, 1280)
                S.op("dve", lambda h: h.scalar_tensor_tensor(out=cn[:, 0:384], in0=csb[:, 0:384], scalar=stat[:, 29:30],
                                                             in1=gb["g_cq"][:, :], op0=ALU.mult, op1=ALU.mult),
                     reads=ck + [("stat", 3), ("gb", "g_cq")], writes=cnk)
                yield
                S.op("dve", lambda h: h.scalar_tensor_tensor(out=cn[:, 384:640], in0=csb[:, 384:640], scalar=stat[:, 30:31],
                                                             in1=gb["g_ckv"][:, :], op0=ALU.mult, op1=ALU.mult),
                     reads=ck + [("stat", 3), ("gb", "g_ckv")] + cnk, writes=cnk)
                kp = AR.f32(6400, 64)
                kpk = AR.keys(6400, 1024)
                S.op("dve", lambda h: h.scalar_tensor_tensor(out=kp, in0=csb[:, 640:704], scalar=stat[:, 31:32],
                                                             in1=gb["g_k_pe"][:, :], op0=ALU.mult, op1=ALU.mult),
                     reads=ck + [("stat", 3), ("gb", "g_k_pe")], writes=kpk)
                yield
                kr = AR.bf(6400 + 768, 64)
                rope(kp.rearrange("p (h d) -> p h d", d=64), kr.rearrange("p (h d) -> p h d", d=64), 1, s, kpk, 6400 + 256, t % 2)
                yield
                bt = gp()
                transposes(bt, [(cn[:, i * 128:(i + 1) * 128], 128, i * 128) for i in range(5)] + [(kr, 64, 640)], cnk + kpk)
                yield
                S.op("act", lambda h, s=s, bt=bt: h.activation(out=cqT[:, :, s * 128:(s + 1) * 128],
                                                              in_=pbb[bt][:, 0:384].rearrange("p (c n) -> p c n", n=128), func=AF.Copy),
                     reads=[pk[bt]], writes=kA(A_cqT, 3072))
                S.op("act", lambda h, s=s, bt=bt: h.activation(out=ckvT[:, :, s * 128:(s + 1) * 128],
                                                              in_=pbb[bt][:, 384:640].rearrange("p (c n) -> p c n", n=128), func=AF.Copy),
                     reads=[pk[bt]], writes=kA(A_ckvT, 2048))
                S.op("act", lambda h, s=s, bt=bt: h.activation(out=kpT[:, t0 + s * 128:t0 + (s + 1) * 128],
                                                              in_=pbb[bt][0:64, 640:768], func=AF.Copy),
                     reads=[pk[bt]], writes=["kpT"])
                yield

            def qm_sub(s):
                bk = gp()
                mmg(pb[bk][:, :], [(hT[:, kc, s * 128:(s + 1) * 128], wqm[:, kc, :]) for kc in range(8)], pk[bk], hT_keys + list(wqmk))
                yield
                junk = AR.bf(8192, 512)
                jk = AR.keys(8192, 1024)
                for hh in range(4):
                    S.op("act", lambda h, hh=hh, bk=bk: h.activation(out=junk[:, 0:128], in_=pb[bk][:, hh * 128:(hh + 1) * 128],
                                                                     func=AF.Square, accum_out=stat[:, 8 + hh:9 + hh]),
                         reads=[pk[bk]], writes=jk + [("stat", 1)])
                    if hh % 2 == 1:
                        yield
                rstd_from_ssq(stat[:, 8:12], stat[:, 12:16], 4, 1.0 / 128, [("stat", 1)], [("stat", 1)], tcol=60)
                yield
                qn = AR.bf(9216, 512)
                qnk = AR.keys(9216, 1024)
                for hh in range(4):
                    S.op("dve", lambda h, hh=hh, bk=bk: h.scalar_tensor_tensor(out=qn[:, hh * 128:(hh + 1) * 128],
                                                                               in0=pb[bk][:, hh * 128:(hh + 1) * 128],
                                                                               scalar=stat[:, 12 + hh:13 + hh], in1=gb["g_mq"][:, :],
                                                                               op0=ALU.mult, op1=ALU.mult),
                         reads=[pk[bk], ("stat", 1), ("gb", "g_mq")], writes=qnk)
                    if hh % 2 == 1:
                        yield
                bt = gp()
                transposes(bt, [(qn[:, hh * 128:(hh + 1) * 128], 128, hh * 128) for hh in range(4)], qnk)
                yield
                S.op("act", lambda h, s=s, bt=bt: h.activation(out=qmT[:, :, s * 128:(s + 1) * 128],
                                                              in_=pbb[bt][:, 0:512].rearrange("p (c n) -> p c n", n=128), func=AF.Copy),
                     reads=[pk[bt]], writes=["qmT"])
                yield

            for s in range(4):
                interleave(c_sub(s), qm_sub(s))
            wrelease("c")
            wrelease("qm")
            S.mark(("qm", b, t))

            if DBG_STOP == "win":
                return
            S.mark(("gmlp", b, t))
            for c in range(4):
                bk = gp()
                for g in range(4):
                    mmg(pb[bk][:, g * 128:(g + 1) * 128], [(v_tok[:, c, g * 128:(g + 1) * 128], wcT[:, g, :])], pk[bk],
                        kA(A_vtok + c * 1024, 1024) + ["wcT"])
                tmp = AR.f32(0, 512)
                tk = AR.keys(0, 2048)
                S.op("dve", lambda h, bk=bk: h.tensor_tensor(out=tmp, in0=pb[bk][:, :], in1=gb["b_spatial"][:, :], op=ALU.add),
                     reads=[pk[bk], ("gb", "b_spatial")], writes=tk)
                S.op("dve", lambda h, c=c: h.tensor_tensor(out=gmT[:, :, c * 128:(c + 1) * 128],
                                                            in0=tmp.rearrange("p (g n) -> p g n", n=128),
                                                            in1=uT[:, :, c * 128:(c + 1) * 128], op=ALU.mult),
                     reads=tk + kA(A_uT, 4096), writes=kA(A_gmT, 4096))

            if DBG_STOP == "gmlp":
                return
            S.mark(("q", b, t))
            wq_all, wq_key = wget("w_uq", 0, 3, 0, 1536, nper=768, group="q")
            wkv_all, wkv_key = wget("w_ukv", 0, 2, 0, 2048, nper=1024, group="kv")
            ktok = pT_all[:, :, :].rearrange("p a n -> p (a n)").bitcast(F32)
            kk = [("pT", i) for i in range(4)]
            omflat = OmT[:, :, :].rearrange("p a n -> p (a n)")
            knb = omflat[:, 0:1024].rearrange("p (h d) -> p h d", d=128)
            kjunk = omflat[:, 1024:1152]
            knbk = ["OmT"]

            def q_sub(s):
                qtok = AR.f32(0, 1536)
                qk = AR.keys(0, 6144)
                sq = AR.f32(6144, 384)
                sqk = AR.keys(6144, 1536)
                for j in range(4):
                    bk = gp()
                    mmg(pb[bk][:, 0:384], [(cqT[:, kc, s * 128:(s + 1) * 128], wq_all[:, kc, j * 384:(j + 1) * 384]) for kc in range(3)], pk[bk],
                        kA(A_cqT, 3072) + list(wq_key))
                    S.op("act", lambda h, bk=bk: h.activation(out=sq, in_=pb[bk][:, 0:384], func=AF.Square), reads=[pk[bk]], writes=sqk)
                    S.op("dve", lambda h, j=j: h.tensor_reduce(out=stat[:, 32 + 6 * j:38 + 6 * j], in_=sq.rearrange("p (a d) -> p a d", d=64),
                                                               axis=AX.X, op=ALU.add),
                         reads=sqk, writes=[("stat", 4), ("stat", 5), ("stat", 6)])
                    S.op("dve", lambda h, j=j, bk=bk: h.tensor_copy(out=qtok[:, j * 384:(j + 1) * 384], in_=pb[bk][:, 0:384]),
                         reads=[pk[bk]], writes=qk)
                    yield
                pump(3)
                st3 = stat[:, 32:56].rearrange("p (h a) -> p h a", a=3)
                S.op("dve", lambda h: h.tensor_tensor(out=stat[:, 0:8].unsqueeze(2), in0=st3[:, :, 0:1], in1=st3[:, :, 1:2], op=ALU.add),
                     reads=[("stat", 4), ("stat", 5), ("stat", 6)], writes=[("stat", 0)])
                S.op("dve", lambda h: h.tensor_copy(out=stat[:, 8:16].unsqueeze(2), in_=st3[:, :, 2:3]),
                     reads=[("stat", 4), ("stat", 5), ("stat", 6)], writes=[("stat", 1)])
                yield
                rstd_from_ssq(stat[:, 0:8], stat[:, 16:24], 8, 1.0 / 128, [("stat", 0)], [("stat", 2)], tcol=56)
                yield
                rstd_from_ssq(stat[:, 8:16], stat[:, 24:32], 8, 1.0 / 64, [("stat", 1)], [("stat", 3)], tcol=56)
                yield
                q3 = qtok.rearrange("p (h d) -> p h d", d=192)
                qnb = AR.bf(7680, 1024).rearrange("p (h d) -> p h d", d=128)
                qnbk = AR.keys(7680, 2048)
                S.op("dve", lambda h: h.tensor_tensor(out=qnb, in0=q3[:, :, 0:128],
                                                      in1=stat[:, 16:24].unsqueeze(2).to_broadcast([128, 8, 128]), op=ALU.mult),
                     reads=qk + [("stat", 2)], writes=qnbk)
                yield
                bt = gp()
                transposes(bt, [(qnb[:, hh, :], 128, hh * 128) for hh in range(8)], qnbk)
                S.op("dve", lambda h: h.tensor_tensor(out=q3[:, :, 128:192], in0=q3[:, :, 128:192],
                                                      in1=stat[:, 24:32].unsqueeze(2).to_broadcast([128, 8, 64]), op=ALU.mult),
                     reads=qk + [("stat", 3)], writes=qk)
                yield
                S.op("act", lambda h, s=s, bt=bt: h.activation(out=qnT[:, :, s * 128:(s + 1) * 128],
                                                              in_=pbb[bt][:, 0:1024].rearrange("p (c n) -> p c n", n=128), func=AF.Identity,
                                                              scale=gcol2[:, 0:1]),
                     reads=[pk[bt], ("gcol2", 0)], writes=kA(A_qnT, 8192))
                S.op("dve", lambda h: h.tensor_tensor(out=q3[:, :, 128:192], in0=q3[:, :, 128:192],
                                                      in1=gb["g_q_pe"][:, :].unsqueeze(1).to_broadcast([128, 8, 64]), op=ALU.mult),
                     reads=qk + [("gb", "g_q_pe")], writes=qk)
                yield
                qrb = AR.bf(9728, 512).rearrange("p (h d) -> p h d", d=64)
                qrbk = AR.keys(9728, 1024)
                rope(q3[:, :, 128:192], qrb, 8, s, qk, 10752, t % 2)
                yield
                pump(3)
                bt2 = gp()
                transposes(bt2, [(qrb[:, hh, :], 64, hh * 128) for hh in range(8)], qrbk)
                yield
                S.op("act", lambda h, s=s, bt2=bt2: h.activation(out=qpT[0:64, :, s * 128:(s + 1) * 128],
                                                                in_=pbb[bt2][0:64, 0:1024].rearrange("p (c n) -> p c n", n=128), func=AF.Copy),
                     reads=[pk[bt2]], writes=kA(A_qpT, 8192))
                yield

            def kv_sub(s):
                blk = t * 4 + s
                for j in range(4):
                    bk = gp()
                    mmg(pb[bk][:, :], [(ckvT[:, kc, s * 128:(s + 1) * 128], wkv_all[:, kc, j * 512:(j + 1) * 512]) for kc in range(2)], pk[bk],
                        kA(A_ckvT, 2048) + list(wkv_key))
                    p4 = pb[bk][:, :].rearrange("p (h a d) -> p h a d", a=2, d=128)
                    S.op("act", lambda h, j=j, p4=p4, blk=blk: h.activation(
                        out=vtok[:, blk, j * 256:(j + 1) * 256].rearrange("p (h d) -> p h d", d=128), in_=p4[:, :, 1, :], func=AF.Copy),
                         reads=[pk[bk]], writes=["vtok"])
                    for hh in range(2):
                        S.op("act", lambda h, j=j, hh=hh, bk=bk: h.activation(out=kjunk, in_=pb[bk][:, hh * 256:hh * 256 + 128], func=AF.Square,
                                                                             accum_out=stat[:, 64 + 2 * j + hh:64 + 2 * j + hh + 1]),
                             reads=[pk[bk]], writes=knbk + [("stat", 8)])
                    for hh in range(2):
                        S.op("dve", lambda h, j=j, hh=hh, bk=bk: h.tensor_copy(out=ktok[:, (2 * j + hh) * 128:(2 * j + hh + 1) * 128],
                                                                              in_=pb[bk][:, hh * 256:hh * 256 + 128]),
                             reads=[pk[bk]], writes=kk)
                    yield
                rstd_from_ssq(stat[:, 64:72], stat[:, 72:80], 8, 1.0 / 128, [("stat", 8)], [("stat", 9)], tcol=80)
                yield
                k3 = ktok.rearrange("p (h d) -> p h d", d=128)
                S.op("dve", lambda h: h.tensor_tensor(out=knb, in0=k3, in1=stat[:, 72:80].unsqueeze(2).to_broadcast([128, 8, 128]), op=ALU.mult),
                     reads=kk + [("stat", 9)], writes=knbk)
                yield
                bt = gp()
                transposes(bt, [(knb[:, hh, :], 128, hh * 128) for hh in range(8)], knbk)
                yield
                S.op("act", lambda h, s=s, bt=bt: h.activation(out=knT[:, :, t0 + s * 128:t0 + (s + 1) * 128],
                                                              in_=pbb[bt][:, 0:1024].rearrange("p (c n) -> p c n", n=128), func=AF.Identity,
                                                              scale=gcol2[:, 1:2]),
                     reads=[pk[bt], ("gcol2", 1)], writes=["knT"])
                yield

            for s in range(4):
                interleave(q_sub(s), kv_sub(s))
            wrelease("q")
            wrelease("kv")
            S.mark(("kv", b, t))

            if DBG_STOP == "kv":
                return
            pump(24)
            wrelease("gate")
            gp_n[0] = 4
            S.mark(("attn", b, t))
            items = []
            nkb = 4 * t + 4
            for hh in range(8):
                for kb in range(nkb):
                    items.append(("mla", hh, kb, nkb))
            for hh in range(4):
                for kb in range(2):
                    items.append(("mem", hh, kb, 2))
            sc_mla = 1.0 / math.sqrt(192.0)
            sc_mem = 1.0 / math.sqrt(128.0)
            pslot = [0]
            inflight = []

            def issue_S(idx):
                kind, hh, kb, n = items[idx]
                bk = gp()
                slot = pslot[0]
                pslot[0] = (slot + 1) % 4
                if kind == "mla":
                    r = kb - 4 * t
                    c0 = 128 * r if r > 0 else 0
                    pairs = [(knT[:, hh, kb * 128:(kb + 1) * 128], qnT[:, hh, c0:T])]
                    out = pb[bk][:, c0:T]
                    rd = ["knT", "kpT"] + kA(A_qnT, 8192) + kA(A_qpT, 8192)
                    if r >= 0:
                        fn0 = lambda h, l=pairs[0][0], rr=pairs[0][1], out=out: h.matmul(out, lhsT=l, rhs=rr, start=True, stop=False)
                        S.op_noinc("pe", fn0, reads=rd, wdeps=[pk[bk]])
                        fn1 = lambda h, bk=bk, c0=c0: h.matmul(pb[bk][:, c0:c0 + 128], lhsT=ident[:, :], rhs=negtri[:, :], start=False, stop=False)
                        S.op_noinc("pe", fn1, reads=["ident", "negtri"])
                        fn2 = lambda h, out=out, kb=kb, hh=hh, c0=c0: h.matmul(out, lhsT=kpT[:, kb * 128:(kb + 1) * 128], rhs=qpT[0:64, hh, c0:T],
                                                                               start=False, stop=True)
                        S.op("pe", fn2, reads=rd, writes=[pk[bk]])
                    else:
                        pairs.append((kpT[:, kb * 128:(kb + 1) * 128], qpT[0:64, hh, c0:T]))
                        mmg(out, pairs, pk[bk], rd)
                    S.op("act", lambda h, bk=bk, slot=slot, c0=c0: h.activation(out=pT[slot][:, c0:T], in_=pb[bk][:, c0:T], func=AF.Exp, scale=sc_mla),
                         reads=[pk[bk]], writes=[("pT", slot)])
                else:
                    c0 = 0
                    mmg(pb[bk][:, :], [(mkT[:, hh, kb * 128:(kb + 1) * 128], qmT[:, hh, :])], pk[bk], ["mkT", "qmT"])
                    S.op("act", lambda h, bk=bk, slot=slot: h.activation(out=pT[slot][:, :], in_=pb[bk][:, :], func=AF.Exp, scale=sc_mem),
                         reads=[pk[bk]], writes=[("pT", slot)])
                return slot, c0

            def issue_PV(idx, slot, c0):
                kind, hh, kb, n = items[idx]
                hidx = hh if kind == "mla" else 8 + hh
                bo = 4 + 2 * (hidx % 2)
                bz = bo + 1
                vsrc = vtok[:, kb, hh * 128:(hh + 1) * 128] if kind == "mla" else mv[:, kb, hh * 128:(hh + 1) * 128]
                vkey = "vtok" if kind == "mla" else "mv"
                st_, en_ = (kb == 0), (kb == n - 1)
                S.op("pe", lambda h, bo=bo, vsrc=vsrc, slot=slot, c0=c0, st_=st_, en_=en_: h.matmul(pb[bo][:, c0:T], lhsT=vsrc, rhs=pT[slot][:, c0:T], start=st_, stop=en_),
                     reads=[vkey, ("pT", slot)], writes=[pk[bo]])
                S.op("pe", lambda h, bz=bz, slot=slot, c0=c0, st_=st_, en_=en_: h.matmul(pb[bz][:, c0:T], lhsT=ones_b[:, :], rhs=pT[slot][:, c0:T], start=st_, stop=en_),
                     reads=["ones_b", ("pT", slot)], writes=[pk[bz]])
                if en_:
                    rz = AR.f32(0, 512)
                    rzk = AR.keys(0, 2048)
                    S.op("dve", lambda h, bz=bz: h.reciprocal(out=rz, in_=pb[bz][:, :]), reads=[pk[bz]], writes=rzk)
                    if kind == "mla":
                        dst, dk = OT[:, hh, :], kA(hh * 1024, 1024)
                    else:
                        dst, dk = OmT[:, hh, :], ["OmT"]
                    S.op("dve", lambda h, bo=bo, dst=dst: h.tensor_tensor(out=dst, in0=pb[bo][:, :], in1=rz, op=ALU.mult),
                         reads=[pk[bo]] + rzk, writes=dk)

            LOOK = 2
            pending = []
            for idx in range(len(items)):
                pending.append((idx,) + issue_S(idx))
                if len(pending) > LOOK:
                    i0, sl, c0 = pending.pop(0)
                    issue_PV(i0, sl, c0)
            while pending:
                i0, sl, c0 = pending.pop(0)
                issue_PV(i0, sl, c0)

            if DBG_STOP == "attn":
                return
            S.mark(("merge", b, t))
            branches = [("w_o_gm", gmT, 4, kA(A_gmT, 4096)), ("w_o_mla", OT, 8, kA(0, 8192)), ("w_o_mem", OmT, 4, ["OmT"])]
            accm = AR.f32(0, 2048).rearrange("p (m n) -> p m n", n=T)
            for mb in range(2):
                for br, (wn, src, nk, skeys) in enumerate(branches):
                    wo, wok = wget(wn, 0, nk, mb * 512, 512)
                    for m in range(4):
                        gt, gtk1 = gate_view((mb * 3 + br) * 4 + m)
                        gtk = [gtk1]
                        S.op("act", lambda h, gt=gt: h.activation(out=gt, in_=gt, func=AF.Sigmoid), reads=gtk, writes=gtk)
                        by = gp()
                        mmg(pb[by][:, :], [(wo[:, k, m * 128:(m + 1) * 128], src[:, k, :]) for k in range(nk)], pk[by], list(skeys) + list(wok))
                        ak = AR.keys(m * 2048, 2048)
                        if br == 0:
                            S.op("dve", lambda h, by=by, gt=gt, m=m: h.tensor_tensor(out=accm[:, m, :], in0=pb[by][:, :], in1=gt, op=ALU.mult),
                                 reads=[pk[by]] + gtk, writes=ak)
                        else:
                            tmp = AR.f32(10240 + (m % 2) * 2048, 512)
                            tk = AR.keys(10240 + (m % 2) * 2048, 2048)
                            S.op("dve", lambda h, by=by, gt=gt, tmp=tmp: h.tensor_tensor(out=tmp, in0=pb[by][:, :], in1=gt, op=ALU.mult),
                                 reads=[pk[by]] + gtk, writes=tk)
                            if br == 1:
                                S.op("pool", lambda h, m=m, tmp=tmp: h.tensor_tensor(out=accm[:, m, :], in0=accm[:, m, :], in1=tmp, op=ALU.add),
                                     reads=ak + tk, writes=ak)
                            else:
                                S.op("pool", lambda h, m=m, mb=mb, tmp=tmp: h.tensor_tensor(out=mergedT[:, mb * 4 + m, :], in0=accm[:, m, :], in1=tmp, op=ALU.add),
                                     reads=ak + tk, writes=kA(A_qnT + (mb * 4 + m) * 1024, 1024))

            if DBG_STOP == "merge":
                return
            S.mark(("wout", b, t))
            mkeys = kA(A_qnT, 8192)
            for c in range(2):
                w, wkey = wget("w_out", 0, 8, c * 512, 512)
                for s in range(4):
                    xi = (c * 4 + s) % 2
                    xsl = xs[xi]
                    S.dma("sp", f"xs{xi}", lambda h, s=s, c=c, xsl=xsl: h.dma_start(out=xsl[:, 0:512],
                                                                                   in_=x_d[b, t0 + s * 128:t0 + (s + 1) * 128, c * 512:(c + 1) * 512]),
                          writes=[("xs", xi)])
                    bk = gp()
                    mmg(pb[bk][:, :], [(mergedT[:, k, s * 128:(s + 1) * 128], w[:, k, :]) for k in range(8)], pk[bk], mkeys + list(wkey))
                    S.op("dve", lambda h, bk=bk, s=s, c=c, xsl=xsl: h.tensor_tensor(out=xmid[:, s, c * 512:(c + 1) * 512], in0=pb[bk][:, :],
                                                                                  in1=xsl[:, 0:512], op=ALU.add),
                         reads=[pk[bk], ("xs", xi)], writes=[("xmid", s)])
            if DBG_STOP == "xmid":
                return
            S.mark(("n2", b, t))
            for s in range(4):
                norm_to_T(xmid[:, s, :], [("xmid", s)], GF, hT, lambda: hT_keys, s * 128)
            S.mark(("ffn1", b, t))
            if t + 1 < ntile:
                rope_tables(t + 1)
            for jb in range(8):
                w, wkey = wget("w_ff1", 0, 8, jb * 512, 512)
                for m in range(4):
                    bk = gp()
                    mmg(pb[bk][:, :], [(w[:, kc, m * 128:(m + 1) * 128], hT[:, kc, :]) for kc in range(8)], pk[bk], hT_keys + list(wkey))
                    rt = AR.f32((m % 2) * 2048, 512)
                    rk = AR.keys((m % 2) * 2048, 2048)
                    S.op("act", lambda h, bk=bk, rt=rt: h.activation(out=rt, in_=pb[bk][:, :], func=AF.Relu), reads=[pk[bk]], writes=rk)
                    eng = "pool" if m % 2 == 0 else "dve"
                    S.op(eng, lambda h, rt=rt, jb=jb, m=m: h.tensor_tensor(out=actT[:, jb * 4 + m, :], in0=rt, in1=rt, op=ALU.mult),
                         reads=rk, writes=kA((jb * 4 + m) * 1024, 1024))
            S.mark(("ffn2", b, t))
            akeys = kA(0, 32768)
            for c in range(2):
                for kg in range(8):
                    w, wkey = wget("w_ff2", kg * 4, 4, c * 512, 512)
                    for s in range(4):
                        for k in range(4):
                            st_ = (kg == 0 and k == 0)
                            en_ = (kg == 7 and k == 3)
                            fn = lambda h, s=s, k=k, kg=kg, w=w, st_=st_, en_=en_: h.matmul(pb[4 + s][:, :], lhsT=actT[:, kg * 4 + k, s * 128:(s + 1) * 128],
                                                                                         rhs=w[:, k, :], start=st_, stop=en_)
                            if k < 3:
                                S.op_noinc("pe", fn, reads=akeys + list(wkey), wdeps=[pk[4 + s]] if st_ else ())
                            else:
                                S.op("pe", fn, reads=akeys + list(wkey), writes=[pk[4 + s]])
                for s in range(4):
                    S.op("dve", lambda h, s=s, c=c: h.tensor_tensor(out=xmid[:, s, c * 512:(c + 1) * 512], in0=pb[4 + s][:, :],
                                                                    in1=xmid[:, s, c * 512:(c + 1) * 512], op=ALU.add),
                         reads=[pk[4 + s], ("xmid", s)], writes=[("xmid", s)])
            for s in range(4):
                S.dma("sp", f"out{s}", lambda h, s=s: h.dma_start(out=out_d[b, t0 + s * 128:t0 + (s + 1) * 128, :], in_=xmid[:, s, :]),
                      reads=[("xmid", s)])

        def rope_tables(t):
            par = t % 2
            ak = AR.keys(8192, 4096)
            ang = AR.f32(8192, 128)
            yk = AR.f32(8192 + 512, 128)
            ki = AR.f32(8192 + 1024, 128).bitcast(I32)
            kf = AR.f32(8192 + 1536, 128)
            r1 = AR.f32(8192 + 2048, 128)
            m1 = AR.f32(8192 + 2560, 128)
            a2 = AR.f32(8192 + 3072, 128)
            S.op("dve", lambda h: h.tensor_tensor(out=ang.rearrange("p (s f) -> p s f", f=32),
                                                  in0=invf[:, :].unsqueeze(1).to_broadcast([128, 4, 32]),
                                                  in1=pos_f[:, t * 4:(t + 1) * 4].unsqueeze(2).to_broadcast([128, 4, 32]), op=ALU.mult),
                 reads=["invf", "pos_f"], writes=ak)
            for which, shift in ((0, 0.0), (1, 0.5 * math.pi)):
                src = ang
                if which == 1:
                    S.op("dve", lambda h: h.tensor_scalar(out=a2, in0=ang, scalar1=shift, scalar2=None, op0=ALU.add), reads=ak, writes=ak)
                    src = a2
                S.op("dve", lambda h, src=src: h.tensor_scalar(out=yk, in0=src, scalar1=1.0 / (2 * math.pi), scalar2=None, op0=ALU.mult),
                     reads=ak, writes=ak)
                S.op("dve", lambda h: h.tensor_copy(out=ki, in_=yk), reads=ak, writes=ak)
                S.op("dve", lambda h: h.tensor_copy(out=kf, in_=ki), reads=ak, writes=ak)
                S.op("dve", lambda h, src=src: h.scalar_tensor_tensor(out=r1, in0=kf, scalar=-2 * math.pi, in1=src, op0=ALU.mult, op1=ALU.add),
                     reads=ak, writes=ak)
                S.op("dve", lambda h: h.tensor_scalar(out=m1, in0=r1, scalar1=math.pi, scalar2=-2 * math.pi, op0=ALU.is_gt, op1=ALU.mult),
                     reads=ak, writes=ak)
                S.op("dve", lambda h: h.tensor_tensor(out=r1, in0=r1, in1=m1, op=ALU.add), reads=ak, writes=ak)
                S.op("act", lambda h, which=which: h.activation(out=sincos[:, par, which, :, :], in_=r1.rearrange("p (s f) -> p s f", f=32), func=AF.Sin),
                     reads=ak, writes=[("sincos", par)])

        def rope(x3, out3, nh, s, xkeys, scratch_off, par):
            sn = sincos[:, par, 0, s, :].unsqueeze(1).to_broadcast([128, nh, 32])
            cs = sincos[:, par, 1, s, :].unsqueeze(1).to_broadcast([128, nh, 32])
            n = nh * 32
            t1 = AR.f32(scratch_off, n).rearrange("p (h d) -> p h d", d=32)
            t2 = AR.f32(scratch_off + 4 * n, n).rearrange("p (h d) -> p h d", d=32)
            tk = AR.keys(scratch_off, 8 * n)
            x1, x2 = x3[:, :, 0:32], x3[:, :, 32:64]
            S.op("dve", lambda h: h.tensor_tensor(out=t1, in0=x1, in1=cs, op=ALU.mult), reads=list(xkeys) + [("sincos", par)], writes=tk)
            S.op("dve", lambda h: h.tensor_tensor(out=t2, in0=x2, in1=sn, op=ALU.mult), reads=list(xkeys) + [("sincos", par)], writes=tk)
            S.op("dve", lambda h: h.tensor_tensor(out=out3[:, :, 0:32], in0=t1, in1=t2, op=ALU.subtract), reads=tk, writes=ROPE_OUT_KEYS[0])
            S.op("dve", lambda h: h.tensor_tensor(out=t1, in0=x2, in1=cs, op=ALU.mult), reads=list(xkeys) + [("sincos", par)] + tk, writes=tk)
            S.op("dve", lambda h: h.tensor_tensor(out=t2, in0=x1, in1=sn, op=ALU.mult), reads=list(xkeys) + [("sincos", par)] + tk, writes=tk)
            S.op("dve", lambda h: h.tensor_tensor(out=out3[:, :, 32:64], in0=t1, in1=t2, op=ALU.add), reads=tk, writes=ROPE_OUT_KEYS[0])

        ROPE_OUT_KEYS = [AR.keys(6400, 14 * 1024 - 6400)]

        def program():
            WS.idx = 0
            WS.issued = 0
            WS.slot_of = {}
            WS.free = list(range(NSLOT))
            WS.groups = {}
            gp_state[0] = 0
            for b in range(nseq):
                if DBG_STOP == "setup":
                    return
                seq_prologue(b)
                if DBG_STOP == "prologue":
                    return
                for t in range(ntile):
                    tile_body(b, t)

        S.null = True
        program()
        S.null = False
        program()
        for s_ in range(4):
            if f"out{s_}" in S.dma_cnt:
                S._wait("sp", ("dma", f"out{s_}", S.dma_cnt[f"out{s_}"]))
        S.emit()
        S.mark(("end", 0, 0))
        build.info = dict(sbuf_remaining=nc.sbuf_bytes_remaining, cnt=dict(S.cnt), nblocks=len(WS.plan), marks=S.marks, npe=S.npe)
    return nc


_INVF = (10000.0 ** (-np.arange(0, 64, 2, dtype=np.float32) / 64.0)).astype(np.float32).reshape(1, 32)


def make_in_maps(inputs, ncores=NCORES):
    maps = []
    for c in range(ncores):
        sl = slice(c * SEQ_PER_CORE, (c + 1) * SEQ_PER_CORE)
        m = {"x": np.ascontiguousarray(inputs["x"][sl]),
             "mem": np.ascontiguousarray(inputs["mem"][sl]),
             "positions": np.ascontiguousarray(inputs["positions"][sl]).astype(np.int32),
             "invf": _INVF,
             "w_spatial": np.ascontiguousarray(inputs["w_spatial"][0].reshape(512, 128))}
        for n, k, mm_ in WEIGHTS:
            m[n] = np.ascontiguousarray(inputs[n][0])
        for n, mm_ in GAINS:
            m[n] = np.ascontiguousarray(inputs[n][0].reshape(1, mm_))
        maps.append(m)
    return maps


def kernel(**inputs):
    inputs = {k: np.asarray(v) for k, v in inputs.items()}
    nc = build()
    maps = make_in_maps(inputs)
    res = run_bass_kernel_spmd(nc, maps, core_ids=list(range(NCORES)))
    outs = [np.asarray(r["out"]) for r in res.results]
    return np.concatenate(outs, axis=0).astype(np.float32)
```

```python
import math
import numpy as np
from contextlib import ExitStack
import concourse.bass as bass
import concourse.mybir as mybir
from concourse.bass_utils import run_bass_kernel_spmd

F32 = mybir.dt.float32
BF16 = mybir.dt.bfloat16
I32 = mybir.dt.int32
AF = mybir.ActivationFunctionType
ALU = mybir.AluOpType
AX = mybir.AxisListType

ENGS = ("pe", "act", "dve", "pool", "sp")
GEN = 12000
EPS = 1e-6
NCORES = 8
DBG_STOP = None
DBG_KV = 9
SEQ_PER_CORE = 4
SEQ = 2048
D = 1024
T = 512
W_IN_COLS = 5312


class _Rec:
    def __init__(self):
        self.call = None

    def __getattr__(self, name):
        def f(*a, **k):
            self.call = (name, a, k)
            return self
        return f


def _freeze(fn):
    r = _Rec()
    fn(r)
    name, a, k = r.call
    return lambda h: getattr(h, name)(*a, **k)


class Sched:
    def __init__(self, nc, stack, n_gen=10):
        self.nc = nc
        self.ops = {e: [] for e in ENGS}
        self.cnt = {e: 0 for e in ENGS}
        self.sems = {e: [stack.enter_context(nc.semaphore(f"s_{e}_{g}")) for g in range(n_gen)]
                     for e in ("pe", "act", "dve", "pool")}
        self.n_gen = n_gen
        self.last_w = {}
        self.readers = {}
        self.seen = {e: {} for e in ENGS}
        self.dma_sems = {}
        self.dma_cnt = {}
        self.stack = stack
        self.null = False
        self.npe = 0
        self.marks = []

    def mark(self, name):
        if not self.null:
            self.marks.append((name, self.npe, self.cnt["act"], self.cnt["dve"]))

    def _wait(self, eng, tok):
        if tok is None:
            return
        if tok[0] == "eng":
            _, pe, seq = tok
            key = ("eng", pe)
            if self.seen[eng].get(key, 0) >= seq:
                return
            self.seen[eng][key] = seq
            g = (seq - 1) // GEN
            v = seq - g * GEN
            sem = self.sems[pe][g]
            self.ops[eng].append(lambda h, sem=sem, v=v: h.wait_ge(sem, v))
        else:
            _, name, val = tok
            key = ("dma", name)
            if self.seen[eng].get(key, 0) >= val:
                return
            self.seen[eng][key] = val
            sem = self.dma_sems[name]
            self.ops[eng].append(lambda h, sem=sem, v=val: h.wait_ge(sem, v))

    def _deps(self, eng, reads, writes):
        toks = []
        for r in reads:
            if isinstance(r, tuple) and r[0] == "ps":
                for t in self.readers.get(r, ()):
                    if not (t[0] == "eng" and t[1] == eng):
                        toks.append(t)
        for r in reads:
            t = self.last_w.get(r)
            if t is not None:
                toks.append(t)
        for w in writes:
            t = self.last_w.get(w)
            if t is not None:
                toks.append(t)
            toks.extend(self.readers.get(w, ()))
        for t in toks:
            if t[0] == "eng" and t[1] == eng and eng == "pe":
                continue
            self._wait(eng, t)

    def _commit(self, tok, reads, writes):
        for r in reads:
            self.readers.setdefault(r, []).append(tok)
        for w in writes:
            self.last_w[w] = tok
            self.readers[w] = []

    def op(self, eng, fn, reads=(), writes=()):
        if self.null:
            return None
        fn = _freeze(fn)
        if eng == "pe":
            self.npe += 1
        self._deps(eng, reads, writes)
        self.cnt[eng] += 1
        seq = self.cnt[eng]
        g = (seq - 1) // GEN
        assert g < self.n_gen, "semaphore generations exhausted"
        sem = self.sems[eng][g]
        self.ops[eng].append(lambda h, fn=fn, sem=sem: fn(h).then_inc(sem, 1))
        tok = ("eng", eng, seq)
        self._commit(tok, reads, writes)
        return tok

    def op_noinc(self, eng, fn, reads=(), wdeps=()):
        if self.null:
            return
        fn = _freeze(fn)
        if eng == "pe":
            self.npe += 1
        self._deps(eng, reads, wdeps)
        self.ops[eng].append(lambda h, fn=fn: fn(h))

    def dma(self, queue, name, fn, reads=(), writes=()):
        if self.null:
            return None
        if name not in self.dma_sems:
            self.dma_sems[name] = self.stack.enter_context(self.nc.semaphore(f"d_{name}"))
            self.dma_cnt[name] = 0
        fn = _freeze(fn)
        self._deps(queue, reads, writes)
        sem = self.dma_sems[name]
        self.dma_cnt[name] += 16
        self.ops[queue].append(lambda h, fn=fn, sem=sem: fn(h).then_inc(sem, 16))
        tok = ("dma", name, self.dma_cnt[name])
        self._commit(tok, reads, writes)
        return tok

    def emit(self):
        nc = self.nc
        with nc.Block() as block:
            @block.tensor
            def _(h):
                for f in self.ops["pe"]:
                    f(h)

            @block.scalar
            def _(h):
                for f in self.ops["act"]:
                    f(h)

            @block.vector
            def _(h):
                for f in self.ops["dve"]:
                    f(h)

            @block.gpsimd
            def _(h):
                for f in self.ops["pool"]:
                    f(h)

            @block.sync
            def _(h):
                for f in self.ops["sp"]:
                    f(h)


class Region:
    def __init__(self, nc, stack, name, nbytes):
        self.name = name
        self.t = stack.enter_context(nc.sbuf_tensor(name, [128, nbytes // 4], F32))
        self.tb = self.t.bitcast(BF16)
        self.nbytes = nbytes

    def f32(self, off, n):
        assert off % 4 == 0 and off + 4 * n <= self.nbytes
        return self.t[:, off // 4: off // 4 + n]

    def bf(self, off, n):
        assert off % 2 == 0 and off + 2 * n <= self.nbytes
        return self.tb[:, off // 2: off // 2 + n]

    def keys(self, off, nbytes):
        return [(self.name, p) for p in range(off // 1024, (off + nbytes - 1) // 1024 + 1)]


WEIGHTS = [("w_in", 1024, 5312), ("w_uq", 384, 1536), ("w_ukv", 256, 2048), ("w_mem_kv", 1024, 1024),
           ("w_o_gm", 512, 1024), ("w_o_mla", 1024, 1024), ("w_o_mem", 512, 1024), ("w_out", 1024, 1024),
           ("w_ff1", 1024, 4096), ("w_ff2", 4096, 1024)]
GAINS = [("g_mix", 1024), ("g_cq", 384), ("g_ckv", 256), ("g_q_nope", 128), ("g_q_pe", 64), ("g_k_nope", 128),
         ("g_k_pe", 64), ("g_gm_ln", 512), ("b_gm_ln", 512), ("b_spatial", 512), ("g_mem", 1024), ("g_mq", 128),
         ("g_mk", 128), ("g_ffn", 1024)]


def build(nseq=SEQ_PER_CORE, ntile=SEQ // T):
    nc = bass.Bass("TRN2", target_bir_lowering=False)
    dram = {}
    x_d = nc.dram_tensor("x", [SEQ_PER_CORE, SEQ, D], F32, kind="ExternalInput").ap()
    mem_d = nc.dram_tensor("mem", [SEQ_PER_CORE, 256, D], F32, kind="ExternalInput").ap()
    pos_d = nc.dram_tensor("positions", [SEQ_PER_CORE, SEQ], I32, kind="ExternalInput").ap()
    invf_d = nc.dram_tensor("invf", [1, 32], F32, kind="ExternalInput").ap()
    wsp_d = nc.dram_tensor("w_spatial", [512, 128], F32, kind="ExternalInput").ap()
    for n, k, m in WEIGHTS:
        dram[n] = nc.dram_tensor(n, [k, m], F32, kind="ExternalInput").ap()
        dram[n + "_b"] = nc.dram_tensor(n + "_b", [k, m], BF16, kind="Internal").ap()
    for n, m in GAINS:
        dram[n] = nc.dram_tensor(n, [1, m], F32, kind="ExternalInput").ap()
    out_d = nc.dram_tensor("out", [SEQ_PER_CORE, SEQ, D], F32, kind="ExternalOutput").ap()

    with ExitStack() as st:
        S = Sched(nc, st)
        sbt = lambda name, shape, dt: st.enter_context(nc.sbuf_tensor(name, shape, dt))
        pb = [st.enter_context(nc.psum_tensor(f"pb{i}", [128, 512], F32)) for i in range(8)]
        pbb = [p.bitcast(BF16) for p in pb]
        pk = [("ps", i) for i in range(8)]
        gp_state = [0]

        gp_n = [8]

        def gp():
            i = gp_state[0] % gp_n[0]
            gp_state[0] = (i + 1) % gp_n[0]
            return i

        ident = sbt("ident", [128, 128], BF16)
        identf = sbt("identf", [128, 128], F32)
        ones_b = sbt("ones_b", [128, 128], BF16)
        negtri = sbt("negtri", [128, 128], BF16)
        wcT = sbt("wcT", [128, 4, 128], BF16)
        gcol = sbt("gcol", [128, 3, 8], F32)
        cst = sbt("cst", [128, 4], F32)
        gcol2 = sbt("gcol2", [128, 2], F32)
        gb = {}
        for n, m in GAINS:
            if n in ("g_mix", "g_ffn", "g_mem"):
                continue
            gb[n] = sbt("gb_" + n, [128, m], F32)
        invf = sbt("invf_sb", [128, 32], F32)
        pos_i = sbt("pos_i", [128, 16], I32)
        pos_f = sbt("pos_f", [128, 16], F32)
        sincos = sbt("sincos", [128, 2, 2, 4, 32], F32)
        knT = sbt("knT", [128, 8, SEQ], BF16)
        kpT = sbt("kpT", [64, SEQ], BF16)
        vtok = sbt("vtok", [128, 16, 1024], BF16)
        mkT = sbt("mkT", [128, 4, 256], BF16)
        mv = sbt("mv", [128, 2, 512], BF16)
        RA = Region(nc, st, "RA", 33 * 1024)
        qmT = sbt("qmT", [128, 4, T], BF16)
        OmT = sbt("OmT", [128, 4, T], BF16)
        hT = sbt("hT", [128, 8, T], BF16)
        xmid = sbt("xmid", [128, 4, D], F32)
        xs = [sbt(f"xs{i}", [128, D], F32) for i in range(2)]
        pT_all = sbt("pT_all", [128, 4, T], BF16)
        pT = [pT_all[:, i, :] for i in range(4)]
        NSLOT = 6
        wring = sbt("wring", [128, NSLOT, 2304], BF16)
        AR = Region(nc, st, "AR", 14 * 1024)
        stat = sbt("stat", [128, 96], F32)

        A_uT, A_vtok, A_gmT, A_cqT, A_ckvT, A_qnT, A_qpT = 0, 4096, 8192, 12288, 15360, 17408, 25600
        uT = RA.bf(A_uT, 2048).rearrange("p (c n) -> p c n", n=T)
        v_tok = RA.bf(A_vtok, 2048).rearrange("p (c n) -> p c n", n=512)
        gmT = RA.bf(A_gmT, 2048).rearrange("p (c n) -> p c n", n=T)
        cqT = RA.bf(A_cqT, 1536).rearrange("p (c n) -> p c n", n=T)
        ckvT = RA.bf(A_ckvT, 1024).rearrange("p (c n) -> p c n", n=T)
        qnT = RA.bf(A_qnT, 4096).rearrange("p (c n) -> p c n", n=T)
        qpT = RA.bf(A_qpT, 4096).rearrange("p (c n) -> p c n", n=T)
        OT = RA.bf(0, 4096).rearrange("p (c n) -> p c n", n=T)
        mergedT = qnT
        actT = RA.bf(0, 16384).rearrange("p (c n) -> p c n", n=T)

        def kA(off, nbytes):
            return RA.keys(off, nbytes)

        class WS:
            plan = []
            idx = 0
            issued = 0
            slot_of = {}
            free = []
            groups = {}

        SLOT_ELEMS = 2304

        def wslot_view(slot, kc, n):
            return wring[:, slot, 0:kc * n].rearrange("p (c n) -> p c n", n=n)

        def w_issue(j, slot):
            name, k0, kc, n0, n = WS.plan[j]
            src = dram[name + "_b"][k0 * 128:(k0 + kc) * 128, n0:n0 + n].rearrange("(c p) n -> p c n", p=128)
            dst = wslot_view(slot, kc, n)
            S.dma("sp", f"w{slot}", lambda h, dst=dst, src=src: h.dma_start(out=dst, in_=src),
                  reads=[("wb", name)], writes=[("w", slot)])

        class WView:
            def __init__(self, parts, kper, nper):
                self.parts, self.kper, self.nper = parts, kper, nper

            def __getitem__(self, idx):
                _, kc, cs = idx
                ki, kr = divmod(kc, self.kper)
                if cs.start is None:
                    assert len(self.parts[ki]) == 1
                    return self.parts[ki][0][:, kr, :]
                ni = cs.start // self.nper
                assert (cs.stop - 1) // self.nper == ni
                return self.parts[ki][ni][:, kr, cs.start - ni * self.nper:cs.stop - ni * self.nper]

        def w_try_issue():
            while WS.issued < len(WS.plan) and WS.free:
                slot = WS.free.pop(0)
                WS.slot_of[WS.issued] = slot
                w_issue(WS.issued, slot)
                WS.issued += 1

        def wrelease(group):
            if S.null:
                return
            WS.free.extend(WS.groups.pop(group, []))
            w_try_issue()

        def wget(name, k0, kc, n0, n, hold=None, nper=None, keep=False, gate=False, group=None):
            nper = nper or n
            kper = kc if kc * nper <= SLOT_ELEMS else min(kc, 4)
            assert kper * nper <= SLOT_ELEMS and kc % kper == 0 and n % nper == 0
            phys = [(name, k0 + ki * kper, kper, n0 + ni * nper, nper) for ki in range(kc // kper) for ni in range(n // nper)]
            parts = [[None] * (n // nper) for _ in range(kc // kper)]
            keys = []
            if S.null:
                for q_, ph in enumerate(phys):
                    WS.plan.append(ph)
                    parts[q_ // (n // nper)][q_ % (n // nper)] = wslot_view(0, kper, nper)
                    keys.append(("w", 0))
                return WView(parts, kper, nper), keys
            if gate:
                group = "gate"
            if group is None:
                group = "default"
            if (group in ("default", "gate")) and not keep:
                WS.free.extend(WS.groups.pop(group, []))
            for q_, ph in enumerate(phys):
                i = WS.idx
                assert WS.plan[i] == ph
                WS.idx += 1
                if WS.issued <= i:
                    assert WS.free, "weight ring too small for the live set"
                    w_try_issue()
                slot = WS.slot_of.pop(i)
                WS.groups.setdefault(group, []).append(slot)
                parts[q_ // (n // nper)][q_ % (n // nper)] = wslot_view(slot, kper, nper)
                keys.append(("w", slot))
            w_try_issue()
            return WView(parts, kper, nper), keys

        def mmg(out_ap, pairs, wkey, reads, start=True, stop=True):
            n = len(pairs)
            reads = list(reads)
            for i, (l, r) in enumerate(pairs):
                s_, e_ = (start and i == 0), (stop and i == n - 1)
                fn = lambda h, l=l, r=r, s_=s_, e_=e_: h.matmul(out_ap, lhsT=l, rhs=r, start=s_, stop=e_)
                if i < n - 1:
                    S.op_noinc("pe", fn, reads=reads, wdeps=[wkey] if i == 0 else ())
                else:
                    S.op("pe", fn, reads=reads, writes=[wkey])

        def transposes(bank, items, reads):
            n = len(items)
            for i, (src, m, c0) in enumerate(items):
                fn = lambda h, src=src, m=m, c0=c0: h.transpose(pbb[bank][0:m, c0:c0 + 128], src, ident[:, :])
                if i < n - 1:
                    S.op_noinc("pe", fn, reads=list(reads) + ["ident"], wdeps=[pk[bank]] if i == 0 else ())
                else:
                    S.op("pe", fn, reads=list(reads) + ["ident"], writes=[pk[bank]])

        def interleave(*gens):
            live = list(gens)
            while live:
                for g_ in list(live):
                    try:
                        next(g_)
                    except StopIteration:
                        live.remove(g_)

        def rstd_from_ssq(ssq_ap, out_ap, n, inv_n, keys_in, keys_out, tcol=56):
            tmpk = ("stat", tcol // 8)
            tmp = stat[:, tcol:tcol + n]
            S.op("act", lambda h: h.activation(out=tmp, in_=ssq_ap, func=AF.Sqrt, scale=inv_n, bias=cst[:, 0:1]),
                 reads=list(keys_in) + ["cst"], writes=[tmpk])
            S.op("dve", lambda h: h.reciprocal(out=out_ap, in_=tmp), reads=[tmpk], writes=list(keys_out))

        S.op("pool", lambda h: h.memset(identf[:, :], 1.0), writes=["identf"])
        S.op("pool", lambda h: h.affine_select(out=identf[:, :], in_=identf[:, :], pattern=[[-1, 128]],
                                                compare_op=ALU.is_equal, fill=0.0, base=0, channel_multiplier=1),
             reads=["identf"], writes=["identf"])
        S.op("dve", lambda h: h.tensor_copy(out=ident[:, :], in_=identf[:, :]), reads=["identf"], writes=["ident"])
        S.op("dve", lambda h: h.memset(ones_b[:, :], 1.0), writes=["ones_b"])
        S.op("dve", lambda h: h.memset(cst[:, 0:1], EPS), writes=["cst"])
        S.op("dve", lambda h: h.memset(cst[:, 1:2], -math.pi), reads=["cst"], writes=["cst"])
        trif = AR.f32(0, 128)
        S.op("pool", lambda h: h.memset(trif, 0.0), writes=AR.keys(0, 512))
        S.op("pool", lambda h: h.affine_select(out=trif, in_=trif, pattern=[[1, 128]], compare_op=ALU.is_ge,
                                                fill=-30000.0, base=0, channel_multiplier=-1),
             reads=AR.keys(0, 512), writes=AR.keys(0, 512))
        S.op("dve", lambda h: h.tensor_copy(out=negtri[:, :], in_=trif), reads=AR.keys(0, 512), writes=["negtri"])
        for n, m in GAINS:
            if n in gb:
                S.dma("sp", "gl", lambda h, n=n: h.dma_start(out=gb[n][:, :], in_=dram[n][0, :].partition_broadcast(128)),
                      writes=[("gb", n)])
        S.dma("sp", "gl", lambda h: h.dma_start(out=invf[:, :], in_=invf_d[0, :].partition_broadcast(128)), writes=["invf"])
        for i, n in enumerate(("g_mix", "g_ffn", "g_mem")):
            S.dma("sp", "gl", lambda h, i=i, n=n: h.dma_start(out=gcol[:, i, :], in_=dram[n][0, :].rearrange("(c p) -> p c", p=128),
                                                               allow_slow_non_contiguous=True), writes=[("gcol", i)])
        for i, n in enumerate(("g_q_nope", "g_k_nope")):
            S.dma("sp", "gl", lambda h, i=i, n=n: h.dma_start(out=gcol2[:, i:i + 1], in_=dram[n][0, :].rearrange("(c p) -> p c", p=128),
                                                               allow_slow_non_contiguous=True), writes=[("gcol2", i)])
        gl_tok = ("dma", "gl", S.dma_cnt["gl"])
        for key in list(S.last_w.keys()):
            if S.last_w[key][0] == "dma" and S.last_w[key][1] == "gl":
                S.last_w[key] = gl_tok
        for g in range(4):
            wtmp = AR.f32(1024, 128)
            wtmp2 = AR.f32(2048, 128)
            S.dma("sp", "wsp", lambda h, g=g: h.dma_start(out=wtmp, in_=wsp_d[g * 128:(g + 1) * 128, :]),
                  writes=AR.keys(1024, 512))
            S.op("pe", lambda h: h.transpose(pb[0][:, 0:128], wtmp, identf[:, :]),
                 reads=AR.keys(1024, 512) + ["identf"], writes=[pk[0]])
            S.op("dve", lambda h: h.tensor_copy(out=wtmp2, in_=pb[0][:, 0:128]), reads=[pk[0]], writes=AR.keys(2048, 512))
            S.op("pool", lambda h, g=g: h.affine_select(out=wcT[:, g, :], in_=wtmp2, pattern=[[1, 128]], compare_op=ALU.is_ge,
                                                         fill=0.0, base=0, channel_multiplier=-1),
                 reads=AR.keys(2048, 512), writes=["wcT"])

        wdims = {n: (k, m) for n, k, m in WEIGHTS}
        for n in ("w_mem_kv", "w_in", "w_uq", "w_ukv", "w_o_gm", "w_o_mla", "w_o_mem", "w_out", "w_ff1", "w_ff2"):
            k, m = wdims[n]
            for r0 in range(0, k, 128):
                S.dma("pool", "cast_" + n, lambda h, n=n, r0=r0: h.dma_start(out=dram[n + "_b"][r0:r0 + 128, :],
                                                                             in_=dram[n][r0:r0 + 128, :]))
            S.last_w[("wb", n)] = ("dma", "cast_" + n, S.dma_cnt["cast_" + n])
            S.readers[("wb", n)] = []

        GM, GF, GME = 0, 1, 2

        def norm_chain(src_ap, src_keys):
            junk = AR.bf(0, 1024)
            jk = AR.keys(0, 2048)
            xn = AR.bf(2048, 1024)
            xk = AR.keys(2048, 2048)
            S.op("act", lambda h: h.activation(out=junk, in_=src_ap, func=AF.Square, accum_out=stat[:, 0:1]),
                 reads=src_keys, writes=jk + [("stat", 0)])
            rstd_from_ssq(stat[:, 0:1], stat[:, 1:2], 1, 1.0 / D, [("stat", 0)], [("stat", 0)])
            S.op("dve", lambda h: h.tensor_scalar(out=xn, in0=src_ap, scalar1=stat[:, 1:2], scalar2=None, op0=ALU.mult),
                 reads=list(src_keys) + [("stat", 0)], writes=xk)

        def norm_trans(gidx, dstT, dst_keys_fn, col0):
            xn = AR.bf(2048, 1024)
            xk = AR.keys(2048, 2048)
            b = gp()
            transposes(b, [(xn[:, kc * 128:(kc + 1) * 128], 128, kc * 128) for kc in range(8)], xk)
            gbc = gcol[:, gidx, :].unsqueeze(2).to_broadcast([128, 8, 128])
            S.op("dve", lambda h: h.tensor_tensor(out=dstT[:, :, col0:col0 + 128],
                                                  in0=pbb[b][:, 0:1024].rearrange("p (c n) -> p c n", n=128),
                                                  in1=gbc, op=ALU.mult),
                 reads=[pk[b], ("gcol", gidx)], writes=dst_keys_fn())

        def norm_to_T(src_ap, src_keys, gidx, dstT, dst_keys_fn, col0, nsub_cols=128):
            norm_chain(src_ap, src_keys)
            norm_trans(gidx, dstT, dst_keys_fn, col0)

        hT_keys = [("hT", 0)]

        def seq_prologue(b):
            S.dma("sp", "pos", lambda h: h.dma_start(out=pos_i[:, :], in_=pos_d[b, :].rearrange("(j p) -> p j", p=128),
                                                       allow_slow_non_contiguous=True), writes=["pos_i"])
            S.op("dve", lambda h: h.tensor_copy(out=pos_f[:, :], in_=pos_i[:, :]), reads=["pos_i"], writes=["pos_f"])
            for s in range(2):
                xsl = xs[s]
                S.dma("sp", f"xs{s}", lambda h, s=s, xsl=xsl: h.dma_start(out=xsl[:, :], in_=mem_d[b, s * 128:(s + 1) * 128, :]),
                      writes=[("xs", s)])
                norm_to_T(xsl[:, :], [("xs", s)], GME, hT, lambda: hT_keys, s * 128)
            wk, wkk = wget("w_mem_kv", 0, 8, 0, 512)
            for s in range(2):
                bk = gp()
                mmg(pb[bk][:, :], [(hT[:, kc, s * 128:(s + 1) * 128], wk[:, kc, :]) for kc in range(8)], pk[bk], hT_keys + list(wkk))
                junk = AR.bf(0, 512)
                for hh in range(4):
                    S.op("act", lambda h, hh=hh: h.activation(out=junk[:, 0:128], in_=pb[bk][:, hh * 128:(hh + 1) * 128],
                                                              func=AF.Square, accum_out=stat[:, 8 + hh:9 + hh]),
                         reads=[pk[bk]], writes=AR.keys(0, 1024) + [("stat", 1)])
                rstd_from_ssq(stat[:, 8:12], stat[:, 12:16], 4, 1.0 / 128, [("stat", 1)], [("stat", 1)])
                kn = AR.bf(4096, 512)
                for hh in range(4):
                    S.op("dve", lambda h, hh=hh: h.scalar_tensor_tensor(out=kn[:, hh * 128:(hh + 1) * 128],
                                                                        in0=pb[bk][:, hh * 128:(hh + 1) * 128],
                                                                        scalar=stat[:, 12 + hh:13 + hh], in1=gb["g_mk"][:, :],
                                                                        op0=ALU.mult, op1=ALU.mult),
                         reads=[pk[bk], ("stat", 1), ("gb", "g_mk")], writes=AR.keys(4096, 1024))
                bt = gp()
                transposes(bt, [(kn[:, hh * 128:(hh + 1) * 128], 128, hh * 128) for hh in range(4)], AR.keys(4096, 1024))
                S.op("act", lambda h, s=s, bt=bt: h.activation(out=mkT[:, :, s * 128:(s + 1) * 128],
                                                              in_=pbb[bt][:, 0:512].rearrange("p (c n) -> p c n", n=128),
                                                              func=AF.Copy),
                     reads=[pk[bt]], writes=["mkT"])
            wv, wvk = wget("w_mem_kv", 0, 8, 512, 512)
            for s in range(2):
                bk = gp()
                mmg(pb[bk][:, :], [(hT[:, kc, s * 128:(s + 1) * 128], wv[:, kc, :]) for kc in range(8)], pk[bk], hT_keys + list(wvk))
                S.op("act", lambda h, s=s, bk=bk: h.activation(out=mv[:, s, :], in_=pb[bk][:, :], func=AF.Copy),
                     reads=[pk[bk]], writes=["mv"])

        def tile_body(b, t, n1_done=False):
            t0 = t * T
            gp_n[0] = 8
            S.mark(("tile", b, t))
            if t == 0:
                rope_tables(t)
            S.mark(("n1", b, t))
            for s in range(4):
                if n1_done:
                    break
                xsl = xs[s % 2]
                S.dma("sp", f"xs{s % 2}", lambda h, s=s, xsl=xsl: h.dma_start(out=xsl[:, :], in_=x_d[b, t0 + s * 128:t0 + (s + 1) * 128, :]),
                      writes=[("xs", s % 2)])
                norm_to_T(xsl[:, :], [("xs", s % 2)], GM, hT, lambda: hT_keys, s * 128)

            S.mark(("zu", b, t))
            xmid_b = xmid.bitcast(BF16)
            xs_b = [x_.bitcast(BF16) for x_ in xs]

            def gate_view(g):
                if g < 16:
                    return xmid_b[:, g // 4, (g % 4) * 512:(g % 4 + 1) * 512], ("xmid", g // 4)
                g2 = g - 16
                return xs_b[g2 // 4][:, (g2 % 4) * 512:(g2 % 4 + 1) * 512], ("xs", g2 // 4)

            def gate_gen():
                for mb in range(2):
                    for br in range(3):
                        wg, wgk = wget("w_in", 0, 8, 2240 + br * 1024 + mb * 512, 512, gate=True)
                        for m in range(4):
                            g = (mb * 3 + br) * 4 + m
                            bg = gp()
                            mmg(pb[bg][:, :], [(wg[:, kc, m * 128:(m + 1) * 128], hT[:, kc, :]) for kc in range(8)], pk[bg], hT_keys + list(wgk))
                            gv, gk = gate_view(g)
                            S.op("act", lambda h, bg=bg, gv=gv: h.activation(out=gv, in_=pb[bg][:, :], func=AF.Identity), reads=[pk[bg]], writes=[gk])
                            yield g

            gg = gate_gen()

            def pump(n):
                for _ in range(n):
                    next(gg, None)

            w, wkey = wget("w_in", 0, 8, 0, 512)
            for m in range(4):
                bk = gp()
                mmg(pb[bk][:, :], [(w[:, kc, m * 128:(m + 1) * 128], hT[:, kc, :]) for kc in range(8)], pk[bk], hT_keys + list(wkey))
                S.op("act", lambda h, m=m, bk=bk: h.activation(out=uT[:, m, :], in_=pb[bk][:, :], func=AF.Gelu_apprx_tanh),
                     reads=[pk[bk]], writes=kA(A_uT + m * 1024, 1024))
            S.mark(("zv", b, t))
            w, wkey = wget("w_in", 0, 8, 512, 512)
            for s in range(4):
                bk = gp()
                mmg(pb[bk][:, :], [(hT[:, kc, s * 128:(s + 1) * 128], w[:, kc, :]) for kc in range(8)], pk[bk], hT_keys + list(wkey))
                vg = AR.f32(s * 2048, 512)
                vgk = AR.keys(s * 2048, 2048)
                S.op("act", lambda h, bk=bk, vg=vg: h.activation(out=vg, in_=pb[bk][:, :], func=AF.Gelu_apprx_tanh), reads=[pk[bk]], writes=vgk)
                S.op("dve", lambda h, s=s, vg=vg: h.bn_stats(out=stat[:, 64 + 6 * s:70 + 6 * s], in_=vg), reads=vgk, writes=[("stat", 8), ("stat", 9), ("stat", 10)])
            for s in range(4):
                S.op("dve", lambda h, s=s: h.bn_aggr(out=stat[:, 16 + 2 * s:18 + 2 * s], in_=stat[:, 64 + 6 * s:70 + 6 * s]),
                     reads=[("stat", 8), ("stat", 9), ("stat", 10)], writes=[("stat", 2)])
            var4 = stat[:, 16:24].rearrange("p (s a) -> p s a", a=2)[:, :, 1:2]
            S.op("dve", lambda h: h.tensor_copy(out=stat[:, 28:32].unsqueeze(2), in_=var4), reads=[("stat", 2)], writes=[("stat", 3)])
            rstd_from_ssq(stat[:, 28:32], stat[:, 24:28], 4, 1.0, [("stat", 3)], [("stat", 3)])
            for s in range(4):
                vg = AR.f32(s * 2048, 512)
                vgk = AR.keys(s * 2048, 2048)
                S.op("dve", lambda h, s=s, vg=vg: h.tensor_scalar(out=vg, in0=vg, scalar1=stat[:, 16 + 2 * s:17 + 2 * s], scalar2=stat[:, 24 + s:25 + s],
                                                              op0=ALU.subtract, op1=ALU.mult),
                     reads=vgk + [("stat", 2), ("stat", 3)], writes=vgk)
                S.op("dve", lambda h, vg=vg: h.tensor_tensor(out=vg, in0=vg, in1=gb["g_gm_ln"][:, :], op=ALU.mult),
                     reads=vgk + [("gb", "g_gm_ln")], writes=vgk)
                S.op("dve", lambda h, s=s, vg=vg: h.tensor_tensor(out=v_tok[:, s, :], in0=vg, in1=gb["b_gm_ln"][:, :], op=ALU.add),
                     reads=vgk + [("gb", "b_gm_ln")], writes=kA(A_vtok + s * 1024, 1024))
            wrelease("default")
            S.mark(("c", b, t))
            w1, wkey1 = wget("w_in", 0, 8, 1024, 512, group="c")
            w2, wkey2 = wget("w_in", 0, 8, 1536, 192, group="c")
            wqm, wqmk = wget("w_in", 0, 8, 1728, 512, group="qm")

            def c_sub(s):
                b1 = gp()
                mmg(pb[b1][:, :], [(hT[:, kc, s * 128:(s + 1) * 128], w1[:, kc, :]) for kc in range(8)], pk[b1], hT_keys + list(wkey1))
                b2 = gp()
                mmg(pb[b2][:, 0:192], [(hT[:, kc, s * 128:(s + 1) * 128], w2[:, kc, :]) for kc in range(8)], pk[b2], hT_keys + list(wkey2))
                yield
                csb = AR.f32(0, 704)
                ck = AR.keys(0, 2816)
                S.op("act", lambda h, b1=b1: h.activation(out=csb[:, 0:512], in_=pb[b1][:, :], func=AF.Copy), reads=[pk[b1]], writes=ck)
                S.op("act", lambda h, b2=b2: h.activation(out=csb[:, 512:704], in_=pb[b2][:, 0:192], func=AF.Copy), reads=[pk[b2]] + ck, writes=ck)
                yield
                junk = AR.bf(4096, 512)
                jk = AR.keys(4096, 1024)
                for i, (c0, c1) in enumerate(((0, 384), (384, 640), (640, 704))):
                    S.op("act", lambda h, i=i, c0=c0, c1=c1: h.activation(out=junk[:, 0:c1 - c0], in_=csb[:, c0:c1], func=AF.Square,
                                                                        accum_out=stat[:, 26 + i:27 + i]),
                         reads=ck, writes=jk + [("stat", 3)])
                yield
                S.op("dve", lambda h: h.tensor_scalar(out=stat[:, 26:27], in0=stat[:, 26:27], scalar1=1.0 / 384, scalar2=None, op0=ALU.mult),
                     reads=[("stat", 3)], writes=[("stat", 3)])
                S.op("dve", lambda h: h.tensor_scalar(out=stat[:, 27:28], in0=stat[:, 27:28], scalar1=1.0 / 256, scalar2=None, op0=ALU.mult),
                     reads=[("stat", 3)], writes=[("stat", 3)])
                S.op("dve", lambda h: h.tensor_scalar(out=stat[:, 28:29], in0=stat[:, 28:29], scalar1=1.0 / 64, scalar2=None, op0=ALU.mult),
                     reads=[("stat", 3)], writes=[("stat", 3)])
                yield
                rstd_from_ssq(stat[:, 26:29], stat[:, 29:32], 3, 1.0, [("stat", 3)], [("stat", 3)], tcol=56)
                yield
                cn = AR.bf(5120, 640)
                cnk = AR.keys(5120, 1280)
                S.op("dve", lambda h: h.scalar_tensor_tensor(out=cn[:, 0:384], in0=csb[:, 0:384], scalar=stat[:, 29:30],
                                                             in1=gb["g_cq"][:, :], op0=ALU.mult, op1=ALU.mult),
                     reads=ck + [("stat", 3), ("gb", "g_cq")], writes=cnk)
                yield
                S.op("dve", lambda h: h.scalar_tensor_tensor(out=cn[:, 384:640], in0=csb[:, 384:640], scalar=stat[:, 30:31],
                                                             in1=gb["g_ckv"][:, :], op0=ALU.mult, op1=ALU.mult),
                     reads=ck + [("stat", 3), ("gb", "g_ckv")] + cnk, writes=cnk)
                kp = AR.f32(6400, 64)
                kpk = AR.keys(6400, 1024)
                S.op("dve", lambda h: h.scalar_tensor_tensor(out=kp, in0=csb[:, 640:704], scalar=stat[:, 31:32],
                                                             in1=gb["g_k_pe"][:, :], op0=ALU.mult, op1=ALU.mult),
                     reads=ck + [("stat", 3), ("gb", "g_k_pe")], writes=kpk)
                yield
                kr = AR.bf(6400 + 768, 64)
                rope(kp.rearrange("p (h d) -> p h d", d=64), kr.rearrange("p (h d) -> p h d", d=64), 1, s, kpk, 6400 + 256, t % 2)
                yield
                bt = gp()
                transposes(bt, [(cn[:, i * 128:(i + 1) * 128], 128, i * 128) for i in range(5)] + [(kr, 64, 640)], cnk + kpk)
                yield
                S.op("act", lambda h, s=s, bt=bt: h.activation(out=cqT[:, :, s * 128:(s + 1) * 128],
                                                              in_=pbb[bt][:, 0:384].rearrange("p (c n) -> p c n", n=128), func=AF.Copy),
                     reads=[pk[bt]], writes=kA(A_cqT, 3072))
                S.op("act", lambda h, s=s, bt=bt: h.activation(out=ckvT[:, :, s * 128:(s + 1) * 128],
                                                              in_=pbb[bt][:, 384:640].rearrange("p (c n) -> p c n", n=128), func=AF.Copy),
                     reads=[pk[bt]], writes=kA(A_ckvT, 2048))
                S.op("act", lambda h, s=s, bt=bt: h.activation(out=kpT[:, t0 + s * 128:t0 + (s + 1) * 128],
                                                              in_=pbb[bt][0:64, 640:768], func=AF.Copy),
                     reads=[pk[bt]], writes=["kpT"])
                yield

            def qm_sub(s):
                bk = gp()
                mmg(pb[bk][:, :], [(hT[:, kc, s * 128:(s + 1) * 128], wqm[:, kc, :]) for kc in range(8)], pk[bk], hT_keys + list(wqmk))
                yield
                junk = AR.bf(8192, 512)
                jk = AR.keys(8192, 1024)
                for hh in range(4):
                    S.op("act", lambda h, hh=hh, bk=bk: h.activation(out=junk[:, 0:128], in_=pb[bk][:, hh * 128:(hh + 1) * 128],
                                                                     func=AF.Square, accum_out=stat[:, 8 + hh:9 + hh]),
                         reads=[pk[bk]], writes=jk + [("stat", 1)])
                    if hh % 2 == 1:
                        yield
                rstd_from_ssq(stat[:, 8:12], stat[:, 12:16], 4, 1.0 / 128, [("stat", 1)], [("stat", 1)], tcol=60)
                yield
                qn = AR.bf(9216, 512)
                qnk = AR.keys(9216, 1024)
                for hh in range(4):
                    S.op("dve", lambda h, hh=hh, bk=bk: h.scalar_tensor_tensor(out=qn[:, hh * 128:(hh + 1) * 128],
                                                                               in0=pb[bk][:, hh * 128:(hh + 1) * 128],
                                                                               scalar=stat[:, 12 + hh:13 + hh], in1=gb["g_mq"][:, :],
                                                                               op0=ALU.mult, op1=ALU.mult),
                         reads=[pk[bk], ("stat", 1), ("gb", "g_mq")], writes=qnk)
                    if hh % 2 == 1:
                        yield
                bt = gp()
                transposes(bt, [(qn[:, hh * 128:(hh + 1) * 128], 128, hh * 128) for hh in range(4)], qnk)
                yield
                S.op("act", lambda h, s=s, bt=bt: h.activation(out=qmT[:, :, s * 128:(s + 1) * 128],
                                                              in_=pbb[bt][:, 0:512].rearrange("p (c n) -> p c n", n=128), func=AF.Copy),
                     reads=[pk[bt]], writes=["qmT"])
                yield

            for s in range(4):
                interleave(c_sub(s), qm_sub(s))
            wrelease("c")
            wrelease("qm")
            S.mark(("qm", b, t))

            if DBG_STOP == "win":
                return
            S.mark(("gmlp", b, t))
            for c in range(4):
                bk = gp()
                for g in range(4):
                    mmg(pb[bk][:, g * 128:(g + 1) * 128], [(v_tok[:, c, g * 128:(g + 1) * 128], wcT[:, g, :])], pk[bk],
                        kA(A_vtok + c * 1024, 1024) + ["wcT"])
                tmp = AR.f32(0, 512)
                tk = AR.keys(0, 2048)
                S.op("dve", lambda h, bk=bk: h.tensor_tensor(out=tmp, in0=pb[bk][:, :], in1=gb["b_spatial"][:, :], op=ALU.add),
                     reads=[pk[bk], ("gb", "b_spatial")], writes=tk)
                S.op("dve", lambda h, c=c: h.tensor_tensor(out=gmT[:, :, c * 128:(c + 1) * 128],
                                                            in0=tmp.rearrange("p (g n) -> p g n", n=128),
                                                            in1=uT[:, :, c * 128:(c + 1) * 128], op=ALU.mult),
                     reads=tk + kA(A_uT, 4096), writes=kA(A_gmT, 4096))

            if DBG_STOP == "gmlp":
                return
            S.mark(("q", b, t))
            wq_all, wq_key = wget("w_uq", 0, 3, 0, 1536, nper=768, group="q")
            wkv_all, wkv_key = wget("w_ukv", 0, 2, 0, 2048, nper=1024, group="kv")
            ktok = pT_all[:, :, :].rearrange("p a n -> p (a n)").bitcast(F32)
            kk = [("pT", i) for i in range(4)]
            omflat = OmT[:, :, :].rearrange("p a n -> p (a n)")
            knb = omflat[:, 0:1024].rearrange("p (h d) -> p h d", d=128)
            kjunk = omflat[:, 1024:1152]
            knbk = ["OmT"]

            def q_sub(s):
                qtok = AR.f32(0, 1536)
                qk = AR.keys(0, 6144)
                sq = AR.f32(6144, 384)
                sqk = AR.keys(6144, 1536)
                for j in range(4):
                    bk = gp()
                    mmg(pb[bk][:, 0:384], [(cqT[:, kc, s * 128:(s + 1) * 128], wq_all[:, kc, j * 384:(j + 1) * 384]) for kc in range(3)], pk[bk],
                        kA(A_cqT, 3072) + list(wq_key))
                    S.op("act", lambda h, bk=bk: h.activation(out=sq, in_=pb[bk][:, 0:384], func=AF.Square), reads=[pk[bk]], writes=sqk)
                    S.op("dve", lambda h, j=j: h.tensor_reduce(out=stat[:, 32 + 6 * j:38 + 6 * j], in_=sq.rearrange("p (a d) -> p a d", d=64),
                                                               axis=AX.X, op=ALU.add),
                         reads=sqk, writes=[("stat", 4), ("stat", 5), ("stat", 6)])
                    S.op("dve", lambda h, j=j, bk=bk: h.tensor_copy(out=qtok[:, j * 384:(j + 1) * 384], in_=pb[bk][:, 0:384]),
                         reads=[pk[bk]], writes=qk)
                    yield
                pump(3)
                st3 = stat[:, 32:56].rearrange("p (h a) -> p h a", a=3)
                S.op("dve", lambda h: h.tensor_tensor(out=stat[:, 0:8].unsqueeze(2), in0=st3[:, :, 0:1], in1=st3[:, :, 1:2], op=ALU.add),
                     reads=[("stat", 4), ("stat", 5), ("stat", 6)], writes=[("stat", 0)])
                S.op("dve", lambda h: h.tensor_copy(out=stat[:, 8:16].unsqueeze(2), in_=st3[:, :, 2:3]),
                     reads=[("stat", 4), ("stat", 5), ("stat", 6)], writes=[("stat", 1)])
                yield
                rstd_from_ssq(stat[:, 0:8], stat[:, 16:24], 8, 1.0 / 128, [("stat", 0)], [("stat", 2)], tcol=56)
                yield
                rstd_from_ssq(stat[:, 8:16], stat[:, 24:32], 8, 1.0 / 64, [("stat", 1)], [("stat", 3)], tcol=56)
                yield
                q3 = qtok.rearrange("p (h d) -> p h d", d=192)
                qnb = AR.bf(7680, 1024).rearrange("p (h d) -> p h d", d=128)
                qnbk = AR.keys(7680, 2048)
                S.op("dve", lambda h: h.tensor_tensor(out=qnb, in0=q3[:, :, 0:128],
                                                      in1=stat[:, 16:24].unsqueeze(2).to_broadcast([128, 8, 128]), op=ALU.mult),
                     reads=qk + [("stat", 2)], writes=qnbk)
                yield
                bt = gp()
                transposes(bt, [(qnb[:, hh, :], 128, hh * 128) for hh in range(8)], qnbk)
                S.op("dve", lambda h: h.tensor_tensor(out=q3[:, :, 128:192], in0=q3[:, :, 128:192],
                                                      in1=stat[:, 24:32].unsqueeze(2).to_broadcast([128, 8, 64]), op=ALU.mult),
                     reads=qk + [("stat", 3)], writes=qk)
                yield
                S.op("act", lambda h, s=s, bt=bt: h.activation(out=qnT[:, :, s * 128:(s + 1) * 128],
                                                              in_=pbb[bt][:, 0:1024].rearrange("p (c n) -> p c n", n=128), func=AF.Identity,
                                                              scale=gcol2[:, 0:1]),
                     reads=[pk[bt], ("gcol2", 0)], writes=kA(A_qnT, 8192))
                S.op("dve", lambda h: h.tensor_tensor(out=q3[:, :, 128:192], in0=q3[:, :, 128:192],
                                                      in1=gb["g_q_pe"][:, :].unsqueeze(1).to_broadcast([128, 8, 64]), op=ALU.mult),
                     reads=qk + [("gb", "g_q_pe")], writes=qk)
                yield
                qrb = AR.bf(9728, 512).rearrange("p (h d) -> p h d", d=64)
                qrbk = AR.keys(9728, 1024)
                rope(q3[:, :, 128:192], qrb, 8, s, qk, 10752, t % 2)
                yield
                pump(3)
                bt2 = gp()
                transposes(bt2, [(qrb[:, hh, :], 64, hh * 128) for hh in range(8)], qrbk)
                yield
                S.op("act", lambda h, s=s, bt2=bt2: h.activation(out=qpT[0:64, :, s * 128:(s + 1) * 128],
                                                                in_=pbb[bt2][0:64, 0:1024].rearrange("p (c n) -> p c n", n=128), func=AF.Copy),
                     reads=[pk[bt2]], writes=kA(A_qpT, 8192))
                yield

            def kv_sub(s):
                blk = t * 4 + s
                for j in range(4):
                    bk = gp()
                    mmg(pb[bk][:, :], [(ckvT[:, kc, s * 128:(s + 1) * 128], wkv_all[:, kc, j * 512:(j + 1) * 512]) for kc in range(2)], pk[bk],
                        kA(A_ckvT, 2048) + list(wkv_key))
                    p4 = pb[bk][:, :].rearrange("p (h a d) -> p h a d", a=2, d=128)
                    S.op("act", lambda h, j=j, p4=p4, blk=blk: h.activation(
                        out=vtok[:, blk, j * 256:(j + 1) * 256].rearrange("p (h d) -> p h d", d=128), in_=p4[:, :, 1, :], func=AF.Copy),
                         reads=[pk[bk]], writes=["vtok"])
                    for hh in range(2):
                        S.op("act", lambda h, j=j, hh=hh, bk=bk: h.activation(out=kjunk, in_=pb[bk][:, hh * 256:hh * 256 + 128], func=AF.Square,
                                                                             accum_out=stat[:, 64 + 2 * j + hh:64 + 2 * j + hh + 1]),
                             reads=[pk[bk]], writes=knbk + [("stat", 8)])
                    for hh in range(2):
                        S.op("dve", lambda h, j=j, hh=hh, bk=bk: h.tensor_copy(out=ktok[:, (2 * j + hh) * 128:(2 * j + hh + 1) * 128],
                                                                              in_=pb[bk][:, hh * 256:hh * 256 + 128]),
                             reads=[pk[bk]], writes=kk)
                    yield
                rstd_from_ssq(stat[:, 64:72], stat[:, 72:80], 8, 1.0 / 128, [("stat", 8)], [("stat", 9)], tcol=80)
                yield
                k3 = ktok.rearrange("p (h d) -> p h d", d=128)
                S.op("dve", lambda h: h.tensor_tensor(out=knb, in0=k3, in1=stat[:, 72:80].unsqueeze(2).to_broadcast([128, 8, 128]), op=ALU.mult),
                     reads=kk + [("stat", 9)], writes=knbk)
                yield
                bt = gp()
                transposes(bt, [(knb[:, hh, :], 128, hh * 128) for hh in range(8)], knbk)
                yield
                S.op("act", lambda h, s=s, bt=bt: h.activation(out=knT[:, :, t0 + s * 128:t0 + (s + 1) * 128],
                                                              in_=pbb[bt][:, 0:1024].rearrange("p (c n) -> p c n", n=128), func=AF.Identity,
                                                              scale=gcol2[:, 1:2]),
                     reads=[pk[bt], ("gcol2", 1)], writes=["knT"])
                yield

            for s in range(4):
                interleave(q_sub(s), kv_sub(s))
            wrelease("q")
            wrelease("kv")
            S.mark(("kv", b, t))

            if DBG_STOP == "kv":
                return
            pump(24)
            wrelease("gate")
            gp_n[0] = 4
            S.mark(("attn", b, t))
            items = []
            nkb = 4 * t + 4
            for hh in range(8):
                for kb in range(nkb):
                    items.append(("mla", hh, kb, nkb))
            for hh in range(4):
                for kb in range(2):
                    items.append(("mem", hh, kb, 2))
            sc_mla = 1.0 / math.sqrt(192.0)
            sc_mem = 1.0 / math.sqrt(128.0)
            pslot = [0]
            inflight = []

            def issue_S(idx):
                kind, hh, kb, n = items[idx]
                bk = gp()
                slot = pslot[0]
                pslot[0] = (slot + 1) % 4
                if kind == "mla":
                    r = kb - 4 * t
                    c0 = 128 * r if r > 0 else 0
                    pairs = [(knT[:, hh, kb * 128:(kb + 1) * 128], qnT[:, hh, c0:T])]
                    out = pb[bk][:, c0:T]
                    rd = ["knT", "kpT"] + kA(A_qnT, 8192) + kA(A_qpT, 8192)
                    if r >= 0:
                        fn0 = lambda h, l=pairs[0][0], rr=pairs[0][1], out=out: h.matmul(out, lhsT=l, rhs=rr, start=True, stop=False)
                        S.op_noinc("pe", fn0, reads=rd, wdeps=[pk[bk]])
                        fn1 = lambda h, bk=bk, c0=c0: h.matmul(pb[bk][:, c0:c0 + 128], lhsT=ident[:, :], rhs=negtri[:, :], start=False, stop=False)
                        S.op_noinc("pe", fn1, reads=["ident", "negtri"])
                        fn2 = lambda h, out=out, kb=kb, hh=hh, c0=c0: h.matmul(out, lhsT=kpT[:, kb * 128:(kb + 1) * 128], rhs=qpT[0:64, hh, c0:T],
                                                                               start=False, stop=True)
                        S.op("pe", fn2, reads=rd, writes=[pk[bk]])
                    else:
                        pairs.append((kpT[:, kb * 128:(kb + 1) * 128], qpT[0:64, hh, c0:T]))
                        mmg(out, pairs, pk[bk], rd)
                    S.op("act", lambda h, bk=bk, slot=slot, c0=c0: h.activation(out=pT[slot][:, c0:T], in_=pb[bk][:, c0:T], func=AF.Exp, scale=sc_mla),
                         reads=[pk[bk]], writes=[("pT", slot)])
                else:
                    c0 = 0
                    mmg(pb[bk][:, :], [(mkT[:, hh, kb * 128:(kb + 1) * 128], qmT[:, hh, :])], pk[bk], ["mkT", "qmT"])
                    S.op("act", lambda h, bk=bk, slot=slot: h.activation(out=pT[slot][:, :], in_=pb[bk][:, :], func=AF.Exp, scale=sc_mem),
                         reads=[pk[bk]], writes=[("pT", slot)])
                return slot, c0

            def issue_PV(idx, slot, c0):
                kind, hh, kb, n = items[idx]
                hidx = hh if kind == "mla" else 8 + hh
                bo = 4 + 2 * (hidx % 2)
                bz = bo + 1
                vsrc = vtok[:, kb, hh * 128:(hh + 1) * 128] if kind == "mla" else mv[:, kb, hh * 128:(hh + 1) * 128]
                vkey = "vtok" if kind == "mla" else "mv"
                st_, en_ = (kb == 0), (kb == n - 1)
                S.op("pe", lambda h, bo=bo, vsrc=vsrc, slot=slot, c0=c0, st_=st_, en_=en_: h.matmul(pb[bo][:, c0:T], lhsT=vsrc, rhs=pT[slot][:, c0:T], start=st_, stop=en_),
                     reads=[vkey, ("pT", slot)], writes=[pk[bo]])
                S.op("pe", lambda h, bz=bz, slot=slot, c0=c0, st_=st_, en_=en_: h.matmul(pb[bz][:, c0:T], lhsT=ones_b[:, :], rhs=pT[slot][:, c0:T], start=st_, stop=en_),
                     reads=["ones_b", ("pT", slot)], writes=[pk[bz]])
                if en_:
                    rz = AR.f32(0, 512)
                    rzk = AR.keys(0, 2048)
                    S.op("dve", lambda h, bz=bz: h.reciprocal(out=rz, in_=pb[bz][:, :]), reads=[pk[bz]], writes=rzk)
                    if kind == "mla":
                        dst, dk = OT[:, hh, :], kA(hh * 1024, 1024)
                    else:
                        dst, dk = OmT[:, hh, :], ["OmT"]
                    S.op("dve", lambda h, bo=bo, dst=dst: h.tensor_tensor(out=dst, in0=pb[bo][:, :], in1=rz, op=ALU.mult),
                         reads=[pk[bo]] + rzk, writes=dk)

            LOOK = 2
            pending = []
            for idx in range(len(items)):
                pending.append((idx,) + issue_S(idx))
                if len(pending) > LOOK:
                    i0, sl, c0 = pending.pop(0)
                    issue_PV(i0, sl, c0)
            while pending:
                i0, sl, c0 = pending.pop(0)
                issue_PV(i0, sl, c0)

            if DBG_STOP == "attn":
                return
            S.mark(("merge", b, t))
            branches = [("w_o_gm", gmT, 4, kA(A_gmT, 4096)), ("w_o_mla", OT, 8, kA(0, 8192)), ("w_o_mem", OmT, 4, ["OmT"])]
            accm = AR.f32(0, 2048).rearrange("p (m n) -> p m n", n=T)
            for mb in range(2):
                for br, (wn, src, nk, skeys) in enumerate(branches):
                    wo, wok = wget(wn, 0, nk, mb * 512, 512)
                    for m in range(4):
                        gt, gtk1 = gate_view((mb * 3 + br) * 4 + m)
                        gtk = [gtk1]
                        S.op("act", lambda h, gt=gt: h.activation(out=gt, in_=gt, func=AF.Sigmoid), reads=gtk, writes=gtk)
                        by = gp()
                        mmg(pb[by][:, :], [(wo[:, k, m * 128:(m + 1) * 128], src[:, k, :]) for k in range(nk)], pk[by], list(skeys) + list(wok))
                        ak = AR.keys(m * 2048, 2048)
                        if br == 0:
                            S.op("dve", lambda h, by=by, gt=gt, m=m: h.tensor_tensor(out=accm[:, m, :], in0=pb[by][:, :], in1=gt, op=ALU.mult),
                                 reads=[pk[by]] + gtk, writes=ak)
                        else:
                            tmp = AR.f32(10240 + (m % 2) * 2048, 512)
                            tk = AR.keys(10240 + (m % 2) * 2048, 2048)
                            S.op("dve", lambda h, by=by, gt=gt, tmp=tmp: h.tensor_tensor(out=tmp, in0=pb[by][:, :], in1=gt, op=ALU.mult),
                                 reads=[pk[by]] + gtk, writes=tk)
                            if br == 1:
                                S.op("pool", lambda h, m=m, tmp=tmp: h.tensor_tensor(out=accm[:, m, :], in0=accm[:, m, :], in1=tmp, op=ALU.add),
                                     reads=ak + tk, writes=ak)
                            else:
                                S.op("pool", lambda h, m=m, mb=mb, tmp=tmp: h.tensor_tensor(out=mergedT[:, mb * 4 + m, :], in0=accm[:, m, :], in1=tmp, op=ALU.add),
                                     reads=ak + tk, writes=kA(A_qnT + (mb * 4 + m) * 1024, 1024))

            if DBG_STOP == "merge":
                return
            S.mark(("wout", b, t))
            mkeys = kA(A_qnT, 8192)
            for c in range(2):
                w, wkey = wget("w_out", 0, 8, c * 512, 512)
                for s in range(4):
                    xi = (c * 4 + s) % 2
                    xsl = xs[xi]
                    S.dma("sp", f"xs{xi}", lambda h, s=s, c=c, xsl=xsl: h.dma_start(out=xsl[:, 0:512],
                                                                                   in_=x_d[b, t0 + s * 128:t0 + (s + 1) * 128, c * 512:(c + 1) * 512]),
                          writes=[("xs", xi)])
                    bk = gp()
                    mmg(pb[bk][:, :], [(mergedT[:, k, s * 128:(s + 1) * 128], w[:, k, :]) for k in range(8)], pk[bk], mkeys + list(wkey))
                    S.op("dve", lambda h, bk=bk, s=s, c=c, xsl=xsl: h.tensor_tensor(out=xmid[:, s, c * 512:(c + 1) * 512], in0=pb[bk][:, :],
                                                                                  in1=xsl[:, 0:512], op=ALU.add),
                         reads=[pk[bk], ("xs", xi)], writes=[("xmid", s)])
            if DBG_STOP == "xmid":
                return
            S.mark(("n2", b, t))
            for s in range(4):
                norm_to_T(xmid[:, s, :], [("xmid", s)], GF, hT, lambda: hT_keys, s * 128)
            S.mark(("ffn1", b, t))
            if t + 1 < ntile:
                rope_tables(t + 1)
            for jb in range(8):
                w, wkey = wget("w_ff1", 0, 8, jb * 512, 512)
                for m in range(4):
                    bk = gp()
                    mmg(pb[bk][:, :], [(w[:, kc, m * 128:(m + 1) * 128], hT[:, kc, :]) for kc in range(8)], pk[bk], hT_keys + list(wkey))
                    rt = AR.f32((m % 2) * 2048, 512)
                    rk = AR.keys((m % 2) * 2048, 2048)
                    S.op("act", lambda h, bk=bk, rt=rt: h.activation(out=rt, in_=pb[bk][:, :], func=AF.Relu), reads=[pk[bk]], writes=rk)
                    eng = "pool" if m % 2 == 0 else "dve"
                    S.op(eng, lambda h, rt=rt, jb=jb, m=m: h.tensor_tensor(out=actT[:, jb * 4 + m, :], in0=rt, in1=rt, op=ALU.mult),
                         reads=rk, writes=kA((jb * 4 + m) * 1024, 1024))
            S.mark(("ffn2", b, t))
            akeys = kA(0, 32768)
            for c in range(2):
                for kg in range(8):
                    w, wkey = wget("w_ff2", kg * 4, 4, c * 512, 512)
                    for s in range(4):
                        for k in range(4):
                            st_ = (kg == 0 and k == 0)
                            en_ = (kg == 7 and k == 3)
                            fn = lambda h, s=s, k=k, kg=kg, w=w, st_=st_, en_=en_: h.matmul(pb[4 + s][:, :], lhsT=actT[:, kg * 4 + k, s * 128:(s + 1) * 128],
                                                                                         rhs=w[:, k, :], start=st_, stop=en_)
                            if k < 3:
                                S.op_noinc("pe", fn, reads=akeys + list(wkey), wdeps=[pk[4 + s]] if st_ else ())
                            else:
                                S.op("pe", fn, reads=akeys + list(wkey), writes=[pk[4 + s]])
                    q_ = c * 8 + kg
                    if t + 1 < ntile and q_ in (1, 3, 5, 7, 9):
                        s2 = (q_ - 1) // 2
                        if s2 >= 1:
                            norm_trans(GM, hT, lambda: hT_keys, (s2 - 1) * 128)
                        if s2 < 4:
                            xsl = xs[s2 % 2]
                            S.dma("sp", f"xs{s2 % 2}", lambda h, s2=s2, xsl=xsl: h.dma_start(out=xsl[:, :], in_=x_d[b, t0 + T + s2 * 128:t0 + T + (s2 + 1) * 128, :]),
                                  writes=[("xs", s2 % 2)])
                            norm_chain(xsl[:, :], [("xs", s2 % 2)])
                for s in range(4):
                    S.op("dve", lambda h, s=s, c=c: h.tensor_tensor(out=xmid[:, s, c * 512:(c + 1) * 512], in0=pb[4 + s][:, :],
                                                                    in1=xmid[:, s, c * 512:(c + 1) * 512], op=ALU.add),
                         reads=[pk[4 + s], ("xmid", s)], writes=[("xmid", s)])
            for s in range(4):
                S.dma("sp", f"out{s}", lambda h, s=s: h.dma_start(out=out_d[b, t0 + s * 128:t0 + (s + 1) * 128, :], in_=xmid[:, s, :]),
                      reads=[("xmid", s)])

        def rope_tables(t):
            par = t % 2
            ak = AR.keys(8192, 4096)
            ang = AR.f32(8192, 128)
            yk = AR.f32(8192 + 512, 128)
            ki = AR.f32(8192 + 1024, 128).bitcast(I32)
            kf = AR.f32(8192 + 1536, 128)
            r1 = AR.f32(8192 + 2048, 128)
            m1 = AR.f32(8192 + 2560, 128)
            a2 = AR.f32(8192 + 3072, 128)
            S.op("dve", lambda h: h.tensor_tensor(out=ang.rearrange("p (s f) -> p s f", f=32),
                                                  in0=invf[:, :].unsqueeze(1).to_broadcast([128, 4, 32]),
                                                  in1=pos_f[:, t * 4:(t + 1) * 4].unsqueeze(2).to_broadcast([128, 4, 32]), op=ALU.mult),
                 reads=["invf", "pos_f"], writes=ak)
            for which, shift in ((0, 0.0), (1, 0.5 * math.pi)):
                src = ang
                if which == 1:
                    S.op("dve", lambda h: h.tensor_scalar(out=a2, in0=ang, scalar1=shift, scalar2=None, op0=ALU.add), reads=ak, writes=ak)
                    src = a2
                S.op("dve", lambda h, src=src: h.tensor_scalar(out=yk, in0=src, scalar1=1.0 / (2 * math.pi), scalar2=None, op0=ALU.mult),
                     reads=ak, writes=ak)
                S.op("dve", lambda h: h.tensor_copy(out=ki, in_=yk), reads=ak, writes=ak)
                S.op("dve", lambda h: h.tensor_copy(out=kf, in_=ki), reads=ak, writes=ak)
                S.op("dve", lambda h, src=src: h.scalar_tensor_tensor(out=r1, in0=kf, scalar=-2 * math.pi, in1=src, op0=ALU.mult, op1=ALU.add),
                     reads=ak, writes=ak)
                S.op("dve", lambda h: h.tensor_scalar(out=m1, in0=r1, scalar1=math.pi, scalar2=-2 * math.pi, op0=ALU.is_gt, op1=ALU.mult),
                     reads=ak, writes=ak)
                S.op("dve", lambda h: h.tensor_tensor(out=r1, in0=r1, in1=m1, op=ALU.add), reads=ak, writes=ak)
                S.op("act", lambda h, which=which: h.activation(out=sincos[:, par, which, :, :], in_=r1.rearrange("p (s f) -> p s f", f=32), func=AF.Sin),
                     reads=ak, writes=[("sincos", par)])

        def rope(x3, out3, nh, s, xkeys, scratch_off, par):
            sn = sincos[:, par, 0, s, :].unsqueeze(1).to_broadcast([128, nh, 32])
            cs = sincos[:, par, 1, s, :].unsqueeze(1).to_broadcast([128, nh, 32])
            n = nh * 32
            t1 = AR.f32(scratch_off, n).rearrange("p (h d) -> p h d", d=32)
            t2 = AR.f32(scratch_off + 4 * n, n).rearrange("p (h d) -> p h d", d=32)
            tk = AR.keys(scratch_off, 8 * n)
            x1, x2 = x3[:, :, 0:32], x3[:, :, 32:64]
            S.op("dve", lambda h: h.tensor_tensor(out=t1, in0=x1, in1=cs, op=ALU.mult), reads=list(xkeys) + [("sincos", par)], writes=tk)
            S.op("dve", lambda h: h.tensor_tensor(out=t2, in0=x2, in1=sn, op=ALU.mult), reads=list(xkeys) + [("sincos", par)], writes=tk)
            S.op("dve", lambda h: h.tensor_tensor(out=out3[:, :, 0:32], in0=t1, in1=t2, op=ALU.subtract), reads=tk, writes=ROPE_OUT_KEYS[0])
            S.op("dve", lambda h: h.tensor_tensor(out=t1, in0=x2, in1=cs, op=ALU.mult), reads=list(xkeys) + [("sincos", par)] + tk, writes=tk)
            S.op("dve", lambda h: h.tensor_tensor(out=t2, in0=x1, in1=sn, op=ALU.mult), reads=list(xkeys) + [("sincos", par)] + tk, writes=tk)
            S.op("dve", lambda h: h.tensor_tensor(out=out3[:, :, 32:64], in0=t1, in1=t2, op=ALU.add), reads=tk, writes=ROPE_OUT_KEYS[0])

        ROPE_OUT_KEYS = [AR.keys(6400, 14 * 1024 - 6400)]

        def program():
            WS.idx = 0
            WS.issued = 0
            WS.slot_of = {}
            WS.free = list(range(NSLOT))
            WS.groups = {}
            gp_state[0] = 0
            for b in range(nseq):
                if DBG_STOP == "setup":
                    return
                seq_prologue(b)
                if DBG_STOP == "prologue":
                    return
                for t in range(ntile):
                    tile_body(b, t, n1_done=(t > 0))

        S.null = True
        program()
        S.null = False
        program()
        for s_ in range(4):
            if f"out{s_}" in S.dma_cnt:
                S._wait("sp", ("dma", f"out{s_}", S.dma_cnt[f"out{s_}"]))
        S.emit()
        S.mark(("end", 0, 0))
        build.info = dict(sbuf_remaining=nc.sbuf_bytes_remaining, cnt=dict(S.cnt), nblocks=len(WS.plan), marks=S.marks, npe=S.npe)
    return nc


_INVF = (10000.0 ** (-np.arange(0, 64, 2, dtype=np.float32) / 64.0)).astype(np.float32).reshape(1, 32)


def make_in_maps(inputs, ncores=NCORES):
    maps = []
    for c in range(ncores):
        sl = slice(c * SEQ_PER_CORE, (c + 1) * SEQ_PER_CORE)
        m = {"x": np.ascontiguousarray(inputs["x"][sl]),
             "mem": np.ascontiguousarray(inputs["mem"][sl]),
             "positions": np.ascontiguousarray(inputs["positions"][sl]).astype(np.int32),
             "invf": _INVF,
             "w_spatial": np.ascontiguousarray(inputs["w_spatial"][0].reshape(512, 128))}
        for n, k, mm_ in WEIGHTS:
            m[n] = np.ascontiguousarray(inputs[n][0])
        for n, mm_ in GAINS:
            m[n] = np.ascontiguousarray(inputs[n][0].reshape(1, mm_))
        maps.append(m)
    return maps


def kernel(**inputs):
    inputs = {k: np.asarray(v) for k, v in inputs.items()}
    nc = build()
    maps = make_in_maps(inputs)
    res = run_bass_kernel_spmd(nc, maps, core_ids=list(range(NCORES)))
    outs = [np.asarray(r["out"]) for r in res.results]
    return np.concatenate(outs, axis=0).astype(np.float32)
```

```python
import math
import numpy as np
from contextlib import ExitStack
import concourse.bass as bass
import concourse.mybir as mybir
from concourse.bass_utils import run_bass_kernel_spmd

F32 = mybir.dt.float32
BF16 = mybir.dt.bfloat16
I32 = mybir.dt.int32
AF = mybir.ActivationFunctionType
ALU = mybir.AluOpType
AX = mybir.AxisListType

ENGS = ("pe", "act", "dve", "pool", "sp")
GEN = 12000
EPS = 1e-6
NCORES = 8
DBG_STOP = None
DBG_KV = 9
SEQ_PER_CORE = 4
SEQ = 2048
D = 1024
T = 512
W_IN_COLS = 5312


class _Rec:
    def __init__(self):
        self.call = None

    def __getattr__(self, name):
        def f(*a, **k):
            self.call = (name, a, k)
            return self
        return f


def _freeze(fn):
    r = _Rec()
    fn(r)
    name, a, k = r.call
    return lambda h: getattr(h, name)(*a, **k)


class Sched:
    def __init__(self, nc, stack, n_gen=10):
        self.nc = nc
        self.ops = {e: [] for e in ENGS}
        self.cnt = {e: 0 for e in ENGS}
        self.sems = {e: [stack.enter_context(nc.semaphore(f"s_{e}_{g}")) for g in range(n_gen)]
                     for e in ("pe", "act", "dve", "pool")}
        self.n_gen = n_gen
        self.last_w = {}
        self.readers = {}
        self.seen = {e: {} for e in ENGS}
        self.dma_sems = {}
        self.dma_cnt = {}
        self.stack = stack
        self.null = False
        self.npe = 0
        self.marks = []

    def mark(self, name):
        if not self.null:
            self.marks.append((name, self.npe, self.cnt["act"], self.cnt["dve"]))

    def _wait(self, eng, tok):
        if tok is None:
            return
        if tok[0] == "eng":
            _, pe, seq = tok
            key = ("eng", pe)
            if self.seen[eng].get(key, 0) >= seq:
                return
            self.seen[eng][key] = seq
            g = (seq - 1) // GEN
            v = seq - g * GEN
            sem = self.sems[pe][g]
            self.ops[eng].append(lambda h, sem=sem, v=v: h.wait_ge(sem, v))
        else:
            _, name, val = tok
            key = ("dma", name)
            if self.seen[eng].get(key, 0) >= val:
                return
            self.seen[eng][key] = val
            sem = self.dma_sems[name]
            self.ops[eng].append(lambda h, sem=sem, v=val: h.wait_ge(sem, v))

    def _deps(self, eng, reads, writes):
        toks = []
        for r in reads:
            if isinstance(r, tuple) and r[0] == "ps":
                for t in self.readers.get(r, ()):
                    if not (t[0] == "eng" and t[1] == eng):
                        toks.append(t)
        for r in reads:
            t = self.last_w.get(r)
            if t is not None:
                toks.append(t)
        for w in writes:
            t = self.last_w.get(w)
            if t is not None:
                toks.append(t)
            toks.extend(self.readers.get(w, ()))
        for t in toks:
            if t[0] == "eng" and t[1] == eng and eng == "pe":
                continue
            self._wait(eng, t)

    def _commit(self, tok, reads, writes):
        for r in reads:
            self.readers.setdefault(r, []).append(tok)
        for w in writes:
            self.last_w[w] = tok
            self.readers[w] = []

    def op(self, eng, fn, reads=(), writes=()):
        if self.null:
            return None
        fn = _freeze(fn)
        if eng == "pe":
            self.npe += 1
        self._deps(eng, reads, writes)
        self.cnt[eng] += 1
        seq = self.cnt[eng]
        g = (seq - 1) // GEN
        assert g < self.n_gen, "semaphore generations exhausted"
        sem = self.sems[eng][g]
        self.ops[eng].append(lambda h, fn=fn, sem=sem: fn(h).then_inc(sem, 1))
        tok = ("eng", eng, seq)
        self._commit(tok, reads, writes)
        return tok

    def op_noinc(self, eng, fn, reads=(), wdeps=()):
        if self.null:
            return
        fn = _freeze(fn)
        if eng == "pe":
            self.npe += 1
        self._deps(eng, reads, wdeps)
        self.ops[eng].append(lambda h, fn=fn: fn(h))

    def dma(self, queue, name, fn, reads=(), writes=()):
        if self.null:
            return None
        if name not in self.dma_sems:
            self.dma_sems[name] = self.stack.enter_context(self.nc.semaphore(f"d_{name}"))
            self.dma_cnt[name] = 0
        fn = _freeze(fn)
        self._deps(queue, reads, writes)
        sem = self.dma_sems[name]
        self.dma_cnt[name] += 16
        self.ops[queue].append(lambda h, fn=fn, sem=sem: fn(h).then_inc(sem, 16))
        tok = ("dma", name, self.dma_cnt[name])
        self._commit(tok, reads, writes)
        return tok

    def emit(self):
        nc = self.nc
        with nc.Block() as block:
            @block.tensor
            def _(h):
                for f in self.ops["pe"]:
                    f(h)

            @block.scalar
            def _(h):
                for f in self.ops["act"]:
                    f(h)

            @block.vector
            def _(h):
                for f in self.ops["dve"]:
                    f(h)

            @block.gpsimd
            def _(h):
                for f in self.ops["pool"]:
                    f(h)

            @block.sync
            def _(h):
                for f in self.ops["sp"]:
                    f(h)


class Region:
    def __init__(self, nc, stack, name, nbytes):
        self.name = name
        self.t = stack.enter_context(nc.sbuf_tensor(name, [128, nbytes // 4], F32))
        self.tb = self.t.bitcast(BF16)
        self.nbytes = nbytes

    def f32(self, off, n):
        assert off % 4 == 0 and off + 4 * n <= self.nbytes
        return self.t[:, off // 4: off // 4 + n]

    def bf(self, off, n):
        assert off % 2 == 0 and off + 2 * n <= self.nbytes
        return self.tb[:, off // 2: off // 2 + n]

    def keys(self, off, nbytes):
        return [(self.name, p) for p in range(off // 1024, (off + nbytes - 1) // 1024 + 1)]


WEIGHTS = [("w_in", 1024, 5312), ("w_uq", 384, 1536), ("w_ukv", 256, 2048), ("w_mem_kv", 1024, 1024),
           ("w_o_gm", 512, 1024), ("w_o_mla", 1024, 1024), ("w_o_mem", 512, 1024), ("w_out", 1024, 1024),
           ("w_ff1", 1024, 4096), ("w_ff2", 4096, 1024)]
GAINS = [("g_mix", 1024), ("g_cq", 384), ("g_ckv", 256), ("g_q_nope", 128), ("g_q_pe", 64), ("g_k_nope", 128),
         ("g_k_pe", 64), ("g_gm_ln", 512), ("b_gm_ln", 512), ("b_spatial", 512), ("g_mem", 1024), ("g_mq", 128),
         ("g_mk", 128), ("g_ffn", 1024)]


def build(nseq=SEQ_PER_CORE, ntile=SEQ // T):
    nc = bass.Bass("TRN2", target_bir_lowering=False)
    dram = {}
    x_d = nc.dram_tensor("x", [SEQ_PER_CORE, SEQ, D], F32, kind="ExternalInput").ap()
    mem_d = nc.dram_tensor("mem", [SEQ_PER_CORE, 256, D], F32, kind="ExternalInput").ap()
    pos_d = nc.dram_tensor("positions", [SEQ_PER_CORE, SEQ], I32, kind="ExternalInput").ap()
    invf_d = nc.dram_tensor("invf", [1, 32], F32, kind="ExternalInput").ap()
    wsp_d = nc.dram_tensor("w_spatial", [512, 128], F32, kind="ExternalInput").ap()
    for n, k, m in WEIGHTS:
        dram[n] = nc.dram_tensor(n, [k, m], F32, kind="ExternalInput").ap()
        dram[n + "_b"] = nc.dram_tensor(n + "_b", [k, m], BF16, kind="Internal").ap()
    for n, m in GAINS:
        dram[n] = nc.dram_tensor(n, [1, m], F32, kind="ExternalInput").ap()
    out_d = nc.dram_tensor("out", [SEQ_PER_CORE, SEQ, D], F32, kind="ExternalOutput").ap()

    with ExitStack() as st:
        S = Sched(nc, st)
        sbt = lambda name, shape, dt: st.enter_context(nc.sbuf_tensor(name, shape, dt))
        pb = [st.enter_context(nc.psum_tensor(f"pb{i}", [128, 512], F32)) for i in range(8)]
        pbb = [p.bitcast(BF16) for p in pb]
        pk = [("ps", i) for i in range(8)]
        gp_state = [0]

        gp_n = [8]

        def gp():
            i = gp_state[0] % gp_n[0]
            gp_state[0] = (i + 1) % gp_n[0]
            return i

        ident = sbt("ident", [128, 128], BF16)
        identf = sbt("identf", [128, 128], F32)
        ones_b = sbt("ones_b", [128, 128], BF16)
        negtri = sbt("negtri", [128, 128], BF16)
        wcT = sbt("wcT", [128, 4, 128], BF16)
        gcol = sbt("gcol", [128, 3, 8], F32)
        cst = sbt("cst", [128, 4], F32)
        gcol2 = sbt("gcol2", [128, 2], F32)
        gb = {}
        for n, m in GAINS:
            if n in ("g_mix", "g_ffn", "g_mem"):
                continue
            gb[n] = sbt("gb_" + n, [128, m], F32)
        invf = sbt("invf_sb", [128, 32], F32)
        pos_i = sbt("pos_i", [128, 16], I32)
        pos_f = sbt("pos_f", [128, 16], F32)
        sincos = sbt("sincos", [128, 2, 2, 4, 32], F32)
        knT = sbt("knT", [128, 8, SEQ], BF16)
        kpT = sbt("kpT", [64, SEQ], BF16)
        vtok = sbt("vtok", [128, 16, 1024], BF16)
        mkT = sbt("mkT", [128, 4, 256], BF16)
        mv = sbt("mv", [128, 2, 512], BF16)
        RA = Region(nc, st, "RA", 33 * 1024)
        qmT = sbt("qmT", [128, 4, T], BF16)
        OmT = sbt("OmT", [128, 4, T], BF16)
        hT = sbt("hT", [128, 8, T], BF16)
        xmid = sbt("xmid", [128, 4, D], F32)
        xs = [sbt(f"xs{i}", [128, D], F32) for i in range(2)]
        pT_all = sbt("pT_all", [128, 4, T], BF16)
        pT = [pT_all[:, i, :] for i in range(4)]
        NSLOT = 6
        wring = sbt("wring", [128, NSLOT, 2304], BF16)
        AR = Region(nc, st, "AR", 14 * 1024)
        stat = sbt("stat", [128, 96], F32)

        A_uT, A_vtok, A_gmT, A_cqT, A_ckvT, A_qnT, A_qpT = 0, 4096, 8192, 12288, 15360, 17408, 25600
        uT = RA.bf(A_uT, 2048).rearrange("p (c n) -> p c n", n=T)
        v_tok = RA.bf(A_vtok, 2048).rearrange("p (c n) -> p c n", n=512)
        gmT = RA.bf(A_gmT, 2048).rearrange("p (c n) -> p c n", n=T)
        cqT = RA.bf(A_cqT, 1536).rearrange("p (c n) -> p c n", n=T)
        ckvT = RA.bf(A_ckvT, 1024).rearrange("p (c n) -> p c n", n=T)
        qnT = RA.bf(A_qnT, 4096).rearrange("p (c n) -> p c n", n=T)
        qpT = RA.bf(A_qpT, 4096).rearrange("p (c n) -> p c n", n=T)
        OT = RA.bf(0, 4096).rearrange("p (c n) -> p c n", n=T)
        mergedT = qnT
        actT = RA.bf(0, 16384).rearrange("p (c n) -> p c n", n=T)

        def kA(off, nbytes):
            return RA.keys(off, nbytes)

        class WS:
            plan = []
            idx = 0
            issued = 0
            slot_of = {}
            free = []
            groups = {}

        SLOT_ELEMS = 2304

        def wslot_view(slot, kc, n):
            return wring[:, slot, 0:kc * n].rearrange("p (c n) -> p c n", n=n)

        def w_issue(j, slot):
            name, k0, kc, n0, n = WS.plan[j]
            src = dram[name + "_b"][k0 * 128:(k0 + kc) * 128, n0:n0 + n].rearrange("(c p) n -> p c n", p=128)
            dst = wslot_view(slot, kc, n)
            S.dma("sp", f"w{slot}", lambda h, dst=dst, src=src: h.dma_start(out=dst, in_=src),
                  reads=[("wb", name)], writes=[("w", slot)])

        class WView:
            def __init__(self, parts, kper, nper):
                self.parts, self.kper, self.nper = parts, kper, nper

            def __getitem__(self, idx):
                _, kc, cs = idx
                ki, kr = divmod(kc, self.kper)
                if cs.start is None:
                    assert len(self.parts[ki]) == 1
                    return self.parts[ki][0][:, kr, :]
                ni = cs.start // self.nper
                assert (cs.stop - 1) // self.nper == ni
                return self.parts[ki][ni][:, kr, cs.start - ni * self.nper:cs.stop - ni * self.nper]

        def w_try_issue():
            while WS.issued < len(WS.plan) and WS.free:
                slot = WS.free.pop(0)
                WS.slot_of[WS.issued] = slot
                w_issue(WS.issued, slot)
                WS.issued += 1

        def wrelease(group):
            if S.null:
                return
            WS.free.extend(WS.groups.pop(group, []))
            w_try_issue()

        def wget(name, k0, kc, n0, n, hold=None, nper=None, keep=False, gate=False, group=None):
            nper = nper or n
            kper = kc if kc * nper <= SLOT_ELEMS else min(kc, 4)
            assert kper * nper <= SLOT_ELEMS and kc % kper == 0 and n % nper == 0
            phys = [(name, k0 + ki * kper, kper, n0 + ni * nper, nper) for ki in range(kc // kper) for ni in range(n // nper)]
            parts = [[None] * (n // nper) for _ in range(kc // kper)]
            keys = []
            if S.null:
                for q_, ph in enumerate(phys):
                    WS.plan.append(ph)
                    parts[q_ // (n // nper)][q_ % (n // nper)] = wslot_view(0, kper, nper)
                    keys.append(("w", 0))
                return WView(parts, kper, nper), keys
            if gate:
                group = "gate"
            if group is None:
                group = "default"
            if (group in ("default", "gate")) and not keep:
                WS.free.extend(WS.groups.pop(group, []))
            for q_, ph in enumerate(phys):
                i = WS.idx
                assert WS.plan[i] == ph
                WS.idx += 1
                if WS.issued <= i:
                    assert WS.free, "weight ring too small for the live set"
                    w_try_issue()
                slot = WS.slot_of.pop(i)
                WS.groups.setdefault(group, []).append(slot)
                parts[q_ // (n // nper)][q_ % (n // nper)] = wslot_view(slot, kper, nper)
                keys.append(("w", slot))
            w_try_issue()
            return WView(parts, kper, nper), keys

        def mmg(out_ap, pairs, wkey, reads, start=True, stop=True):
            n = len(pairs)
            reads = list(reads)
            for i, (l, r) in enumerate(pairs):
                s_, e_ = (start and i == 0), (stop and i == n - 1)
                fn = lambda h, l=l, r=r, s_=s_, e_=e_: h.matmul(out_ap, lhsT=l, rhs=r, start=s_, stop=e_)
                if i < n - 1:
                    S.op_noinc("pe", fn, reads=reads, wdeps=[wkey] if i == 0 else ())
                else:
                    S.op("pe", fn, reads=reads, writes=[wkey])

        def transposes(bank, items, reads):
            n = len(items)
            for i, (src, m, c0) in enumerate(items):
                fn = lambda h, src=src, m=m, c0=c0: h.transpose(pbb[bank][0:m, c0:c0 + 128], src, ident[:, :])
                if i < n - 1:
                    S.op_noinc("pe", fn, reads=list(reads) + ["ident"], wdeps=[pk[bank]] if i == 0 else ())
                else:
                    S.op("pe", fn, reads=list(reads) + ["ident"], writes=[pk[bank]])

        def interleave(*gens):
            live = list(gens)
            while live:
                for g_ in list(live):
                    try:
                        next(g_)
                    except StopIteration:
                        live.remove(g_)

        def rstd_from_ssq(ssq_ap, out_ap, n, inv_n, keys_in, keys_out, tcol=56):
            tmpk = ("stat", tcol // 8)
            tmp = stat[:, tcol:tcol + n]
            S.op("act", lambda h: h.activation(out=tmp, in_=ssq_ap, func=AF.Sqrt, scale=inv_n, bias=cst[:, 0:1]),
                 reads=list(keys_in) + ["cst"], writes=[tmpk])
            S.op("dve", lambda h: h.reciprocal(out=out_ap, in_=tmp), reads=[tmpk], writes=list(keys_out))

        S.op("pool", lambda h: h.memset(identf[:, :], 1.0), writes=["identf"])
        S.op("pool", lambda h: h.affine_select(out=identf[:, :], in_=identf[:, :], pattern=[[-1, 128]],
                                                compare_op=ALU.is_equal, fill=0.0, base=0, channel_multiplier=1),
             reads=["identf"], writes=["identf"])
        S.op("dve", lambda h: h.tensor_copy(out=ident[:, :], in_=identf[:, :]), reads=["identf"], writes=["ident"])
        S.op("dve", lambda h: h.memset(ones_b[:, :], 1.0), writes=["ones_b"])
        S.op("dve", lambda h: h.memset(cst[:, 0:1], EPS), writes=["cst"])
        S.op("dve", lambda h: h.memset(cst[:, 1:2], -math.pi), reads=["cst"], writes=["cst"])
        trif = AR.f32(0, 128)
        S.op("pool", lambda h: h.memset(trif, 0.0), writes=AR.keys(0, 512))
        S.op("pool", lambda h: h.affine_select(out=trif, in_=trif, pattern=[[1, 128]], compare_op=ALU.is_ge,
                                                fill=-30000.0, base=0, channel_multiplier=-1),
             reads=AR.keys(0, 512), writes=AR.keys(0, 512))
        S.op("dve", lambda h: h.tensor_copy(out=negtri[:, :], in_=trif), reads=AR.keys(0, 512), writes=["negtri"])
        for n, m in GAINS:
            if n in gb:
                S.dma("sp", "gl", lambda h, n=n: h.dma_start(out=gb[n][:, :], in_=dram[n][0, :].partition_broadcast(128)),
                      writes=[("gb", n)])
        S.dma("sp", "gl", lambda h: h.dma_start(out=invf[:, :], in_=invf_d[0, :].partition_broadcast(128)), writes=["invf"])
        for i, n in enumerate(("g_mix", "g_ffn", "g_mem")):
            S.dma("sp", "gl", lambda h, i=i, n=n: h.dma_start(out=gcol[:, i, :], in_=dram[n][0, :].rearrange("(c p) -> p c", p=128),
                                                               allow_slow_non_contiguous=True), writes=[("gcol", i)])
        for i, n in enumerate(("g_q_nope", "g_k_nope")):
            S.dma("sp", "gl", lambda h, i=i, n=n: h.dma_start(out=gcol2[:, i:i + 1], in_=dram[n][0, :].rearrange("(c p) -> p c", p=128),
                                                               allow_slow_non_contiguous=True), writes=[("gcol2", i)])
        gl_tok = ("dma", "gl", S.dma_cnt["gl"])
        for key in list(S.last_w.keys()):
            if S.last_w[key][0] == "dma" and S.last_w[key][1] == "gl":
                S.last_w[key] = gl_tok
        for g in range(4):
            wtmp = AR.f32(1024, 128)
            wtmp2 = AR.f32(2048, 128)
            S.dma("sp", "wsp", lambda h, g=g: h.dma_start(out=wtmp, in_=wsp_d[g * 128:(g + 1) * 128, :]),
                  writes=AR.keys(1024, 512))
            S.op("pe", lambda h: h.transpose(pb[0][:, 0:128], wtmp, identf[:, :]),
                 reads=AR.keys(1024, 512) + ["identf"], writes=[pk[0]])
            S.op("dve", lambda h: h.tensor_copy(out=wtmp2, in_=pb[0][:, 0:128]), reads=[pk[0]], writes=AR.keys(2048, 512))
            S.op("pool", lambda h, g=g: h.affine_select(out=wcT[:, g, :], in_=wtmp2, pattern=[[1, 128]], compare_op=ALU.is_ge,
                                                         fill=0.0, base=0, channel_multiplier=-1),
                 reads=AR.keys(2048, 512), writes=["wcT"])

        wdims = {n: (k, m) for n, k, m in WEIGHTS}
        for n in ("w_mem_kv", "w_in", "w_uq", "w_ukv", "w_o_gm", "w_o_mla", "w_o_mem", "w_out", "w_ff1", "w_ff2"):
            k, m = wdims[n]
            for r0 in range(0, k, 128):
                S.dma("pool", "cast_" + n, lambda h, n=n, r0=r0: h.dma_start(out=dram[n + "_b"][r0:r0 + 128, :],
                                                                             in_=dram[n][r0:r0 + 128, :]))
            S.last_w[("wb", n)] = ("dma", "cast_" + n, S.dma_cnt["cast_" + n])
            S.readers[("wb", n)] = []

        GM, GF, GME = 0, 1, 2

        def norm_chain(src_ap, src_keys):
            junk = AR.bf(0, 1024)
            jk = AR.keys(0, 2048)
            xn = AR.bf(2048, 1024)
            xk = AR.keys(2048, 2048)
            S.op("act", lambda h: h.activation(out=junk, in_=src_ap, func=AF.Square, accum_out=stat[:, 0:1]),
                 reads=src_keys, writes=jk + [("stat", 0)])
            rstd_from_ssq(stat[:, 0:1], stat[:, 1:2], 1, 1.0 / D, [("stat", 0)], [("stat", 0)])
            S.op("dve", lambda h: h.tensor_scalar(out=xn, in0=src_ap, scalar1=stat[:, 1:2], scalar2=None, op0=ALU.mult),
                 reads=list(src_keys) + [("stat", 0)], writes=xk)

        def norm_trans(gidx, dstT, dst_keys_fn, col0):
            xn = AR.bf(2048, 1024)
            xk = AR.keys(2048, 2048)
            b = gp()
            transposes(b, [(xn[:, kc * 128:(kc + 1) * 128], 128, kc * 128) for kc in range(8)], xk)
            gbc = gcol[:, gidx, :].unsqueeze(2).to_broadcast([128, 8, 128])
            S.op("dve", lambda h: h.tensor_tensor(out=dstT[:, :, col0:col0 + 128],
                                                  in0=pbb[b][:, 0:1024].rearrange("p (c n) -> p c n", n=128),
                                                  in1=gbc, op=ALU.mult),
                 reads=[pk[b], ("gcol", gidx)], writes=dst_keys_fn())

        def norm_to_T(src_ap, src_keys, gidx, dstT, dst_keys_fn, col0, nsub_cols=128):
            norm_chain(src_ap, src_keys)
            norm_trans(gidx, dstT, dst_keys_fn, col0)

        hT_keys = [("hT", 0)]

        def seq_prologue(b):
            S.dma("sp", "pos", lambda h: h.dma_start(out=pos_i[:, :], in_=pos_d[b, :].rearrange("(j p) -> p j", p=128),
                                                       allow_slow_non_contiguous=True), writes=["pos_i"])
            S.op("dve", lambda h: h.tensor_copy(out=pos_f[:, :], in_=pos_i[:, :]), reads=["pos_i"], writes=["pos_f"])
            for s in range(2):
                xsl = xs[s]
                S.dma("sp", f"xs{s}", lambda h, s=s, xsl=xsl: h.dma_start(out=xsl[:, :], in_=mem_d[b, s * 128:(s + 1) * 128, :]),
                      writes=[("xs", s)])
                norm_to_T(xsl[:, :], [("xs", s)], GME, hT, lambda: hT_keys, s * 128)
            wk, wkk = wget("w_mem_kv", 0, 8, 0, 512)
            for s in range(2):
                bk = gp()
                mmg(pb[bk][:, :], [(hT[:, kc, s * 128:(s + 1) * 128], wk[:, kc, :]) for kc in range(8)], pk[bk], hT_keys + list(wkk))
                junk = AR.bf(0, 512)
                for hh in range(4):
                    S.op("act", lambda h, hh=hh: h.activation(out=junk[:, 0:128], in_=pb[bk][:, hh * 128:(hh + 1) * 128],
                                                              func=AF.Square, accum_out=stat[:, 8 + hh:9 + hh]),
                         reads=[pk[bk]], writes=AR.keys(0, 1024) + [("stat", 1)])
                rstd_from_ssq(stat[:, 8:12], stat[:, 12:16], 4, 1.0 / 128, [("stat", 1)], [("stat", 1)])
                kn = AR.bf(4096, 512)
                for hh in range(4):
                    S.op("dve", lambda h, hh=hh: h.scalar_tensor_tensor(out=kn[:, hh * 128:(hh + 1) * 128],
                                                                        in0=pb[bk][:, hh * 128:(hh + 1) * 128],
                                                                        scalar=stat[:, 12 + hh:13 + hh], in1=gb["g_mk"][:, :],
                                                                        op0=ALU.mult, op1=ALU.mult),
                         reads=[pk[bk], ("stat", 1), ("gb", "g_mk")], writes=AR.keys(4096, 1024))
                bt = gp()
                transposes(bt, [(kn[:, hh * 128:(hh + 1) * 128], 128, hh * 128) for hh in range(4)], AR.keys(4096, 1024))
                S.op("act", lambda h, s=s, bt=bt: h.activation(out=mkT[:, :, s * 128:(s + 1) * 128],
                                                              in_=pbb[bt][:, 0:512].rearrange("p (c n) -> p c n", n=128),
                                                              func=AF.Copy),
                     reads=[pk[bt]], writes=["mkT"])
            wv, wvk = wget("w_mem_kv", 0, 8, 512, 512)
            for s in range(2):
                bk = gp()
                mmg(pb[bk][:, :], [(hT[:, kc, s * 128:(s + 1) * 128], wv[:, kc, :]) for kc in range(8)], pk[bk], hT_keys + list(wvk))
                S.op("act", lambda h, s=s, bk=bk: h.activation(out=mv[:, s, :], in_=pb[bk][:, :], func=AF.Copy),
                     reads=[pk[bk]], writes=["mv"])

        def tile_body(b, t, n1_done=False):
            t0 = t * T
            gp_n[0] = 8
            S.mark(("tile", b, t))
            if t == 0:
                rope_tables(t)
            S.mark(("n1", b, t))
            for s in range(4):
                if n1_done:
                    break
                xsl = xs[s % 2]
                S.dma("sp", f"xs{s % 2}", lambda h, s=s, xsl=xsl: h.dma_start(out=xsl[:, :], in_=x_d[b, t0 + s * 128:t0 + (s + 1) * 128, :]),
                      writes=[("xs", s % 2)])
                norm_to_T(xsl[:, :], [("xs", s % 2)], GM, hT, lambda: hT_keys, s * 128)

            S.mark(("zu", b, t))
            xmid_b = xmid.bitcast(BF16)
            xs_b = [x_.bitcast(BF16) for x_ in xs]

            def gate_view(g):
                if g < 16:
                    return xmid_b[:, g // 4, (g % 4) * 512:(g % 4 + 1) * 512], ("xmid", g // 4)
                g2 = g - 16
                return xs_b[g2 // 4][:, (g2 % 4) * 512:(g2 % 4 + 1) * 512], ("xs", g2 // 4)

            def gate_gen():
                for mb in range(2):
                    for br in range(3):
                        wg, wgk = wget("w_in", 0, 8, 2240 + br * 1024 + mb * 512, 512, gate=True)
                        for m in range(4):
                            g = (mb * 3 + br) * 4 + m
                            bg = gp()
                            mmg(pb[bg][:, :], [(wg[:, kc, m * 128:(m + 1) * 128], hT[:, kc, :]) for kc in range(8)], pk[bg], hT_keys + list(wgk))
                            gv, gk = gate_view(g)
                            S.op("act", lambda h, bg=bg, gv=gv: h.activation(out=gv, in_=pb[bg][:, :], func=AF.Identity), reads=[pk[bg]], writes=[gk])
                            yield g

            gg = gate_gen()

            def pump(n):
                for _ in range(n):
                    next(gg, None)

            w, wkey = wget("w_in", 0, 8, 0, 512)
            for m in range(4):
                bk = gp()
                mmg(pb[bk][:, :], [(w[:, kc, m * 128:(m + 1) * 128], hT[:, kc, :]) for kc in range(8)], pk[bk], hT_keys + list(wkey))
                S.op("act", lambda h, m=m, bk=bk: h.activation(out=uT[:, m, :], in_=pb[bk][:, :], func=AF.Gelu_apprx_tanh),
                     reads=[pk[bk]], writes=kA(A_uT + m * 1024, 1024))
            S.mark(("zv", b, t))
            w, wkey = wget("w_in", 0, 8, 512, 512)
            for s in range(4):
                bk = gp()
                mmg(pb[bk][:, :], [(hT[:, kc, s * 128:(s + 1) * 128], w[:, kc, :]) for kc in range(8)], pk[bk], hT_keys + list(wkey))
                vg = AR.f32(s * 2048, 512)
                vgk = AR.keys(s * 2048, 2048)
                S.op("act", lambda h, bk=bk, vg=vg: h.activation(out=vg, in_=pb[bk][:, :], func=AF.Gelu_apprx_tanh), reads=[pk[bk]], writes=vgk)
                S.op("dve", lambda h, s=s, vg=vg: h.bn_stats(out=stat[:, 64 + 6 * s:70 + 6 * s], in_=vg), reads=vgk, writes=[("stat", 8), ("stat", 9), ("stat", 10)])
            for s in range(4):
                S.op("dve", lambda h, s=s: h.bn_aggr(out=stat[:, 16 + 2 * s:18 + 2 * s], in_=stat[:, 64 + 6 * s:70 + 6 * s]),
                     reads=[("stat", 8), ("stat", 9), ("stat", 10)], writes=[("stat", 2)])
            var4 = stat[:, 16:24].rearrange("p (s a) -> p s a", a=2)[:, :, 1:2]
            S.op("dve", lambda h: h.tensor_copy(out=stat[:, 28:32].unsqueeze(2), in_=var4), reads=[("stat", 2)], writes=[("stat", 3)])
            rstd_from_ssq(stat[:, 28:32], stat[:, 24:28], 4, 1.0, [("stat", 3)], [("stat", 3)])
            for s in range(4):
                vg = AR.f32(s * 2048, 512)
                vgk = AR.keys(s * 2048, 2048)
                S.op("dve", lambda h, s=s, vg=vg: h.tensor_scalar(out=vg, in0=vg, scalar1=stat[:, 16 + 2 * s:17 + 2 * s], scalar2=stat[:, 24 + s:25 + s],
                                                              op0=ALU.subtract, op1=ALU.mult),
                     reads=vgk + [("stat", 2), ("stat", 3)], writes=vgk)
                S.op("dve", lambda h, vg=vg: h.tensor_tensor(out=vg, in0=vg, in1=gb["g_gm_ln"][:, :], op=ALU.mult),
                     reads=vgk + [("gb", "g_gm_ln")], writes=vgk)
                S.op("dve", lambda h, s=s, vg=vg: h.tensor_tensor(out=v_tok[:, s, :], in0=vg, in1=gb["b_gm_ln"][:, :], op=ALU.add),
                     reads=vgk + [("gb", "b_gm_ln")], writes=kA(A_vtok + s * 1024, 1024))
            wrelease("default")
            S.mark(("c", b, t))
            w1, wkey1 = wget("w_in", 0, 8, 1024, 512, group="c")
            w2, wkey2 = wget("w_in", 0, 8, 1536, 192, group="c")
            wqm, wqmk = wget("w_in", 0, 8, 1728, 512, group="qm")

            def c_sub(s):
                b1 = gp()
                mmg(pb[b1][:, :], [(hT[:, kc, s * 128:(s + 1) * 128], w1[:, kc, :]) for kc in range(8)], pk[b1], hT_keys + list(wkey1))
                b2 = gp()
                mmg(pb[b2][:, 0:192], [(hT[:, kc, s * 128:(s + 1) * 128], w2[:, kc, :]) for kc in range(8)], pk[b2], hT_keys + list(wkey2))
                yield
                csb = AR.f32(0, 704)
                ck = AR.keys(0, 2816)
                S.op("act", lambda h, b1=b1: h.activation(out=csb[:, 0:512], in_=pb[b1][:, :], func=AF.Copy), reads=[pk[b1]], writes=ck)
                S.op("act", lambda h, b2=b2: h.activation(out=csb[:, 512:704], in_=pb[b2][:, 0:192], func=AF.Copy), reads=[pk[b2]] + ck, writes=ck)
                yield
                junk = AR.bf(4096, 512)
                jk = AR.keys(4096, 1024)
                for i, (c0, c1) in enumerate(((0, 384), (384, 640), (640, 704))):
                    S.op("act", lambda h, i=i, c0=c0, c1=c1: h.activation(out=junk[:, 0:c1 - c0], in_=csb[:, c0:c1], func=AF.Square,
                                                                        accum_out=stat[:, 26 + i:27 + i]),
                         reads=ck, writes=jk + [("stat", 3)])
                yield
                S.op("dve", lambda h: h.tensor_scalar(out=stat[:, 26:27], in0=stat[:, 26:27], scalar1=1.0 / 384, scalar2=None, op0=ALU.mult),
                     reads=[("stat", 3)], writes=[("stat", 3)])
                S.op("dve", lambda h: h.tensor_scalar(out=stat[:, 27:28], in0=stat[:, 27:28], scalar1=1.0 / 256, scalar2=None, op0=ALU.mult),
                     reads=[("stat", 3)], writes=[("stat", 3)])
                S.op("dve", lambda h: h.tensor_scalar(out=stat[:, 28:29], in0=stat[:, 28:29], scalar1=1.0 / 64, scalar2=None, op0=ALU.mult),
                     reads=[("stat", 3)], writes=[("stat", 3)])
                yield
                rstd_from_ssq(stat[:, 26:29], stat[:, 29:32], 3, 1.0, [("stat", 3)], [("stat", 3)], tcol=56)
                yield
                cn = AR.bf(5120, 640)
                cnk = AR.keys(5120, 1280)
                S.op("dve", lambda h: h.scalar_tensor_tensor(out=cn[:, 0:384], in0=csb[:, 0:384], scalar=stat[:, 29:30],
                                                             in1=gb["g_cq"][:, :], op0=ALU.mult, op1=ALU.mult),
                     reads=ck + [("stat", 3), ("gb", "g_cq")], writes=cnk)
                yield
                S.op("dve", lambda h: h.scalar_tensor_tensor(out=cn[:, 384:640], in0=csb[:, 384:640], scalar=stat[:, 30:31],
                                                             in1=gb["g_ckv"][:, :], op0=ALU.mult, op1=ALU.mult),
                     reads=ck + [("stat", 3), ("gb", "g_ckv")] + cnk, writes=cnk)
                kp = AR.f32(6400, 64)
                kpk = AR.keys(6400, 1024)
                S.op("dve", lambda h: h.scalar_tensor_tensor(out=kp, in0=csb[:, 640:704], scalar=stat[:, 31:32],
                                                             in1=gb["g_k_pe"][:, :], op0=ALU.mult, op1=ALU.mult),
                     reads=ck + [("stat", 3), ("gb", "g_k_pe")], writes=kpk)
                yield
                kr = AR.bf(6400 + 768, 64)
                rope(kp.rearrange("p (h d) -> p h d", d=64), kr.rearrange("p (h d) -> p h d", d=64), 1, s, kpk, 6400 + 256, t % 2)
                yield
                bt = gp()
                transposes(bt, [(cn[:, i * 128:(i + 1) * 128], 128, i * 128) for i in range(5)] + [(kr, 64, 640)], cnk + kpk)
                yield
                S.op("act", lambda h, s=s, bt=bt: h.activation(out=cqT[:, :, s * 128:(s + 1) * 128],
                                                              in_=pbb[bt][:, 0:384].rearrange("p (c n) -> p c n", n=128), func=AF.Copy),
                     reads=[pk[bt]], writes=kA(A_cqT, 3072))
                S.op("act", lambda h, s=s, bt=bt: h.activation(out=ckvT[:, :, s * 128:(s + 1) * 128],
                                                              in_=pbb[bt][:, 384:640].rearrange("p (c n) -> p c n", n=128), func=AF.Copy),
                     reads=[pk[bt]], writes=kA(A_ckvT, 2048))
                S.op("act", lambda h, s=s, bt=bt: h.activation(out=kpT[:, t0 + s * 128:t0 + (s + 1) * 128],
                                                              in_=pbb[bt][0:64, 640:768], func=AF.Copy),
                     reads=[pk[bt]], writes=["kpT"])
                yield

            def qm_sub(s):
                bk = gp()
                mmg(pb[bk][:, :], [(hT[:, kc, s * 128:(s + 1) * 128], wqm[:, kc, :]) for kc in range(8)], pk[bk], hT_keys + list(wqmk))
                yield
                junk = AR.bf(8192, 512)
                jk = AR.keys(8192, 1024)
                for hh in range(4):
                    S.op("act", lambda h, hh=hh, bk=bk: h.activation(out=junk[:, 0:128], in_=pb[bk][:, hh * 128:(hh + 1) * 128],
                                                                     func=AF.Square, accum_out=stat[:, 8 + hh:9 + hh]),
                         reads=[pk[bk]], writes=jk + [("stat", 1)])
                    if hh % 2 == 1:
                        yield
                rstd_from_ssq(stat[:, 8:12], stat[:, 12:16], 4, 1.0 / 128, [("stat", 1)], [("stat", 1)], tcol=60)
                yield
                qn = AR.bf(9216, 512)
                qnk = AR.keys(9216, 1024)
                for hh in range(4):
                    S.op("dve", lambda h, hh=hh, bk=bk: h.scalar_tensor_tensor(out=qn[:, hh * 128:(hh + 1) * 128],
                                                                               in0=pb[bk][:, hh * 128:(hh + 1) * 128],
                                                                               scalar=stat[:, 12 + hh:13 + hh], in1=gb["g_mq"][:, :],
                                                                               op0=ALU.mult, op1=ALU.mult),
                         reads=[pk[bk], ("stat", 1), ("gb", "g_mq")], writes=qnk)
                    if hh % 2 == 1:
                        yield
                bt = gp()
                transposes(bt, [(qn[:, hh * 128:(hh + 1) * 128], 128, hh * 128) for hh in range(4)], qnk)
                yield
                S.op("act", lambda h, s=s, bt=bt: h.activation(out=qmT[:, :, s * 128:(s + 1) * 128],
                                                              in_=pbb[bt][:, 0:512].rearrange("p (c n) -> p c n", n=128), func=AF.Copy),
                     reads=[pk[bt]], writes=["qmT"])
                yield

            for s in range(4):
                interleave(c_sub(s), qm_sub(s))
            wrelease("c")
            wrelease("qm")
            S.mark(("qm", b, t))

            if DBG_STOP == "win":
                return
            S.mark(("gmlp", b, t))
            for c in range(4):
                bk = gp()
                for g in range(4):
                    mmg(pb[bk][:, g * 128:(g + 1) * 128], [(v_tok[:, c, g * 128:(g + 1) * 128], wcT[:, g, :])], pk[bk],
                        kA(A_vtok + c * 1024, 1024) + ["wcT"])
                tmp = AR.f32(0, 512)
                tk = AR.keys(0, 2048)
                S.op("dve", lambda h, bk=bk: h.tensor_tensor(out=tmp, in0=pb[bk][:, :], in1=gb["b_spatial"][:, :], op=ALU.add),
                     reads=[pk[bk], ("gb", "b_spatial")], writes=tk)
                S.op("dve", lambda h, c=c: h.tensor_tensor(out=gmT[:, :, c * 128:(c + 1) * 128],
                                                            in0=tmp.rearrange("p (g n) -> p g n", n=128),
                                                            in1=uT[:, :, c * 128:(c + 1) * 128], op=ALU.mult),
                     reads=tk + kA(A_uT, 4096), writes=kA(A_gmT, 4096))

            if DBG_STOP == "gmlp":
                return
            S.mark(("q", b, t))
            wq_all, wq_key = wget("w_uq", 0, 3, 0, 1536, nper=768, group="q")
            wkv_all, wkv_key = wget("w_ukv", 0, 2, 0, 2048, nper=1024, group="kv")
            ktok = pT_all[:, :, :].rearrange("p a n -> p (a n)").bitcast(F32)
            kk = [("pT", i) for i in range(4)]
            omflat = OmT[:, :, :].rearrange("p a n -> p (a n)")
            knb = omflat[:, 0:1024].rearrange("p (h d) -> p h d", d=128)
            kjunk = omflat[:, 1024:1152]
            knbk = ["OmT"]

            def q_sub(s):
                qtok = AR.f32(0, 1536)
                qk = AR.keys(0, 6144)
                sq = AR.f32(6144, 384)
                sqk = AR.keys(6144, 1536)
                for j in range(4):
                    bk = gp()
                    mmg(pb[bk][:, 0:384], [(cqT[:, kc, s * 128:(s + 1) * 128], wq_all[:, kc, j * 384:(j + 1) * 384]) for kc in range(3)], pk[bk],
                        kA(A_cqT, 3072) + list(wq_key))
                    S.op("act", lambda h, bk=bk: h.activation(out=sq, in_=pb[bk][:, 0:384], func=AF.Square), reads=[pk[bk]], writes=sqk)
                    S.op("dve", lambda h, j=j: h.tensor_reduce(out=stat[:, 32 + 6 * j:38 + 6 * j], in_=sq.rearrange("p (a d) -> p a d", d=64),
                                                               axis=AX.X, op=ALU.add),
                         reads=sqk, writes=[("stat", 4), ("stat", 5), ("stat", 6)])
                    S.op("dve", lambda h, j=j, bk=bk: h.tensor_copy(out=qtok[:, j * 384:(j + 1) * 384], in_=pb[bk][:, 0:384]),
                         reads=[pk[bk]], writes=qk)
                    yield
                pump(3)
                st3 = stat[:, 32:56].rearrange("p (h a) -> p h a", a=3)
                S.op("dve", lambda h: h.tensor_tensor(out=stat[:, 0:8].unsqueeze(2), in0=st3[:, :, 0:1], in1=st3[:, :, 1:2], op=ALU.add),
                     reads=[("stat", 4), ("stat", 5), ("stat", 6)], writes=[("stat", 0)])
                S.op("dve", lambda h: h.tensor_copy(out=stat[:, 8:16].unsqueeze(2), in_=st3[:, :, 2:3]),
                     reads=[("stat", 4), ("stat", 5), ("stat", 6)], writes=[("stat", 1)])
                yield
                rstd_from_ssq(stat[:, 0:8], stat[:, 16:24], 8, 1.0 / 128, [("stat", 0)], [("stat", 2)], tcol=56)
                yield
                rstd_from_ssq(stat[:, 8:16], stat[:, 24:32], 8, 1.0 / 64, [("stat", 1)], [("stat", 3)], tcol=56)
                yield
                q3 = qtok.rearrange("p (h d) -> p h d", d=192)
                qnb = AR.bf(7680, 1024).rearrange("p (h d) -> p h d", d=128)
                qnbk = AR.keys(7680, 2048)
                S.op("dve", lambda h: h.tensor_tensor(out=qnb, in0=q3[:, :, 0:128],
                                                      in1=stat[:, 16:24].unsqueeze(2).to_broadcast([128, 8, 128]), op=ALU.mult),
                     reads=qk + [("stat", 2)], writes=qnbk)
                yield
                bt = gp()
                transposes(bt, [(qnb[:, hh, :], 128, hh * 128) for hh in range(8)], qnbk)
                S.op("dve", lambda h: h.tensor_tensor(out=q3[:, :, 128:192], in0=q3[:, :, 128:192],
                                                      in1=stat[:, 24:32].unsqueeze(2).to_broadcast([128, 8, 64]), op=ALU.mult),
                     reads=qk + [("stat", 3)], writes=qk)
                yield
                S.op("act", lambda h, s=s, bt=bt: h.activation(out=qnT[:, :, s * 128:(s + 1) * 128],
                                                              in_=pbb[bt][:, 0:1024].rearrange("p (c n) -> p c n", n=128), func=AF.Identity,
                                                              scale=gcol2[:, 0:1]),
                     reads=[pk[bt], ("gcol2", 0)], writes=kA(A_qnT, 8192))
                S.op("dve", lambda h: h.tensor_tensor(out=q3[:, :, 128:192], in0=q3[:, :, 128:192],
                                                      in1=gb["g_q_pe"][:, :].unsqueeze(1).to_broadcast([128, 8, 64]), op=ALU.mult),
                     reads=qk + [("gb", "g_q_pe")], writes=qk)
                yield
                qrb = AR.bf(9728, 512).rearrange("p (h d) -> p h d", d=64)
                qrbk = AR.keys(9728, 1024)
                rope(q3[:, :, 128:192], qrb, 8, s, qk, 10752, t % 2)
                yield
                pump(3)
                bt2 = gp()
                transposes(bt2, [(qrb[:, hh, :], 64, hh * 128) for hh in range(8)], qrbk)
                yield
                S.op("act", lambda h, s=s, bt2=bt2: h.activation(out=qpT[0:64, :, s * 128:(s + 1) * 128],
                                                                in_=pbb[bt2][0:64, 0:1024].rearrange("p (c n) -> p c n", n=128), func=AF.Copy),
                     reads=[pk[bt2]], writes=kA(A_qpT, 8192))
                yield

            def kv_sub(s):
                blk = t * 4 + s
                for j in range(4):
                    bk = gp()
                    mmg(pb[bk][:, :], [(ckvT[:, kc, s * 128:(s + 1) * 128], wkv_all[:, kc, j * 512:(j + 1) * 512]) for kc in range(2)], pk[bk],
                        kA(A_ckvT, 2048) + list(wkv_key))
                    p4 = pb[bk][:, :].rearrange("p (h a d) -> p h a d", a=2, d=128)
                    S.op("act", lambda h, j=j, p4=p4, blk=blk: h.activation(
                        out=vtok[:, blk, j * 256:(j + 1) * 256].rearrange("p (h d) -> p h d", d=128), in_=p4[:, :, 1, :], func=AF.Copy),
                         reads=[pk[bk]], writes=["vtok"])
                    for hh in range(2):
                        S.op("act", lambda h, j=j, hh=hh, bk=bk: h.activation(out=kjunk, in_=pb[bk][:, hh * 256:hh * 256 + 128], func=AF.Square,
                                                                             accum_out=stat[:, 64 + 2 * j + hh:64 + 2 * j + hh + 1]),
                             reads=[pk[bk]], writes=knbk + [("stat", 8)])
                    for hh in range(2):
                        S.op("dve", lambda h, j=j, hh=hh, bk=bk: h.tensor_copy(out=ktok[:, (2 * j + hh) * 128:(2 * j + hh + 1) * 128],
                                                                              in_=pb[bk][:, hh * 256:hh * 256 + 128]),
                             reads=[pk[bk]], writes=kk)
                    yield
                rstd_from_ssq(stat[:, 64:72], stat[:, 72:80], 8, 1.0 / 128, [("stat", 8)], [("stat", 9)], tcol=80)
                yield
                k3 = ktok.rearrange("p (h d) -> p h d", d=128)
                S.op("dve", lambda h: h.tensor_tensor(out=knb, in0=k3, in1=stat[:, 72:80].unsqueeze(2).to_broadcast([128, 8, 128]), op=ALU.mult),
                     reads=kk + [("stat", 9)], writes=knbk)
                yield
                bt = gp()
                transposes(bt, [(knb[:, hh, :], 128, hh * 128) for hh in range(8)], knbk)
                yield
                S.op("act", lambda h, s=s, bt=bt: h.activation(out=knT[:, :, t0 + s * 128:t0 + (s + 1) * 128],
                                                              in_=pbb[bt][:, 0:1024].rearrange("p (c n) -> p c n", n=128), func=AF.Identity,
                                                              scale=gcol2[:, 1:2]),
                     reads=[pk[bt], ("gcol2", 1)], writes=["knT"])
                yield

            for s in range(4):
                interleave(q_sub(s), kv_sub(s))
            wrelease("q")
            wrelease("kv")
            S.mark(("kv", b, t))

            if DBG_STOP == "kv":
                return
            pump(24)
            wrelease("gate")
            gp_n[0] = 4
            S.mark(("attn", b, t))
            items = []
            nkb = 4 * t + 4
            for hh in range(8):
                for kb in range(nkb):
                    items.append(("mla", hh, kb, nkb))
            for hh in range(4):
                for kb in range(2):
                    items.append(("mem", hh, kb, 2))
            sc_mla = 1.0 / math.sqrt(192.0)
            sc_mem = 1.0 / math.sqrt(128.0)
            pslot = [0]
            inflight = []

            def issue_S(idx):
                kind, hh, kb, n = items[idx]
                bk = gp()
                slot = pslot[0]
                pslot[0] = (slot + 1) % 4
                if kind == "mla":
                    r = kb - 4 * t
                    c0 = 128 * r if r > 0 else 0
                    pairs = [(knT[:, hh, kb * 128:(kb + 1) * 128], qnT[:, hh, c0:T])]
                    out = pb[bk][:, c0:T]
                    rd = ["knT", "kpT"] + kA(A_qnT, 8192) + kA(A_qpT, 8192)
                    if r >= 0:
                        fn0 = lambda h, l=pairs[0][0], rr=pairs[0][1], out=out: h.matmul(out, lhsT=l, rhs=rr, start=True, stop=False)
                        S.op_noinc("pe", fn0, reads=rd, wdeps=[pk[bk]])
                        fn1 = lambda h, bk=bk, c0=c0: h.matmul(pb[bk][:, c0:c0 + 128], lhsT=ident[:, :], rhs=negtri[:, :], start=False, stop=False)
                        S.op_noinc("pe", fn1, reads=["ident", "negtri"])
                        fn2 = lambda h, out=out, kb=kb, hh=hh, c0=c0: h.matmul(out, lhsT=kpT[:, kb * 128:(kb + 1) * 128], rhs=qpT[0:64, hh, c0:T],
                                                                               start=False, stop=True)
                        S.op("pe", fn2, reads=rd, writes=[pk[bk]])
                    else:
                        pairs.append((kpT[:, kb * 128:(kb + 1) * 128], qpT[0:64, hh, c0:T]))
                        mmg(out, pairs, pk[bk], rd)
                    S.op("act", lambda h, bk=bk, slot=slot, c0=c0: h.activation(out=pT[slot][:, c0:T], in_=pb[bk][:, c0:T], func=AF.Exp, scale=sc_mla),
                         reads=[pk[bk]], writes=[("pT", slot)])
                else:
                    c0 = 0
                    mmg(pb[bk][:, :], [(mkT[:, hh, kb * 128:(kb + 1) * 128], qmT[:, hh, :])], pk[bk], ["mkT", "qmT"])
                    S.op("act", lambda h, bk=bk, slot=slot: h.activation(out=pT[slot][:, :], in_=pb[bk][:, :], func=AF.Exp, scale=sc_mem),
                         reads=[pk[bk]], writes=[("pT", slot)])
                return slot, c0

            def issue_PV(idx, slot, c0):
                kind, hh, kb, n = items[idx]
                hidx = hh if kind == "mla" else 8 + hh
                bo = 4 + 2 * (hidx % 2)
                bz = bo + 1
                vsrc = vtok[:, kb, hh * 128:(hh + 1) * 128] if kind == "mla" else mv[:, kb, hh * 128:(hh + 1) * 128]
                vkey = "vtok" if kind == "mla" else "mv"
                st_, en_ = (kb == 0), (kb == n - 1)
                S.op("pe", lambda h, bo=bo, vsrc=vsrc, slot=slot, c0=c0, st_=st_, en_=en_: h.matmul(pb[bo][:, c0:T], lhsT=vsrc, rhs=pT[slot][:, c0:T], start=st_, stop=en_),
                     reads=[vkey, ("pT", slot)], writes=[pk[bo]])
                S.op("pe", lambda h, bz=bz, slot=slot, c0=c0, st_=st_, en_=en_: h.matmul(pb[bz][:, c0:T], lhsT=ones_b[:, :], rhs=pT[slot][:, c0:T], start=st_, stop=en_),
                     reads=["ones_b", ("pT", slot)], writes=[pk[bz]])
                if en_:
                    rz = AR.f32(0, 512)
                    rzk = AR.keys(0, 2048)
                    S.op("dve", lambda h, bz=bz: h.reciprocal(out=rz, in_=pb[bz][:, :]), reads=[pk[bz]], writes=rzk)
                    if kind == "mla":
                        dst, dk = OT[:, hh, :], kA(hh * 1024, 1024)
                    else:
                        dst, dk = OmT[:, hh, :], ["OmT"]
                    S.op("dve", lambda h, bo=bo, dst=dst: h.tensor_tensor(out=dst, in0=pb[bo][:, :], in1=rz, op=ALU.mult),
                         reads=[pk[bo]] + rzk, writes=dk)

            LOOK = 2
            pending = []
            for idx in range(len(items)):
                pending.append((idx,) + issue_S(idx))
                if len(pending) > LOOK:
                    i0, sl, c0 = pending.pop(0)
                    issue_PV(i0, sl, c0)
            while pending:
                i0, sl, c0 = pending.pop(0)
                issue_PV(i0, sl, c0)

            if DBG_STOP == "attn":
                return
            S.mark(("merge", b, t))
            branches = [("w_o_gm", gmT, 4, kA(A_gmT, 4096)), ("w_o_mla", OT, 8, kA(0, 8192)), ("w_o_mem", OmT, 4, ["OmT"])]
            accm = AR.f32(0, 2048).rearrange("p (m n) -> p m n", n=T)
            for mb in range(2):
                for br, (wn, src, nk, skeys) in enumerate(branches):
                    wo, wok = wget(wn, 0, nk, mb * 512, 512)
                    for m in range(4):
                        gt, gtk1 = gate_view((mb * 3 + br) * 4 + m)
                        gtk = [gtk1]
                        S.op("act", lambda h, gt=gt: h.activation(out=gt, in_=gt, func=AF.Sigmoid), reads=gtk, writes=gtk)
                        by = gp()
                        mmg(pb[by][:, :], [(wo[:, k, m * 128:(m + 1) * 128], src[:, k, :]) for k in range(nk)], pk[by], list(skeys) + list(wok))
                        ak = AR.keys(m * 2048, 2048)
                        if br == 0:
                            S.op("dve", lambda h, by=by, gt=gt, m=m: h.tensor_tensor(out=accm[:, m, :], in0=pb[by][:, :], in1=gt, op=ALU.mult),
                                 reads=[pk[by]] + gtk, writes=ak)
                        else:
                            tmp = AR.f32(10240 + (m % 2) * 2048, 512)
                            tk = AR.keys(10240 + (m % 2) * 2048, 2048)
                            S.op("dve", lambda h, by=by, gt=gt, tmp=tmp: h.tensor_tensor(out=tmp, in0=pb[by][:, :], in1=gt, op=ALU.mult),
                                 reads=[pk[by]] + gtk, writes=tk)
                            if br == 1:
                                S.op("pool", lambda h, m=m, tmp=tmp: h.tensor_tensor(out=accm[:, m, :], in0=accm[:, m, :], in1=tmp, op=ALU.add),
                                     reads=ak + tk, writes=ak)
                            else:
                                S.op("pool", lambda h, m=m, mb=mb, tmp=tmp: h.tensor_tensor(out=mergedT[:, mb * 4 + m, :], in0=accm[:, m, :], in1=tmp, op=ALU.add),
                                     reads=ak + tk, writes=kA(A_qnT + (mb * 4 + m) * 1024, 1024))

            if DBG_STOP == "merge":
                return
            S.mark(("wout", b, t))
            mkeys = kA(A_qnT, 8192)
            wouts = [wget("w_out", 0, 8, c * 512, 512, group="wout") for c in range(2)]
            for s in range(4):
                for c in range(2):
                    w, wkey = wouts[c]
                    xi = (s * 2 + c) % 2
                    xsl = xs[xi]
                    S.dma("sp", f"xs{xi}", lambda h, s=s, c=c, xsl=xsl: h.dma_start(out=xsl[:, 0:512],
                                                                                   in_=x_d[b, t0 + s * 128:t0 + (s + 1) * 128, c * 512:(c + 1) * 512]),
                          writes=[("xs", xi)])
                    bk = gp()
                    mmg(pb[bk][:, :], [(mergedT[:, k, s * 128:(s + 1) * 128], w[:, k, :]) for k in range(8)], pk[bk], mkeys + list(wkey))
                    S.op("dve", lambda h, bk=bk, s=s, c=c, xsl=xsl: h.tensor_tensor(out=xmid[:, s, c * 512:(c + 1) * 512], in0=pb[bk][:, :],
                                                                                  in1=xsl[:, 0:512], op=ALU.add),
                         reads=[pk[bk], ("xs", xi)], writes=[("xmid", s)])
                if s >= 1:
                    norm_trans(GF, hT, lambda: hT_keys, (s - 1) * 128)
                norm_chain(xmid[:, s, :], [("xmid", s)])
            S.mark(("n2", b, t))
            norm_trans(GF, hT, lambda: hT_keys, 3 * 128)
            wrelease("wout")
            S.mark(("ffn1", b, t))
            if t + 1 < ntile:
                rope_tables(t + 1)
            for jb in range(8):
                w, wkey = wget("w_ff1", 0, 8, jb * 512, 512)
                for m in range(4):
                    bk = gp()
                    mmg(pb[bk][:, :], [(w[:, kc, m * 128:(m + 1) * 128], hT[:, kc, :]) for kc in range(8)], pk[bk], hT_keys + list(wkey))
                    rt = AR.f32((m % 2) * 2048, 512)
                    rk = AR.keys((m % 2) * 2048, 2048)
                    S.op("act", lambda h, bk=bk, rt=rt: h.activation(out=rt, in_=pb[bk][:, :], func=AF.Relu), reads=[pk[bk]], writes=rk)
                    eng = "pool" if m % 2 == 0 else "dve"
                    S.op(eng, lambda h, rt=rt, jb=jb, m=m: h.tensor_tensor(out=actT[:, jb * 4 + m, :], in0=rt, in1=rt, op=ALU.mult),
                         reads=rk, writes=kA((jb * 4 + m) * 1024, 1024))
            S.mark(("ffn2", b, t))
            akeys = kA(0, 32768)
            for c in range(2):
                for kg in range(8):
                    w, wkey = wget("w_ff2", kg * 4, 4, c * 512, 512)
                    for s in range(4):
                        for k in range(4):
                            st_ = (kg == 0 and k == 0)
                            en_ = (kg == 7 and k == 3)
                            fn = lambda h, s=s, k=k, kg=kg, w=w, st_=st_, en_=en_: h.matmul(pb[4 + s][:, :], lhsT=actT[:, kg * 4 + k, s * 128:(s + 1) * 128],
                                                                                         rhs=w[:, k, :], start=st_, stop=en_)
                            if k < 3:
                                S.op_noinc("pe", fn, reads=akeys + list(wkey), wdeps=[pk[4 + s]] if st_ else ())
                            else:
                                S.op("pe", fn, reads=akeys + list(wkey), writes=[pk[4 + s]])
                    q_ = c * 8 + kg
                    if t + 1 < ntile and q_ in (1, 3, 5, 7, 9):
                        s2 = (q_ - 1) // 2
                        if s2 >= 1:
                            norm_trans(GM, hT, lambda: hT_keys, (s2 - 1) * 128)
                        if s2 < 4:
                            xsl = xs[s2 % 2]
                            S.dma("sp", f"xs{s2 % 2}", lambda h, s2=s2, xsl=xsl: h.dma_start(out=xsl[:, :], in_=x_d[b, t0 + T + s2 * 128:t0 + T + (s2 + 1) * 128, :]),
                                  writes=[("xs", s2 % 2)])
                            norm_chain(xsl[:, :], [("xs", s2 % 2)])
                for s in range(4):
                    S.op("dve", lambda h, s=s, c=c: h.tensor_tensor(out=xmid[:, s, c * 512:(c + 1) * 512], in0=pb[4 + s][:, :],
                                                                    in1=xmid[:, s, c * 512:(c + 1) * 512], op=ALU.add),
                         reads=[pk[4 + s], ("xmid", s)], writes=[("xmid", s)])
            for s in range(4):
                S.dma("sp", f"out{s}", lambda h, s=s: h.dma_start(out=out_d[b, t0 + s * 128:t0 + (s + 1) * 128, :], in_=xmid[:, s, :]),
                      reads=[("xmid", s)])

        def rope_tables(t):
            par = t % 2
            ak = AR.keys(8192, 4096)
            ang = AR.f32(8192, 128)
            yk = AR.f32(8192 + 512, 128)
            ki = AR.f32(8192 + 1024, 128).bitcast(I32)
            kf = AR.f32(8192 + 1536, 128)
            r1 = AR.f32(8192 + 2048, 128)
            m1 = AR.f32(8192 + 2560, 128)
            a2 = AR.f32(8192 + 3072, 128)
            S.op("dve", lambda h: h.tensor_tensor(out=ang.rearrange("p (s f) -> p s f", f=32),
                                                  in0=invf[:, :].unsqueeze(1).to_broadcast([128, 4, 32]),
                                                  in1=pos_f[:, t * 4:(t + 1) * 4].unsqueeze(2).to_broadcast([128, 4, 32]), op=ALU.mult),
                 reads=["invf", "pos_f"], writes=ak)
            for which, shift in ((0, 0.0), (1, 0.5 * math.pi)):
                src = ang
                if which == 1:
                    S.op("dve", lambda h: h.tensor_scalar(out=a2, in0=ang, scalar1=shift, scalar2=None, op0=ALU.add), reads=ak, writes=ak)
                    src = a2
                S.op("dve", lambda h, src=src: h.tensor_scalar(out=yk, in0=src, scalar1=1.0 / (2 * math.pi), scalar2=None, op0=ALU.mult),
                     reads=ak, writes=ak)
                S.op("dve", lambda h: h.tensor_copy(out=ki, in_=yk), reads=ak, writes=ak)
                S.op("dve", lambda h: h.tensor_copy(out=kf, in_=ki), reads=ak, writes=ak)
                S.op("dve", lambda h, src=src: h.scalar_tensor_tensor(out=r1, in0=kf, scalar=-2 * math.pi, in1=src, op0=ALU.mult, op1=ALU.add),
                     reads=ak, writes=ak)
                S.op("dve", lambda h: h.tensor_scalar(out=m1, in0=r1, scalar1=math.pi, scalar2=-2 * math.pi, op0=ALU.is_gt, op1=ALU.mult),
                     reads=ak, writes=ak)
                S.op("dve", lambda h: h.tensor_tensor(out=r1, in0=r1, in1=m1, op=ALU.add), reads=ak, writes=ak)
                S.op("act", lambda h, which=which: h.activation(out=sincos[:, par, which, :, :], in_=r1.rearrange("p (s f) -> p s f", f=32), func=AF.Sin),
                     reads=ak, writes=[("sincos", par)])

        def rope(x3, out3, nh, s, xkeys, scratch_off, par):
            sn = sincos[:, par, 0, s, :].unsqueeze(1).to_broadcast([128, nh, 32])
            cs = sincos[:, par, 1, s, :].unsqueeze(1).to_broadcast([128, nh, 32])
            n = nh * 32
            t1 = AR.f32(scratch_off, n).rearrange("p (h d) -> p h d", d=32)
            t2 = AR.f32(scratch_off + 4 * n, n).rearrange("p (h d) -> p h d", d=32)
            tk = AR.keys(scratch_off, 8 * n)
            x1, x2 = x3[:, :, 0:32], x3[:, :, 32:64]
            S.op("dve", lambda h: h.tensor_tensor(out=t1, in0=x1, in1=cs, op=ALU.mult), reads=list(xkeys) + [("sincos", par)], writes=tk)
            S.op("dve", lambda h: h.tensor_tensor(out=t2, in0=x2, in1=sn, op=ALU.mult), reads=list(xkeys) + [("sincos", par)], writes=tk)
            S.op("dve", lambda h: h.tensor_tensor(out=out3[:, :, 0:32], in0=t1, in1=t2, op=ALU.subtract), reads=tk, writes=ROPE_OUT_KEYS[0])
            S.op("dve", lambda h: h.tensor_tensor(out=t1, in0=x2, in1=cs, op=ALU.mult), reads=list(xkeys) + [("sincos", par)] + tk, writes=tk)
            S.op("dve", lambda h: h.tensor_tensor(out=t2, in0=x1, in1=sn, op=ALU.mult), reads=list(xkeys) + [("sincos", par)] + tk, writes=tk)
            S.op("dve", lambda h: h.tensor_tensor(out=out3[:, :, 32:64], in0=t1, in1=t2, op=ALU.add), reads=tk, writes=ROPE_OUT_KEYS[0])

        ROPE_OUT_KEYS = [AR.keys(6400, 14 * 1024 - 6400)]

        def program():
            WS.idx = 0
            WS.issued = 0
            WS.slot_of = {}
            WS.free = list(range(NSLOT))
            WS.groups = {}
            gp_state[0] = 0
            for b in range(nseq):
                if DBG_STOP == "setup":
                    return
                seq_prologue(b)
                if DBG_STOP == "prologue":
                    return
                for t in range(ntile):
                    tile_body(b, t, n1_done=(t > 0))

        S.null = True
        program()
        S.null = False
        program()
        for s_ in range(4):
            if f"out{s_}" in S.dma_cnt:
                S._wait("sp", ("dma", f"out{s_}", S.dma_cnt[f"out{s_}"]))
        S.emit()
        S.mark(("end", 0, 0))
        build.info = dict(sbuf_remaining=nc.sbuf_bytes_remaining, cnt=dict(S.cnt), nblocks=len(WS.plan), marks=S.marks, npe=S.npe)
    return nc


_INVF = (10000.0 ** (-np.arange(0, 64, 2, dtype=np.float32) / 64.0)).astype(np.float32).reshape(1, 32)


def make_in_maps(inputs, ncores=NCORES):
    maps = []
    for c in range(ncores):
        sl = slice(c * SEQ_PER_CORE, (c + 1) * SEQ_PER_CORE)
        m = {"x": np.ascontiguousarray(inputs["x"][sl]),
             "mem": np.ascontiguousarray(inputs["mem"][sl]),
             "positions": np.ascontiguousarray(inputs["positions"][sl]).astype(np.int32),
             "invf": _INVF,
             "w_spatial": np.ascontiguousarray(inputs["w_spatial"][0].reshape(512, 128))}
        for n, k, mm_ in WEIGHTS:
            m[n] = np.ascontiguousarray(inputs[n][0])
        for n, mm_ in GAINS:
            m[n] = np.ascontiguousarray(inputs[n][0].reshape(1, mm_))
        maps.append(m)
    return maps


def kernel(**inputs):
    inputs = {k: np.asarray(v) for k, v in inputs.items()}
    nc = build()
    maps = make_in_maps(inputs)
    res = run_bass_kernel_spmd(nc, maps, core_ids=list(range(NCORES)))
    outs = [np.asarray(r["out"]) for r in res.results]
    return np.concatenate(outs, axis=0).astype(np.float32)
```

```python
import math
import numpy as np
from contextlib import ExitStack
import concourse.bass as bass
import concourse.mybir as mybir
from concourse.bass_utils import run_bass_kernel_spmd

F32 = mybir.dt.float32
BF16 = mybir.dt.bfloat16
I32 = mybir.dt.int32
AF = mybir.ActivationFunctionType
ALU = mybir.AluOpType
AX = mybir.AxisListType

ENGS = ("pe", "act", "dve", "pool", "sp")
GEN = 12000
EPS = 1e-6
NCORES = 8
DBG_STOP = None
DBG_KV = 9
SEQ_PER_CORE = 4
SEQ = 2048
D = 1024
T = 512
W_IN_COLS = 5312


class _Rec:
    def __init__(self):
        self.call = None

    def __getattr__(self, name):
        def f(*a, **k):
            self.call = (name, a, k)
            return self
        return f


def _freeze(fn):
    r = _Rec()
    fn(r)
    name, a, k = r.call
    return lambda h: getattr(h, name)(*a, **k)


class Sched:
    def __init__(self, nc, stack, n_gen=10):
        self.nc = nc
        self.ops = {e: [] for e in ENGS}
        self.cnt = {e: 0 for e in ENGS}
        self.sems = {e: [stack.enter_context(nc.semaphore(f"s_{e}_{g}")) for g in range(n_gen)]
                     for e in ("pe", "act", "dve", "pool")}
        self.n_gen = n_gen
        self.last_w = {}
        self.readers = {}
        self.seen = {e: {} for e in ENGS}
        self.dma_sems = {}
        self.dma_cnt = {}
        self.stack = stack
        self.null = False
        self.npe = 0
        self.marks = []

    def mark(self, name):
        if not self.null:
            self.marks.append((name, self.npe, self.cnt["act"], self.cnt["dve"]))

    def _wait(self, eng, tok):
        if tok is None:
            return
        if tok[0] == "eng":
            _, pe, seq = tok
            key = ("eng", pe)
            if self.seen[eng].get(key, 0) >= seq:
                return
            self.seen[eng][key] = seq
            g = (seq - 1) // GEN
            v = seq - g * GEN
            sem = self.sems[pe][g]
            self.ops[eng].append(lambda h, sem=sem, v=v: h.wait_ge(sem, v))
        else:
            _, name, val = tok
            key = ("dma", name)
            if self.seen[eng].get(key, 0) >= val:
                return
            self.seen[eng][key] = val
            sem = self.dma_sems[name]
            self.ops[eng].append(lambda h, sem=sem, v=val: h.wait_ge(sem, v))

    def _deps(self, eng, reads, writes):
        toks = []
        for r in reads:
            if isinstance(r, tuple) and r[0] == "ps":
                for t in self.readers.get(r, ()):
                    if not (t[0] == "eng" and t[1] == eng):
                        toks.append(t)
        for r in reads:
            t = self.last_w.get(r)
            if t is not None:
                toks.append(t)
        for w in writes:
            t = self.last_w.get(w)
            if t is not None:
                toks.append(t)
            toks.extend(self.readers.get(w, ()))
        for t in toks:
            if t[0] == "eng" and t[1] == eng and eng == "pe":
                continue
            self._wait(eng, t)

    def _commit(self, tok, reads, writes):
        for r in reads:
            self.readers.setdefault(r, []).append(tok)
        for w in writes:
            self.last_w[w] = tok
            self.readers[w] = []

    def op(self, eng, fn, reads=(), writes=()):
        if self.null:
            return None
        fn = _freeze(fn)
        if eng == "pe":
            self.npe += 1
        self._deps(eng, reads, writes)
        self.cnt[eng] += 1
        seq = self.cnt[eng]
        g = (seq - 1) // GEN
        assert g < self.n_gen, "semaphore generations exhausted"
        sem = self.sems[eng][g]
        self.ops[eng].append(lambda h, fn=fn, sem=sem: fn(h).then_inc(sem, 1))
        tok = ("eng", eng, seq)
        self._commit(tok, reads, writes)
        return tok

    def op_noinc(self, eng, fn, reads=(), wdeps=()):
        if self.null:
            return
        fn = _freeze(fn)
        if eng == "pe":
            self.npe += 1
        self._deps(eng, reads, wdeps)
        self.ops[eng].append(lambda h, fn=fn: fn(h))

    def dma(self, queue, name, fn, reads=(), writes=()):
        if self.null:
            return None
        if name not in self.dma_sems:
            self.dma_sems[name] = self.stack.enter_context(self.nc.semaphore(f"d_{name}"))
            self.dma_cnt[name] = 0
        fn = _freeze(fn)
        self._deps(queue, reads, writes)
        sem = self.dma_sems[name]
        self.dma_cnt[name] += 16
        self.ops[queue].append(lambda h, fn=fn, sem=sem: fn(h).then_inc(sem, 16))
        tok = ("dma", name, self.dma_cnt[name])
        self._commit(tok, reads, writes)
        return tok

    def emit(self):
        nc = self.nc
        with nc.Block() as block:
            @block.tensor
            def _(h):
                for f in self.ops["pe"]:
                    f(h)

            @block.scalar
            def _(h):
                for f in self.ops["act"]:
                    f(h)

            @block.vector
            def _(h):
                for f in self.ops["dve"]:
                    f(h)

            @block.gpsimd
            def _(h):
                for f in self.ops["pool"]:
                    f(h)

            @block.sync
            def _(h):
                for f in self.ops["sp"]:
                    f(h)


class Region:
    def __init__(self, nc, stack, name, nbytes):
        self.name = name
        self.t = stack.enter_context(nc.sbuf_tensor(name, [128, nbytes // 4], F32))
        self.tb = self.t.bitcast(BF16)
        self.nbytes = nbytes

    def f32(self, off, n):
        assert off % 4 == 0 and off + 4 * n <= self.nbytes
        return self.t[:, off // 4: off // 4 + n]

    def bf(self, off, n):
        assert off % 2 == 0 and off + 2 * n <= self.nbytes
        return self.tb[:, off // 2: off // 2 + n]

    def keys(self, off, nbytes):
        return [(self.name, p) for p in range(off // 1024, (off + nbytes - 1) // 1024 + 1)]


WEIGHTS = [("w_in", 1024, 5312), ("w_uq", 384, 1536), ("w_ukv", 256, 2048), ("w_mem_kv", 1024, 1024),
           ("w_o_gm", 512, 1024), ("w_o_mla", 1024, 1024), ("w_o_mem", 512, 1024), ("w_out", 1024, 1024),
           ("w_ff1", 1024, 4096), ("w_ff2", 4096, 1024)]
GAINS = [("g_mix", 1024), ("g_cq", 384), ("g_ckv", 256), ("g_q_nope", 128), ("g_q_pe", 64), ("g_k_nope", 128),
         ("g_k_pe", 64), ("g_gm_ln", 512), ("b_gm_ln", 512), ("b_spatial", 512), ("g_mem", 1024), ("g_mq", 128),
         ("g_mk", 128), ("g_ffn", 1024)]


def build(nseq=SEQ_PER_CORE, ntile=SEQ // T):
    nc = bass.Bass("TRN2", target_bir_lowering=False)
    dram = {}
    x_d = nc.dram_tensor("x", [SEQ_PER_CORE, SEQ, D], F32, kind="ExternalInput").ap()
    mem_d = nc.dram_tensor("mem", [SEQ_PER_CORE, 256, D], F32, kind="ExternalInput").ap()
    pos_d = nc.dram_tensor("positions", [SEQ_PER_CORE, SEQ], I32, kind="ExternalInput").ap()
    invf_d = nc.dram_tensor("invf", [1, 32], F32, kind="ExternalInput").ap()
    wsp_d = nc.dram_tensor("w_spatial", [512, 128], F32, kind="ExternalInput").ap()
    for n, k, m in WEIGHTS:
        dram[n] = nc.dram_tensor(n, [k, m], F32, kind="ExternalInput").ap()
        dram[n + "_b"] = nc.dram_tensor(n + "_b", [k, m], BF16, kind="Internal").ap()
    for n, m in GAINS:
        dram[n] = nc.dram_tensor(n, [1, m], F32, kind="ExternalInput").ap()
    out_d = nc.dram_tensor("out", [SEQ_PER_CORE, SEQ, D], F32, kind="ExternalOutput").ap()

    with ExitStack() as st:
        S = Sched(nc, st)
        sbt = lambda name, shape, dt: st.enter_context(nc.sbuf_tensor(name, shape, dt))
        pb = [st.enter_context(nc.psum_tensor(f"pb{i}", [128, 512], F32)) for i in range(8)]
        pbb = [p.bitcast(BF16) for p in pb]
        pk = [("ps", i) for i in range(8)]
        gp_state = [0]

        gp_n = [8]

        def gp():
            i = gp_state[0] % gp_n[0]
            gp_state[0] = (i + 1) % gp_n[0]
            return i

        ident = sbt("ident", [128, 128], BF16)
        identf = sbt("identf", [128, 128], F32)
        ones_b = sbt("ones_b", [128, 128], BF16)
        negtri = sbt("negtri", [128, 128], BF16)
        wcT = sbt("wcT", [128, 4, 128], BF16)
        gcol = sbt("gcol", [128, 3, 8], F32)
        cst = sbt("cst", [128, 4], F32)
        gcol2 = sbt("gcol2", [128, 2], F32)
        gb = {}
        for n, m in GAINS:
            if n in ("g_mix", "g_ffn", "g_mem"):
                continue
            gb[n] = sbt("gb_" + n, [128, m], F32)
        invf = sbt("invf_sb", [128, 32], F32)
        pos_i = sbt("pos_i", [128, 16], I32)
        pos_f = sbt("pos_f", [128, 16], F32)
        sincos = sbt("sincos", [128, 2, 2, 4, 32], F32)
        knT = sbt("knT", [128, 8, SEQ], BF16)
        kpT = sbt("kpT", [128, SEQ], BF16)
        vtok = sbt("vtok", [128, 16, 1024], BF16)
        mkT = sbt("mkT", [128, 4, 256], BF16)
        mv = sbt("mv", [128, 2, 512], BF16)
        RA = Region(nc, st, "RA", 33 * 1024)
        qmT = sbt("qmT", [128, 4, T], BF16)
        OmT = sbt("OmT", [128, 4, T], BF16)
        hT = sbt("hT", [128, 8, T], BF16)
        xmid = sbt("xmid", [128, 4, D], F32)
        xs = [sbt(f"xs{i}", [128, D], F32) for i in range(2)]
        pT_all = sbt("pT_all", [128, 4, T], BF16)
        pT = [pT_all[:, i, :] for i in range(4)]
        NSLOT = 6
        wring = sbt("wring", [128, NSLOT, 2304], BF16)
        AR = Region(nc, st, "AR", 14 * 1024)
        stat = sbt("stat", [128, 96], F32)

        A_uT, A_vtok, A_gmT, A_cqT, A_ckvT, A_qnT, A_qpT = 0, 4096, 8192, 12288, 15360, 17408, 25600
        uT = RA.bf(A_uT, 2048).rearrange("p (c n) -> p c n", n=T)
        v_tok = RA.bf(A_vtok, 2048).rearrange("p (c n) -> p c n", n=512)
        gmT = RA.bf(A_gmT, 2048).rearrange("p (c n) -> p c n", n=T)
        cqT = RA.bf(A_cqT, 1536).rearrange("p (c n) -> p c n", n=T)
        ckvT = RA.bf(A_ckvT, 1024).rearrange("p (c n) -> p c n", n=T)
        qnT = RA.bf(A_qnT, 4096).rearrange("p (c n) -> p c n", n=T)
        qpT = RA.bf(A_qpT, 4096).rearrange("p (c n) -> p c n", n=T)
        OT = RA.bf(0, 4096).rearrange("p (c n) -> p c n", n=T)
        mergedT = qnT
        actT = RA.bf(0, 16384).rearrange("p (c n) -> p c n", n=T)

        def kA(off, nbytes):
            return RA.keys(off, nbytes)

        class WS:
            plan = []
            idx = 0
            issued = 0
            slot_of = {}
            free = []
            groups = {}

        SLOT_ELEMS = 2304

        def wslot_view(slot, kc, n):
            return wring[:, slot, 0:kc * n].rearrange("p (c n) -> p c n", n=n)

        def w_issue(j, slot):
            name, k0, kc, n0, n = WS.plan[j]
            src = dram[name + "_b"][k0 * 128:(k0 + kc) * 128, n0:n0 + n].rearrange("(c p) n -> p c n", p=128)
            dst = wslot_view(slot, kc, n)
            S.dma("sp", f"w{slot}", lambda h, dst=dst, src=src: h.dma_start(out=dst, in_=src),
                  reads=[("wb", name)], writes=[("w", slot)])

        class WView:
            def __init__(self, parts, kper, nper):
                self.parts, self.kper, self.nper = parts, kper, nper

            def __getitem__(self, idx):
                _, kc, cs = idx
                ki, kr = divmod(kc, self.kper)
                if cs.start is None:
                    assert len(self.parts[ki]) == 1
                    return self.parts[ki][0][:, kr, :]
                ni = cs.start // self.nper
                assert (cs.stop - 1) // self.nper == ni
                return self.parts[ki][ni][:, kr, cs.start - ni * self.nper:cs.stop - ni * self.nper]

        def w_try_issue():
            while WS.issued < len(WS.plan) and WS.free:
                slot = WS.free.pop(0)
                WS.slot_of[WS.issued] = slot
                w_issue(WS.issued, slot)
                WS.issued += 1

        def wrelease(group):
            if S.null:
                return
            WS.free.extend(WS.groups.pop(group, []))
            w_try_issue()

        def wget(name, k0, kc, n0, n, hold=None, nper=None, keep=False, gate=False, group=None):
            nper = nper or n
            kper = kc if kc * nper <= SLOT_ELEMS else min(kc, 4)
            assert kper * nper <= SLOT_ELEMS and kc % kper == 0 and n % nper == 0
            phys = [(name, k0 + ki * kper, kper, n0 + ni * nper, nper) for ki in range(kc // kper) for ni in range(n // nper)]
            parts = [[None] * (n // nper) for _ in range(kc // kper)]
            keys = []
            if S.null:
                for q_, ph in enumerate(phys):
                    WS.plan.append(ph)
                    parts[q_ // (n // nper)][q_ % (n // nper)] = wslot_view(0, kper, nper)
                    keys.append(("w", 0))
                return WView(parts, kper, nper), keys
            if gate:
                group = "gate"
            if group is None:
                group = "default"
            if (group in ("default", "gate")) and not keep:
                WS.free.extend(WS.groups.pop(group, []))
            for q_, ph in enumerate(phys):
                i = WS.idx
                assert WS.plan[i] == ph
                WS.idx += 1
                if WS.issued <= i:
                    assert WS.free, "weight ring too small for the live set"
                    w_try_issue()
                slot = WS.slot_of.pop(i)
                WS.groups.setdefault(group, []).append(slot)
                parts[q_ // (n // nper)][q_ % (n // nper)] = wslot_view(slot, kper, nper)
                keys.append(("w", slot))
            w_try_issue()
            return WView(parts, kper, nper), keys

        def mmg(out_ap, pairs, wkey, reads, start=True, stop=True):
            n = len(pairs)
            reads = list(reads)
            for i, (l, r) in enumerate(pairs):
                s_, e_ = (start and i == 0), (stop and i == n - 1)
                fn = lambda h, l=l, r=r, s_=s_, e_=e_: h.matmul(out_ap, lhsT=l, rhs=r, start=s_, stop=e_)
                if i < n - 1:
                    S.op_noinc("pe", fn, reads=reads, wdeps=[wkey] if i == 0 else ())
                else:
                    S.op("pe", fn, reads=reads, writes=[wkey])

        def transposes(bank, items, reads):
            n = len(items)
            for i, (src, m, c0) in enumerate(items):
                fn = lambda h, src=src, m=m, c0=c0: h.transpose(pbb[bank][0:m, c0:c0 + 128], src, ident[:, :])
                if i < n - 1:
                    S.op_noinc("pe", fn, reads=list(reads) + ["ident"], wdeps=[pk[bank]] if i == 0 else ())
                else:
                    S.op("pe", fn, reads=list(reads) + ["ident"], writes=[pk[bank]])

        def interleave(*gens):
            live = list(gens)
            while live:
                for g_ in list(live):
                    try:
                        next(g_)
                    except StopIteration:
                        live.remove(g_)

        def rstd_from_ssq(ssq_ap, out_ap, n, inv_n, keys_in, keys_out, tcol=56):
            tmpk = ("stat", tcol // 8)
            tmp = stat[:, tcol:tcol + n]
            S.op("act", lambda h: h.activation(out=tmp, in_=ssq_ap, func=AF.Sqrt, scale=inv_n, bias=cst[:, 0:1]),
                 reads=list(keys_in) + ["cst"], writes=[tmpk])
            S.op("dve", lambda h: h.reciprocal(out=out_ap, in_=tmp), reads=[tmpk], writes=list(keys_out))

        S.op("pool", lambda h: h.memset(identf[:, :], 1.0), writes=["identf"])
        S.op("pool", lambda h: h.affine_select(out=identf[:, :], in_=identf[:, :], pattern=[[-1, 128]],
                                                compare_op=ALU.is_equal, fill=0.0, base=0, channel_multiplier=1),
             reads=["identf"], writes=["identf"])
        S.op("dve", lambda h: h.tensor_copy(out=ident[:, :], in_=identf[:, :]), reads=["identf"], writes=["ident"])
        S.op("dve", lambda h: h.memset(ones_b[:, :], 1.0), writes=["ones_b"])
        S.op("pool", lambda h: h.memset(kpT[64:128, :], 0.0), writes=["kpT"])
        S.op("dve", lambda h: h.memset(cst[:, 0:1], EPS), writes=["cst"])
        S.op("dve", lambda h: h.memset(cst[:, 1:2], -math.pi), reads=["cst"], writes=["cst"])
        trif = AR.f32(0, 128)
        S.op("pool", lambda h: h.memset(trif, 0.0), writes=AR.keys(0, 512))
        S.op("pool", lambda h: h.affine_select(out=trif, in_=trif, pattern=[[1, 128]], compare_op=ALU.is_ge,
                                                fill=-30000.0, base=0, channel_multiplier=-1),
             reads=AR.keys(0, 512), writes=AR.keys(0, 512))
        S.op("dve", lambda h: h.tensor_copy(out=negtri[:, :], in_=trif), reads=AR.keys(0, 512), writes=["negtri"])
        for n, m in GAINS:
            if n in gb:
                S.dma("sp", "gl", lambda h, n=n: h.dma_start(out=gb[n][:, :], in_=dram[n][0, :].partition_broadcast(128)),
                      writes=[("gb", n)])
        S.dma("sp", "gl", lambda h: h.dma_start(out=invf[:, :], in_=invf_d[0, :].partition_broadcast(128)), writes=["invf"])
        for i, n in enumerate(("g_mix", "g_ffn", "g_mem")):
            S.dma("sp", "gl", lambda h, i=i, n=n: h.dma_start(out=gcol[:, i, :], in_=dram[n][0, :].rearrange("(c p) -> p c", p=128),
                                                               allow_slow_non_contiguous=True), writes=[("gcol", i)])
        for i, n in enumerate(("g_q_nope", "g_k_nope")):
            S.dma("sp", "gl", lambda h, i=i, n=n: h.dma_start(out=gcol2[:, i:i + 1], in_=dram[n][0, :].rearrange("(c p) -> p c", p=128),
                                                               allow_slow_non_contiguous=True), writes=[("gcol2", i)])
        gl_tok = ("dma", "gl", S.dma_cnt["gl"])
        for key in list(S.last_w.keys()):
            if S.last_w[key][0] == "dma" and S.last_w[key][1] == "gl":
                S.last_w[key] = gl_tok
        for g in range(4):
            wtmp = AR.f32(1024, 128)
            wtmp2 = AR.f32(2048, 128)
            S.dma("sp", "wsp", lambda h, g=g: h.dma_start(out=wtmp, in_=wsp_d[g * 128:(g + 1) * 128, :]),
                  writes=AR.keys(1024, 512))
            S.op("pe", lambda h: h.transpose(pb[0][:, 0:128], wtmp, identf[:, :]),
                 reads=AR.keys(1024, 512) + ["identf"], writes=[pk[0]])
            S.op("dve", lambda h: h.tensor_copy(out=wtmp2, in_=pb[0][:, 0:128]), reads=[pk[0]], writes=AR.keys(2048, 512))
            S.op("pool", lambda h, g=g: h.affine_select(out=wcT[:, g, :], in_=wtmp2, pattern=[[1, 128]], compare_op=ALU.is_ge,
                                                         fill=0.0, base=0, channel_multiplier=-1),
                 reads=AR.keys(2048, 512), writes=["wcT"])

        wdims = {n: (k, m) for n, k, m in WEIGHTS}
        for n in ("w_mem_kv", "w_in", "w_uq", "w_ukv", "w_o_gm", "w_o_mla", "w_o_mem", "w_out", "w_ff1", "w_ff2"):
            k, m = wdims[n]
            for r0 in range(0, k, 128):
                S.dma("pool", "cast_" + n, lambda h, n=n, r0=r0: h.dma_start(out=dram[n + "_b"][r0:r0 + 128, :],
                                                                             in_=dram[n][r0:r0 + 128, :]))
            S.last_w[("wb", n)] = ("dma", "cast_" + n, S.dma_cnt["cast_" + n])
            S.readers[("wb", n)] = []

        GM, GF, GME = 0, 1, 2

        def norm_chain(src_ap, src_keys):
            junk = AR.bf(0, 1024)
            jk = AR.keys(0, 2048)
            xn = AR.bf(2048, 1024)
            xk = AR.keys(2048, 2048)
            S.op("act", lambda h: h.activation(out=junk, in_=src_ap, func=AF.Square, accum_out=stat[:, 0:1]),
                 reads=src_keys, writes=jk + [("stat", 0)])
            rstd_from_ssq(stat[:, 0:1], stat[:, 1:2], 1, 1.0 / D, [("stat", 0)], [("stat", 0)])
            S.op("dve", lambda h: h.tensor_scalar(out=xn, in0=src_ap, scalar1=stat[:, 1:2], scalar2=None, op0=ALU.mult),
                 reads=list(src_keys) + [("stat", 0)], writes=xk)

        def norm_trans(gidx, dstT, dst_keys_fn, col0):
            xn = AR.bf(2048, 1024)
            xk = AR.keys(2048, 2048)
            b = gp()
            transposes(b, [(xn[:, kc * 128:(kc + 1) * 128], 128, kc * 128) for kc in range(8)], xk)
            gbc = gcol[:, gidx, :].unsqueeze(2).to_broadcast([128, 8, 128])
            S.op("dve", lambda h: h.tensor_tensor(out=dstT[:, :, col0:col0 + 128],
                                                  in0=pbb[b][:, 0:1024].rearrange("p (c n) -> p c n", n=128),
                                                  in1=gbc, op=ALU.mult),
                 reads=[pk[b], ("gcol", gidx)], writes=dst_keys_fn())

        def norm_to_T(src_ap, src_keys, gidx, dstT, dst_keys_fn, col0, nsub_cols=128):
            norm_chain(src_ap, src_keys)
            norm_trans(gidx, dstT, dst_keys_fn, col0)

        hT_keys = [("hT", 0)]

        def seq_prologue(b):
            S.dma("sp", "pos", lambda h: h.dma_start(out=pos_i[:, :], in_=pos_d[b, :].rearrange("(j p) -> p j", p=128),
                                                       allow_slow_non_contiguous=True), writes=["pos_i"])
            S.op("dve", lambda h: h.tensor_copy(out=pos_f[:, :], in_=pos_i[:, :]), reads=["pos_i"], writes=["pos_f"])
            for s in range(2):
                xsl = xs[s]
                S.dma("sp", f"xs{s}", lambda h, s=s, xsl=xsl: h.dma_start(out=xsl[:, :], in_=mem_d[b, s * 128:(s + 1) * 128, :]),
                      writes=[("xs", s)])
                norm_to_T(xsl[:, :], [("xs", s)], GME, hT, lambda: hT_keys, s * 128)
            wk, wkk = wget("w_mem_kv", 0, 8, 0, 512)
            for s in range(2):
                bk = gp()
                mmg(pb[bk][:, :], [(hT[:, kc, s * 128:(s + 1) * 128], wk[:, kc, :]) for kc in range(8)], pk[bk], hT_keys + list(wkk))
                junk = AR.bf(0, 512)
                for hh in range(4):
                    S.op("act", lambda h, hh=hh: h.activation(out=junk[:, 0:128], in_=pb[bk][:, hh * 128:(hh + 1) * 128],
                                                              func=AF.Square, accum_out=stat[:, 8 + hh:9 + hh]),
                         reads=[pk[bk]], writes=AR.keys(0, 1024) + [("stat", 1)])
                rstd_from_ssq(stat[:, 8:12], stat[:, 12:16], 4, 1.0 / 128, [("stat", 1)], [("stat", 1)])
                kn = AR.bf(4096, 512)
                for hh in range(4):
                    S.op("dve", lambda h, hh=hh: h.scalar_tensor_tensor(out=kn[:, hh * 128:(hh + 1) * 128],
                                                                        in0=pb[bk][:, hh * 128:(hh + 1) * 128],
                                                                        scalar=stat[:, 12 + hh:13 + hh], in1=gb["g_mk"][:, :],
                                                                        op0=ALU.mult, op1=ALU.mult),
                         reads=[pk[bk], ("stat", 1), ("gb", "g_mk")], writes=AR.keys(4096, 1024))
                bt = gp()
                transposes(bt, [(kn[:, hh * 128:(hh + 1) * 128], 128, hh * 128) for hh in range(4)], AR.keys(4096, 1024))
                S.op("act", lambda h, s=s, bt=bt: h.activation(out=mkT[:, :, s * 128:(s + 1) * 128],
                                                              in_=pbb[bt][:, 0:512].rearrange("p (c n) -> p c n", n=128),
                                                              func=AF.Copy),
                     reads=[pk[bt]], writes=["mkT"])
            wv, wvk = wget("w_mem_kv", 0, 8, 512, 512)
            for s in range(2):
                bk = gp()
                mmg(pb[bk][:, :], [(hT[:, kc, s * 128:(s + 1) * 128], wv[:, kc, :]) for kc in range(8)], pk[bk], hT_keys + list(wvk))
                S.op("act", lambda h, s=s, bk=bk: h.activation(out=mv[:, s, :], in_=pb[bk][:, :], func=AF.Copy),
                     reads=[pk[bk]], writes=["mv"])

        def tile_body(b, t, n1_done=False):
            t0 = t * T
            gp_n[0] = 8
            S.mark(("tile", b, t))
            if t == 0:
                rope_tables(t)
            S.mark(("n1", b, t))
            for s in range(4):
                if n1_done:
                    break
                xsl = xs[s % 2]
                S.dma("sp", f"xs{s % 2}", lambda h, s=s, xsl=xsl: h.dma_start(out=xsl[:, :], in_=x_d[b, t0 + s * 128:t0 + (s + 1) * 128, :]),
                      writes=[("xs", s % 2)])
                norm_to_T(xsl[:, :], [("xs", s % 2)], GM, hT, lambda: hT_keys, s * 128)

            S.mark(("zu", b, t))
            xmid_b = xmid.bitcast(BF16)
            xs_b = [x_.bitcast(BF16) for x_ in xs]

            def gate_view(g):
                if g < 16:
                    return xmid_b[:, g // 4, (g % 4) * 512:(g % 4 + 1) * 512], ("xmid", g // 4)
                g2 = g - 16
                return xs_b[g2 // 4][:, (g2 % 4) * 512:(g2 % 4 + 1) * 512], ("xs", g2 // 4)

            def gate_gen():
                for mb in range(2):
                    for br in range(3):
                        wg, wgk = wget("w_in", 0, 8, 2240 + br * 1024 + mb * 512, 512, gate=True)
                        for m in range(4):
                            g = (mb * 3 + br) * 4 + m
                            bg = gp()
                            mmg(pb[bg][:, :], [(wg[:, kc, m * 128:(m + 1) * 128], hT[:, kc, :]) for kc in range(8)], pk[bg], hT_keys + list(wgk))
                            gv, gk = gate_view(g)
                            S.op("act", lambda h, bg=bg, gv=gv: h.activation(out=gv, in_=pb[bg][:, :], func=AF.Identity), reads=[pk[bg]], writes=[gk])
                            yield g

            gg = gate_gen()

            def pump(n):
                for _ in range(n):
                    next(gg, None)

            w, wkey = wget("w_in", 0, 8, 0, 512)
            for m in range(4):
                bk = gp()
                mmg(pb[bk][:, :], [(w[:, kc, m * 128:(m + 1) * 128], hT[:, kc, :]) for kc in range(8)], pk[bk], hT_keys + list(wkey))
                S.op("act", lambda h, m=m, bk=bk: h.activation(out=uT[:, m, :], in_=pb[bk][:, :], func=AF.Gelu_apprx_tanh),
                     reads=[pk[bk]], writes=kA(A_uT + m * 1024, 1024))
            S.mark(("zv", b, t))
            w, wkey = wget("w_in", 0, 8, 512, 512)
            for s in range(4):
                bk = gp()
                mmg(pb[bk][:, :], [(hT[:, kc, s * 128:(s + 1) * 128], w[:, kc, :]) for kc in range(8)], pk[bk], hT_keys + list(wkey))
                vg = AR.f32(s * 2048, 512)
                vgk = AR.keys(s * 2048, 2048)
                S.op("act", lambda h, bk=bk, vg=vg: h.activation(out=vg, in_=pb[bk][:, :], func=AF.Gelu_apprx_tanh), reads=[pk[bk]], writes=vgk)
                S.op("dve", lambda h, s=s, vg=vg: h.bn_stats(out=stat[:, 64 + 6 * s:70 + 6 * s], in_=vg), reads=vgk, writes=[("stat", 8), ("stat", 9), ("stat", 10)])
            for s in range(4):
                S.op("dve", lambda h, s=s: h.bn_aggr(out=stat[:, 16 + 2 * s:18 + 2 * s], in_=stat[:, 64 + 6 * s:70 + 6 * s]),
                     reads=[("stat", 8), ("stat", 9), ("stat", 10)], writes=[("stat", 2)])
            var4 = stat[:, 16:24].rearrange("p (s a) -> p s a", a=2)[:, :, 1:2]
            S.op("dve", lambda h: h.tensor_copy(out=stat[:, 28:32].unsqueeze(2), in_=var4), reads=[("stat", 2)], writes=[("stat", 3)])
            rstd_from_ssq(stat[:, 28:32], stat[:, 24:28], 4, 1.0, [("stat", 3)], [("stat", 3)])
            for s in range(4):
                vg = AR.f32(s * 2048, 512)
                vgk = AR.keys(s * 2048, 2048)
                S.op("dve", lambda h, s=s, vg=vg: h.tensor_scalar(out=vg, in0=vg, scalar1=stat[:, 16 + 2 * s:17 + 2 * s], scalar2=stat[:, 24 + s:25 + s],
                                                              op0=ALU.subtract, op1=ALU.mult),
                     reads=vgk + [("stat", 2), ("stat", 3)], writes=vgk)
                S.op("dve", lambda h, vg=vg: h.tensor_tensor(out=vg, in0=vg, in1=gb["g_gm_ln"][:, :], op=ALU.mult),
                     reads=vgk + [("gb", "g_gm_ln")], writes=vgk)
                S.op("dve", lambda h, s=s, vg=vg: h.tensor_tensor(out=v_tok[:, s, :], in0=vg, in1=gb["b_gm_ln"][:, :], op=ALU.add),
                     reads=vgk + [("gb", "b_gm_ln")], writes=kA(A_vtok + s * 1024, 1024))
            wrelease("default")
            S.mark(("c", b, t))
            w1, wkey1 = wget("w_in", 0, 8, 1024, 512, group="c")
            w2, wkey2 = wget("w_in", 0, 8, 1536, 192, group="c")
            wqm, wqmk = wget("w_in", 0, 8, 1728, 512, group="qm")

            def c_sub(s):
                b1 = gp()
                mmg(pb[b1][:, :], [(hT[:, kc, s * 128:(s + 1) * 128], w1[:, kc, :]) for kc in range(8)], pk[b1], hT_keys + list(wkey1))
                b2 = gp()
                mmg(pb[b2][:, 0:192], [(hT[:, kc, s * 128:(s + 1) * 128], w2[:, kc, :]) for kc in range(8)], pk[b2], hT_keys + list(wkey2))
                yield
                csb = AR.f32(0, 704)
                ck = AR.keys(0, 2816)
                S.op("act", lambda h, b1=b1: h.activation(out=csb[:, 0:512], in_=pb[b1][:, :], func=AF.Copy), reads=[pk[b1]], writes=ck)
                S.op("act", lambda h, b2=b2: h.activation(out=csb[:, 512:704], in_=pb[b2][:, 0:192], func=AF.Copy), reads=[pk[b2]] + ck, writes=ck)
                yield
                junk = AR.bf(4096, 512)
                jk = AR.keys(4096, 1024)
                for i, (c0, c1) in enumerate(((0, 384), (384, 640), (640, 704))):
                    S.op("act", lambda h, i=i, c0=c0, c1=c1: h.activation(out=junk[:, 0:c1 - c0], in_=csb[:, c0:c1], func=AF.Square,
                                                                        accum_out=stat[:, 26 + i:27 + i]),
                         reads=ck, writes=jk + [("stat", 3)])
                yield
                S.op("dve", lambda h: h.tensor_scalar(out=stat[:, 26:27], in0=stat[:, 26:27], scalar1=1.0 / 384, scalar2=None, op0=ALU.mult),
                     reads=[("stat", 3)], writes=[("stat", 3)])
                S.op("dve", lambda h: h.tensor_scalar(out=stat[:, 27:28], in0=stat[:, 27:28], scalar1=1.0 / 256, scalar2=None, op0=ALU.mult),
                     reads=[("stat", 3)], writes=[("stat", 3)])
                S.op("dve", lambda h: h.tensor_scalar(out=stat[:, 28:29], in0=stat[:, 28:29], scalar1=1.0 / 64, scalar2=None, op0=ALU.mult),
                     reads=[("stat", 3)], writes=[("stat", 3)])
                yield
                rstd_from_ssq(stat[:, 26:29], stat[:, 29:32], 3, 1.0, [("stat", 3)], [("stat", 3)], tcol=56)
                yield
                cn = AR.bf(5120, 640)
                cnk = AR.keys(5120, 1280)
                S.op("dve", lambda h: h.scalar_tensor_tensor(out=cn[:, 0:384], in0=csb[:, 0:384], scalar=stat[:, 29:30],
                                                             in1=gb["g_cq"][:, :], op0=ALU.mult, op1=ALU.mult),
                     reads=ck + [("stat", 3), ("gb", "g_cq")], writes=cnk)
                yield
                S.op("dve", lambda h: h.scalar_tensor_tensor(out=cn[:, 384:640], in0=csb[:, 384:640], scalar=stat[:, 30:31],
                                                             in1=gb["g_ckv"][:, :], op0=ALU.mult, op1=ALU.mult),
                     reads=ck + [("stat", 3), ("gb", "g_ckv")] + cnk, writes=cnk)
                kp = AR.f32(6400, 64)
                kpk = AR.keys(6400, 1024)
                S.op("dve", lambda h: h.scalar_tensor_tensor(out=kp, in0=csb[:, 640:704], scalar=stat[:, 31:32],
                                                             in1=gb["g_k_pe"][:, :], op0=ALU.mult, op1=ALU.mult),
                     reads=ck + [("stat", 3), ("gb", "g_k_pe")], writes=kpk)
                yield
                kr = AR.bf(6400 + 768, 64)
                rope(kp.rearrange("p (h d) -> p h d", d=64), kr.rearrange("p (h d) -> p h d", d=64), 1, s, kpk, 6400 + 256, t % 2)
                yield
                bt = gp()
                transposes(bt, [(cn[:, i * 128:(i + 1) * 128], 128, i * 128) for i in range(5)] + [(kr, 64, 640)], cnk + kpk)
                yield
                S.op("act", lambda h, s=s, bt=bt: h.activation(out=cqT[:, :, s * 128:(s + 1) * 128],
                                                              in_=pbb[bt][:, 0:384].rearrange("p (c n) -> p c n", n=128), func=AF.Copy),
                     reads=[pk[bt]], writes=kA(A_cqT, 3072))
                S.op("act", lambda h, s=s, bt=bt: h.activation(out=ckvT[:, :, s * 128:(s + 1) * 128],
                                                              in_=pbb[bt][:, 384:640].rearrange("p (c n) -> p c n", n=128), func=AF.Copy),
                     reads=[pk[bt]], writes=kA(A_ckvT, 2048))
                S.op("act", lambda h, s=s, bt=bt: h.activation(out=kpT[0:64, t0 + s * 128:t0 + (s + 1) * 128],
                                                              in_=pbb[bt][0:64, 640:768], func=AF.Copy),
                     reads=[pk[bt]], writes=["kpT"])
                yield

            def qm_sub(s):
                bk = gp()
                mmg(pb[bk][:, :], [(hT[:, kc, s * 128:(s + 1) * 128], wqm[:, kc, :]) for kc in range(8)], pk[bk], hT_keys + list(wqmk))
                yield
                junk = AR.bf(8192, 512)
                jk = AR.keys(8192, 1024)
                for hh in range(4):
                    S.op("act", lambda h, hh=hh, bk=bk: h.activation(out=junk[:, 0:128], in_=pb[bk][:, hh * 128:(hh + 1) * 128],
                                                                     func=AF.Square, accum_out=stat[:, 8 + hh:9 + hh]),
                         reads=[pk[bk]], writes=jk + [("stat", 1)])
                    if hh % 2 == 1:
                        yield
                rstd_from_ssq(stat[:, 8:12], stat[:, 12:16], 4, 1.0 / 128, [("stat", 1)], [("stat", 1)], tcol=60)
                yield
                qn = AR.bf(9216, 512)
                qnk = AR.keys(9216, 1024)
                for hh in range(4):
                    S.op("dve", lambda h, hh=hh, bk=bk: h.scalar_tensor_tensor(out=qn[:, hh * 128:(hh + 1) * 128],
                                                                               in0=pb[bk][:, hh * 128:(hh + 1) * 128],
                                                                               scalar=stat[:, 12 + hh:13 + hh], in1=gb["g_mq"][:, :],
                                                                               op0=ALU.mult, op1=ALU.mult),
                         reads=[pk[bk], ("stat", 1), ("gb", "g_mq")], writes=qnk)
                    if hh % 2 == 1:
                        yield
                bt = gp()
                transposes(bt, [(qn[:, hh * 128:(hh + 1) * 128], 128, hh * 128) for hh in range(4)], qnk)
                yield
                S.op("act", lambda h, s=s, bt=bt: h.activation(out=qmT[:, :, s * 128:(s + 1) * 128],
                                                              in_=pbb[bt][:, 0:512].rearrange("p (c n) -> p c n", n=128), func=AF.Copy),
                     reads=[pk[bt]], writes=["qmT"])
                yield

            for s in range(4):
                interleave(c_sub(s), qm_sub(s))
            wrelease("c")
            wrelease("qm")
            S.mark(("qm", b, t))

            if DBG_STOP == "win":
                return
            S.mark(("gmlp", b, t))
            for c in range(4):
                bk = gp()
                for g in range(4):
                    mmg(pb[bk][:, g * 128:(g + 1) * 128], [(v_tok[:, c, g * 128:(g + 1) * 128], wcT[:, g, :])], pk[bk],
                        kA(A_vtok + c * 1024, 1024) + ["wcT"])
                tmp = AR.f32(0, 512)
                tk = AR.keys(0, 2048)
                S.op("dve", lambda h, bk=bk: h.tensor_tensor(out=tmp, in0=pb[bk][:, :], in1=gb["b_spatial"][:, :], op=ALU.add),
                     reads=[pk[bk], ("gb", "b_spatial")], writes=tk)
                S.op("dve", lambda h, c=c: h.tensor_tensor(out=gmT[:, :, c * 128:(c + 1) * 128],
                                                            in0=tmp.rearrange("p (g n) -> p g n", n=128),
                                                            in1=uT[:, :, c * 128:(c + 1) * 128], op=ALU.mult),
                     reads=tk + kA(A_uT, 4096), writes=kA(A_gmT, 4096))

            if DBG_STOP == "gmlp":
                return
            S.mark(("q", b, t))
            wq_all, wq_key = wget("w_uq", 0, 3, 0, 1536, nper=768, group="q")
            wkv_all, wkv_key = wget("w_ukv", 0, 2, 0, 2048, nper=1024, group="kv")
            ktok = pT_all[:, :, :].rearrange("p a n -> p (a n)").bitcast(F32)
            kk = [("pT", i) for i in range(4)]
            omflat = OmT[:, :, :].rearrange("p a n -> p (a n)")
            knb = omflat[:, 0:1024].rearrange("p (h d) -> p h d", d=128)
            kjunk = omflat[:, 1024:1152]
            knbk = ["OmT"]

            S.op("pool", lambda h: h.memset(RA.bf(A_qpT, 4096)[64:128, :], 0.0), writes=kA(A_qpT, 8192))

            def q_sub(s):
                qtok = AR.f32(0, 1536)
                qk = AR.keys(0, 6144)
                sq = AR.f32(6144, 384)
                sqk = AR.keys(6144, 1536)
                for j in range(4):
                    bk = gp()
                    mmg(pb[bk][:, 0:384], [(cqT[:, kc, s * 128:(s + 1) * 128], wq_all[:, kc, j * 384:(j + 1) * 384]) for kc in range(3)], pk[bk],
                        kA(A_cqT, 3072) + list(wq_key))
                    S.op("act", lambda h, bk=bk: h.activation(out=sq, in_=pb[bk][:, 0:384], func=AF.Square), reads=[pk[bk]], writes=sqk)
                    S.op("dve", lambda h, j=j: h.tensor_reduce(out=stat[:, 32 + 6 * j:38 + 6 * j], in_=sq.rearrange("p (a d) -> p a d", d=64),
                                                               axis=AX.X, op=ALU.add),
                         reads=sqk, writes=[("stat", 4), ("stat", 5), ("stat", 6)])
                    S.op("dve", lambda h, j=j, bk=bk: h.tensor_copy(out=qtok[:, j * 384:(j + 1) * 384], in_=pb[bk][:, 0:384]),
                         reads=[pk[bk]], writes=qk)
                    yield
                pump(3)
                st3 = stat[:, 32:56].rearrange("p (h a) -> p h a", a=3)
                S.op("dve", lambda h: h.tensor_tensor(out=stat[:, 0:8].unsqueeze(2), in0=st3[:, :, 0:1], in1=st3[:, :, 1:2], op=ALU.add),
                     reads=[("stat", 4), ("stat", 5), ("stat", 6)], writes=[("stat", 0)])
                S.op("dve", lambda h: h.tensor_copy(out=stat[:, 8:16].unsqueeze(2), in_=st3[:, :, 2:3]),
                     reads=[("stat", 4), ("stat", 5), ("stat", 6)], writes=[("stat", 1)])
                yield
                rstd_from_ssq(stat[:, 0:8], stat[:, 16:24], 8, 1.0 / 128, [("stat", 0)], [("stat", 2)], tcol=56)
                yield
                rstd_from_ssq(stat[:, 8:16], stat[:, 24:32], 8, 1.0 / 64, [("stat", 1)], [("stat", 3)], tcol=56)
                yield
                q3 = qtok.rearrange("p (h d) -> p h d", d=192)
                qnb = AR.bf(7680, 1024).rearrange("p (h d) -> p h d", d=128)
                qnbk = AR.keys(7680, 2048)
                S.op("dve", lambda h: h.tensor_tensor(out=qnb, in0=q3[:, :, 0:128],
                                                      in1=stat[:, 16:24].unsqueeze(2).to_broadcast([128, 8, 128]), op=ALU.mult),
                     reads=qk + [("stat", 2)], writes=qnbk)
                yield
                bt = gp()
                transposes(bt, [(qnb[:, hh, :], 128, hh * 128) for hh in range(8)], qnbk)
                S.op("dve", lambda h: h.tensor_tensor(out=q3[:, :, 128:192], in0=q3[:, :, 128:192],
                                                      in1=stat[:, 24:32].unsqueeze(2).to_broadcast([128, 8, 64]), op=ALU.mult),
                     reads=qk + [("stat", 3)], writes=qk)
                yield
                S.op("act", lambda h, s=s, bt=bt: h.activation(out=qnT[:, :, s * 128:(s + 1) * 128],
                                                              in_=pbb[bt][:, 0:1024].rearrange("p (c n) -> p c n", n=128), func=AF.Identity,
                                                              scale=gcol2[:, 0:1]),
                     reads=[pk[bt], ("gcol2", 0)], writes=kA(A_qnT, 8192))
                S.op("dve", lambda h: h.tensor_tensor(out=q3[:, :, 128:192], in0=q3[:, :, 128:192],
                                                      in1=gb["g_q_pe"][:, :].unsqueeze(1).to_broadcast([128, 8, 64]), op=ALU.mult),
                     reads=qk + [("gb", "g_q_pe")], writes=qk)
                yield
                qrb = AR.bf(9728, 512).rearrange("p (h d) -> p h d", d=64)
                qrbk = AR.keys(9728, 1024)
                rope(q3[:, :, 128:192], qrb, 8, s, qk, 10752, t % 2)
                yield
                pump(3)
                bt2 = gp()
                transposes(bt2, [(qrb[:, hh, :], 64, hh * 128) for hh in range(8)], qrbk)
                yield
                S.op("act", lambda h, s=s, bt2=bt2: h.activation(out=qpT[0:64, :, s * 128:(s + 1) * 128],
                                                                in_=pbb[bt2][0:64, 0:1024].rearrange("p (c n) -> p c n", n=128), func=AF.Copy),
                     reads=[pk[bt2]], writes=kA(A_qpT, 8192))
                yield

            def kv_sub(s):
                blk = t * 4 + s
                for j in range(4):
                    bk = gp()
                    mmg(pb[bk][:, :], [(ckvT[:, kc, s * 128:(s + 1) * 128], wkv_all[:, kc, j * 512:(j + 1) * 512]) for kc in range(2)], pk[bk],
                        kA(A_ckvT, 2048) + list(wkv_key))
                    p4 = pb[bk][:, :].rearrange("p (h a d) -> p h a d", a=2, d=128)
                    S.op("act", lambda h, j=j, p4=p4, blk=blk: h.activation(
                        out=vtok[:, blk, j * 256:(j + 1) * 256].rearrange("p (h d) -> p h d", d=128), in_=p4[:, :, 1, :], func=AF.Copy),
                         reads=[pk[bk]], writes=["vtok"])
                    for hh in range(2):
                        S.op("act", lambda h, j=j, hh=hh, bk=bk: h.activation(out=kjunk, in_=pb[bk][:, hh * 256:hh * 256 + 128], func=AF.Square,
                                                                             accum_out=stat[:, 64 + 2 * j + hh:64 + 2 * j + hh + 1]),
                             reads=[pk[bk]], writes=knbk + [("stat", 8)])
                    for hh in range(2):
                        S.op("dve", lambda h, j=j, hh=hh, bk=bk: h.tensor_copy(out=ktok[:, (2 * j + hh) * 128:(2 * j + hh + 1) * 128],
                                                                              in_=pb[bk][:, hh * 256:hh * 256 + 128]),
                             reads=[pk[bk]], writes=kk)
                    yield
                rstd_from_ssq(stat[:, 64:72], stat[:, 72:80], 8, 1.0 / 128, [("stat", 8)], [("stat", 9)], tcol=80)
                yield
                k3 = ktok.rearrange("p (h d) -> p h d", d=128)
                S.op("dve", lambda h: h.tensor_tensor(out=knb, in0=k3, in1=stat[:, 72:80].unsqueeze(2).to_broadcast([128, 8, 128]), op=ALU.mult),
                     reads=kk + [("stat", 9)], writes=knbk)
                yield
                bt = gp()
                transposes(bt, [(knb[:, hh, :], 128, hh * 128) for hh in range(8)], knbk)
                yield
                S.op("act", lambda h, s=s, bt=bt: h.activation(out=knT[:, :, t0 + s * 128:t0 + (s + 1) * 128],
                                                              in_=pbb[bt][:, 0:1024].rearrange("p (c n) -> p c n", n=128), func=AF.Identity,
                                                              scale=gcol2[:, 1:2]),
                     reads=[pk[bt], ("gcol2", 1)], writes=["knT"])
                yield

            for s in range(4):
                interleave(q_sub(s), kv_sub(s))
            wrelease("q")
            wrelease("kv")
            S.mark(("kv", b, t))

            if DBG_STOP == "kv":
                return
            pump(24)
            wrelease("gate")
            gp_n[0] = 4
            S.mark(("attn", b, t))
            items = []
            nkb = 4 * t + 4
            for hh in range(8):
                for kb in range(nkb):
                    items.append(("mla", hh, kb, nkb))
            for hh in range(4):
                for kb in range(2):
                    items.append(("mem", hh, kb, 2))
            sc_mla = 1.0 / math.sqrt(192.0)
            sc_mem = 1.0 / math.sqrt(128.0)
            pslot = [0]
            inflight = []

            def issue_S(idx):
                kind, hh, kb, n = items[idx]
                bk = gp()
                slot = pslot[0]
                pslot[0] = (slot + 1) % 4
                if kind == "mla":
                    r = kb - 4 * t
                    c0 = 128 * r if r > 0 else 0
                    pairs = [(knT[:, hh, kb * 128:(kb + 1) * 128], qnT[:, hh, c0:T])]
                    out = pb[bk][:, c0:T]
                    rd = ["knT", "kpT"] + kA(A_qnT, 8192) + kA(A_qpT, 8192)
                    if r >= 0:
                        fn0 = lambda h, l=pairs[0][0], rr=pairs[0][1], out=out: h.matmul(out, lhsT=l, rhs=rr, start=True, stop=False)
                        S.op_noinc("pe", fn0, reads=rd, wdeps=[pk[bk]])
                        fn1 = lambda h, bk=bk, c0=c0: h.matmul(pb[bk][:, c0:c0 + 128], lhsT=ident[:, :], rhs=negtri[:, :], start=False, stop=False)
                        S.op_noinc("pe", fn1, reads=["ident", "negtri"])
                        fn2 = lambda h, out=out, kb=kb, hh=hh, c0=c0: h.matmul(out, lhsT=kpT[:, kb * 128:(kb + 1) * 128], rhs=qpT[:, hh, c0:T],
                                                                               start=False, stop=True)
                        S.op("pe", fn2, reads=rd, writes=[pk[bk]])
                    else:
                        pairs.append((kpT[:, kb * 128:(kb + 1) * 128], qpT[:, hh, c0:T]))
                        mmg(out, pairs, pk[bk], rd)
                    S.op("act", lambda h, bk=bk, slot=slot, c0=c0: h.activation(out=pT[slot][:, c0:T], in_=pb[bk][:, c0:T], func=AF.Exp, scale=sc_mla),
                         reads=[pk[bk]], writes=[("pT", slot)])
                else:
                    c0 = 0
                    mmg(pb[bk][:, :], [(mkT[:, hh, kb * 128:(kb + 1) * 128], qmT[:, hh, :])], pk[bk], ["mkT", "qmT"])
                    S.op("act", lambda h, bk=bk, slot=slot: h.activation(out=pT[slot][:, :], in_=pb[bk][:, :], func=AF.Exp, scale=sc_mem),
                         reads=[pk[bk]], writes=[("pT", slot)])
                return slot, c0

            def issue_PV(idx, slot, c0):
                kind, hh, kb, n = items[idx]
                hidx = hh if kind == "mla" else 8 + hh
                bo = 4 + 2 * (hidx % 2)
                bz = bo + 1
                vsrc = vtok[:, kb, hh * 128:(hh + 1) * 128] if kind == "mla" else mv[:, kb, hh * 128:(hh + 1) * 128]
                vkey = "vtok" if kind == "mla" else "mv"
                st_, en_ = (kb == 0), (kb == n - 1)
                S.op("pe", lambda h, bo=bo, vsrc=vsrc, slot=slot, c0=c0, st_=st_, en_=en_: h.matmul(pb[bo][:, c0:T], lhsT=vsrc, rhs=pT[slot][:, c0:T], start=st_, stop=en_),
                     reads=[vkey, ("pT", slot)], writes=[pk[bo]])
                S.op("pe", lambda h, bz=bz, slot=slot, c0=c0, st_=st_, en_=en_: h.matmul(pb[bz][:, c0:T], lhsT=ones_b[:, :], rhs=pT[slot][:, c0:T], start=st_, stop=en_),
                     reads=["ones_b", ("pT", slot)], writes=[pk[bz]])
                if en_:
                    rz = AR.f32(0, 512)
                    rzk = AR.keys(0, 2048)
                    S.op("dve", lambda h, bz=bz: h.reciprocal(out=rz, in_=pb[bz][:, :]), reads=[pk[bz]], writes=rzk)
                    if kind == "mla":
                        dst, dk = OT[:, hh, :], kA(hh * 1024, 1024)
                    else:
                        dst, dk = OmT[:, hh, :], ["OmT"]
                    S.op("dve", lambda h, bo=bo, dst=dst: h.tensor_tensor(out=dst, in0=pb[bo][:, :], in1=rz, op=ALU.mult),
                         reads=[pk[bo]] + rzk, writes=dk)

            LOOK = 2
            pending = []
            for idx in range(len(items)):
                pending.append((idx,) + issue_S(idx))
                if len(pending) > LOOK:
                    i0, sl, c0 = pending.pop(0)
                    issue_PV(i0, sl, c0)
            while pending:
                i0, sl, c0 = pending.pop(0)
                issue_PV(i0, sl, c0)

            if DBG_STOP == "attn":
                return
            S.mark(("merge", b, t))
            branches = [("w_o_gm", gmT, 4, kA(A_gmT, 4096)), ("w_o_mla", OT, 8, kA(0, 8192)), ("w_o_mem", OmT, 4, ["OmT"])]
            accm = AR.f32(0, 2048).rearrange("p (m n) -> p m n", n=T)
            for mb in range(2):
                for br, (wn, src, nk, skeys) in enumerate(branches):
                    wo, wok = wget(wn, 0, nk, mb * 512, 512)
                    for m in range(4):
                        gt, gtk1 = gate_view((mb * 3 + br) * 4 + m)
                        gtk = [gtk1]
                        S.op("act", lambda h, gt=gt: h.activation(out=gt, in_=gt, func=AF.Sigmoid), reads=gtk, writes=gtk)
                        by = gp()
                        mmg(pb[by][:, :], [(wo[:, k, m * 128:(m + 1) * 128], src[:, k, :]) for k in range(nk)], pk[by], list(skeys) + list(wok))
                        ak = AR.keys(m * 2048, 2048)
                        if br == 0:
                            S.op("dve", lambda h, by=by, gt=gt, m=m: h.tensor_tensor(out=accm[:, m, :], in0=pb[by][:, :], in1=gt, op=ALU.mult),
                                 reads=[pk[by]] + gtk, writes=ak)
                        else:
                            tmp = AR.f32(10240 + (m % 2) * 2048, 512)
                            tk = AR.keys(10240 + (m % 2) * 2048, 2048)
                            S.op("dve", lambda h, by=by, gt=gt, tmp=tmp: h.tensor_tensor(out=tmp, in0=pb[by][:, :], in1=gt, op=ALU.mult),
                                 reads=[pk[by]] + gtk, writes=tk)
                            if br == 1:
                                S.op("pool", lambda h, m=m, tmp=tmp: h.tensor_tensor(out=accm[:, m, :], in0=accm[:, m, :], in1=tmp, op=ALU.add),
                                     reads=ak + tk, writes=ak)
                            else:
                                S.op("pool", lambda h, m=m, mb=mb, tmp=tmp: h.tensor_tensor(out=mergedT[:, mb * 4 + m, :], in0=accm[:, m, :], in1=tmp, op=ALU.add),
                                     reads=ak + tk, writes=kA(A_qnT + (mb * 4 + m) * 1024, 1024))

            if DBG_STOP == "merge":
                return
            S.mark(("wout", b, t))
            mkeys = kA(A_qnT, 8192)
            wouts = [wget("w_out", 0, 8, c * 512, 512, group="wout") for c in range(2)]
            for s in range(4):
                for c in range(2):
                    w, wkey = wouts[c]
                    xi = (s * 2 + c) % 2
                    xsl = xs[xi]
                    S.dma("sp", f"xs{xi}", lambda h, s=s, c=c, xsl=xsl: h.dma_start(out=xsl[:, 0:512],
                                                                                   in_=x_d[b, t0 + s * 128:t0 + (s + 1) * 128, c * 512:(c + 1) * 512]),
                          writes=[("xs", xi)])
                    bk = gp()
                    mmg(pb[bk][:, :], [(mergedT[:, k, s * 128:(s + 1) * 128], w[:, k, :]) for k in range(8)], pk[bk], mkeys + list(wkey))
                    S.op("dve", lambda h, bk=bk, s=s, c=c, xsl=xsl: h.tensor_tensor(out=xmid[:, s, c * 512:(c + 1) * 512], in0=pb[bk][:, :],
                                                                                  in1=xsl[:, 0:512], op=ALU.add),
                         reads=[pk[bk], ("xs", xi)], writes=[("xmid", s)])
                if s >= 1:
                    norm_trans(GF, hT, lambda: hT_keys, (s - 1) * 128)
                norm_chain(xmid[:, s, :], [("xmid", s)])
            S.mark(("n2", b, t))
            norm_trans(GF, hT, lambda: hT_keys, 3 * 128)
            wrelease("wout")
            S.mark(("ffn1", b, t))
            if t + 1 < ntile:
                rope_tables(t + 1)
            for jb in range(8):
                w, wkey = wget("w_ff1", 0, 8, jb * 512, 512)
                for m in range(4):
                    bk = gp()
                    mmg(pb[bk][:, :], [(w[:, kc, m * 128:(m + 1) * 128], hT[:, kc, :]) for kc in range(8)], pk[bk], hT_keys + list(wkey))
                    rt = AR.f32((m % 2) * 2048, 512)
                    rk = AR.keys((m % 2) * 2048, 2048)
                    S.op("act", lambda h, bk=bk, rt=rt: h.activation(out=rt, in_=pb[bk][:, :], func=AF.Relu), reads=[pk[bk]], writes=rk)
                    eng = "pool" if m % 2 == 0 else "dve"
                    S.op(eng, lambda h, rt=rt, jb=jb, m=m: h.tensor_tensor(out=actT[:, jb * 4 + m, :], in0=rt, in1=rt, op=ALU.mult),
                         reads=rk, writes=kA((jb * 4 + m) * 1024, 1024))
            S.mark(("ffn2", b, t))
            akeys = kA(0, 32768)
            for c in range(2):
                for kg in range(8):
                    w, wkey = wget("w_ff2", kg * 4, 4, c * 512, 512)
                    for s in range(4):
                        for k in range(4):
                            st_ = (kg == 0 and k == 0)
                            en_ = (kg == 7 and k == 3)
                            fn = lambda h, s=s, k=k, kg=kg, w=w, st_=st_, en_=en_: h.matmul(pb[4 + s][:, :], lhsT=actT[:, kg * 4 + k, s * 128:(s + 1) * 128],
                                                                                         rhs=w[:, k, :], start=st_, stop=en_)
                            if k < 3:
                                S.op_noinc("pe", fn, reads=akeys + list(wkey), wdeps=[pk[4 + s]] if st_ else ())
                            else:
                                S.op("pe", fn, reads=akeys + list(wkey), writes=[pk[4 + s]])
                    q_ = c * 8 + kg
                    if t + 1 < ntile and q_ in (1, 3, 5, 7, 9):
                        s2 = (q_ - 1) // 2
                        if s2 >= 1:
                            norm_trans(GM, hT, lambda: hT_keys, (s2 - 1) * 128)
                        if s2 < 4:
                            xsl = xs[s2 % 2]
                            S.dma("sp", f"xs{s2 % 2}", lambda h, s2=s2, xsl=xsl: h.dma_start(out=xsl[:, :], in_=x_d[b, t0 + T + s2 * 128:t0 + T + (s2 + 1) * 128, :]),
                                  writes=[("xs", s2 % 2)])
                            norm_chain(xsl[:, :], [("xs", s2 % 2)])
                for s in range(4):
                    S.op("dve", lambda h, s=s, c=c: h.tensor_tensor(out=xmid[:, s, c * 512:(c + 1) * 512], in0=pb[4 + s][:, :],
                                                                    in1=xmid[:, s, c * 512:(c + 1) * 512], op=ALU.add),
                         reads=[pk[4 + s], ("xmid", s)], writes=[("xmid", s)])
            for s in range(4):
                S.dma("sp", f"out{s}", lambda h, s=s: h.dma_start(out=out_d[b, t0 + s * 128:t0 + (s + 1) * 128, :], in_=xmid[:, s, :]),
                      reads=[("xmid", s)])

        def rope_tables(t):
            par = t % 2
            ak = AR.keys(8192, 4096)
            ang = AR.f32(8192, 128)
            yk = AR.f32(8192 + 512, 128)
            ki = AR.f32(8192 + 1024, 128).bitcast(I32)
            kf = AR.f32(8192 + 1536, 128)
            r1 = AR.f32(8192 + 2048, 128)
            m1 = AR.f32(8192 + 2560, 128)
            a2 = AR.f32(8192 + 3072, 128)
            S.op("dve", lambda h: h.tensor_tensor(out=ang.rearrange("p (s f) -> p s f", f=32),
                                                  in0=invf[:, :].unsqueeze(1).to_broadcast([128, 4, 32]),
                                                  in1=pos_f[:, t * 4:(t + 1) * 4].unsqueeze(2).to_broadcast([128, 4, 32]), op=ALU.mult),
                 reads=["invf", "pos_f"], writes=ak)
            for which, shift in ((0, 0.0), (1, 0.5 * math.pi)):
                src = ang
                if which == 1:
                    S.op("dve", lambda h: h.tensor_scalar(out=a2, in0=ang, scalar1=shift, scalar2=None, op0=ALU.add), reads=ak, writes=ak)
                    src = a2
                S.op("dve", lambda h, src=src: h.tensor_scalar(out=yk, in0=src, scalar1=1.0 / (2 * math.pi), scalar2=None, op0=ALU.mult),
                     reads=ak, writes=ak)
                S.op("dve", lambda h: h.tensor_copy(out=ki, in_=yk), reads=ak, writes=ak)
                S.op("dve", lambda h: h.tensor_copy(out=kf, in_=ki), reads=ak, writes=ak)
                S.op("dve", lambda h, src=src: h.scalar_tensor_tensor(out=r1, in0=kf, scalar=-2 * math.pi, in1=src, op0=ALU.mult, op1=ALU.add),
                     reads=ak, writes=ak)
                S.op("dve", lambda h: h.tensor_scalar(out=m1, in0=r1, scalar1=math.pi, scalar2=-2 * math.pi, op0=ALU.is_gt, op1=ALU.mult),
                     reads=ak, writes=ak)
                S.op("dve", lambda h: h.tensor_tensor(out=r1, in0=r1, in1=m1, op=ALU.add), reads=ak, writes=ak)
                S.op("act", lambda h, which=which: h.activation(out=sincos[:, par, which, :, :], in_=r1.rearrange("p (s f) -> p s f", f=32), func=AF.Sin),
                     reads=ak, writes=[("sincos", par)])

        def rope(x3, out3, nh, s, xkeys, scratch_off, par):
            sn = sincos[:, par, 0, s, :].unsqueeze(1).to_broadcast([128, nh, 32])
            cs = sincos[:, par, 1, s, :].unsqueeze(1).to_broadcast([128, nh, 32])
            n = nh * 32
            t1 = AR.f32(scratch_off, n).rearrange("p (h d) -> p h d", d=32)
            t2 = AR.f32(scratch_off + 4 * n, n).rearrange("p (h d) -> p h d", d=32)
            tk = AR.keys(scratch_off, 8 * n)
            x1, x2 = x3[:, :, 0:32], x3[:, :, 32:64]
            S.op("dve", lambda h: h.tensor_tensor(out=t1, in0=x1, in1=cs, op=ALU.mult), reads=list(xkeys) + [("sincos", par)], writes=tk)
            S.op("dve", lambda h: h.tensor_tensor(out=t2, in0=x2, in1=sn, op=ALU.mult), reads=list(xkeys) + [("sincos", par)], writes=tk)
            S.op("dve", lambda h: h.tensor_tensor(out=out3[:, :, 0:32], in0=t1, in1=t2, op=ALU.subtract), reads=tk, writes=ROPE_OUT_KEYS[0])
            S.op("dve", lambda h: h.tensor_tensor(out=t1, in0=x2, in1=cs, op=ALU.mult), reads=list(xkeys) + [("sincos", par)] + tk, writes=tk)
            S.op("dve", lambda h: h.tensor_tensor(out=t2, in0=x1, in1=sn, op=ALU.mult), reads=list(xkeys) + [("sincos", par)] + tk, writes=tk)
            S.op("dve", lambda h: h.tensor_tensor(out=out3[:, :, 32:64], in0=t1, in1=t2, op=ALU.add), reads=tk, writes=ROPE_OUT_KEYS[0])

        ROPE_OUT_KEYS = [AR.keys(6400, 14 * 1024 - 6400)]

        def program():
            WS.idx = 0
            WS.issued = 0
            WS.slot_of = {}
            WS.free = list(range(NSLOT))
            WS.groups = {}
            gp_state[0] = 0
            for b in range(nseq):
                if DBG_STOP == "setup":
                    return
                seq_prologue(b)
                if DBG_STOP == "prologue":
                    return
                for t in range(ntile):
                    tile_body(b, t, n1_done=(t > 0))

        S.null = True
        program()
        S.null = False
        program()
        for s_ in range(4):
            if f"out{s_}" in S.dma_cnt:
                S._wait("sp", ("dma", f"out{s_}", S.dma_cnt[f"out{s_}"]))
        S.emit()
        S.mark(("end", 0, 0))
        build.info = dict(sbuf_remaining=nc.sbuf_bytes_remaining, cnt=dict(S.cnt), nblocks=len(WS.plan), marks=S.marks, npe=S.npe)
    return nc


_INVF = (10000.0 ** (-np.arange(0, 64, 2, dtype=np.float32) / 64.0)).astype(np.float32).reshape(1, 32)


def make_in_maps(inputs, ncores=NCORES):
    maps = []
    for c in range(ncores):
        sl = slice(c * SEQ_PER_CORE, (c + 1) * SEQ_PER_CORE)
        m = {"x": np.ascontiguousarray(inputs["x"][sl]),
             "mem": np.ascontiguousarray(inputs["mem"][sl]),
             "positions": np.ascontiguousarray(inputs["positions"][sl]).astype(np.int32),
             "invf": _INVF,
             "w_spatial": np.ascontiguousarray(inputs["w_spatial"][0].reshape(512, 128))}
        for n, k, mm_ in WEIGHTS:
            m[n] = np.ascontiguousarray(inputs[n][0])
        for n, mm_ in GAINS:
            m[n] = np.ascontiguousarray(inputs[n][0].reshape(1, mm_))
        maps.append(m)
    return maps


def kernel(**inputs):
    inputs = {k: np.asarray(v) for k, v in inputs.items()}
    nc = build()
    maps = make_in_maps(inputs)
    res = run_bass_kernel_spmd(nc, maps, core_ids=list(range(NCORES)))
    outs = [np.asarray(r["out"]) for r in res.results]
    return np.concatenate(outs, axis=0).astype(np.float32)
```
